# Optimizing a Trainium2 kernel written in Bass

```python
import jax, jax.numpy as jnp
from jax import lax
import numpy as np

D_MODEL = 2048
BATCH = 4
SEQ = 2048
DEPTH = 1

N_META = 16
POOL_WIDTH = D_MODEL
POOL_WINDOWS = (2, 4, 8, 16)
N_POOL_GROUPS = len(POOL_WINDOWS)
POOL_GROUP = POOL_WIDTH // N_POOL_GROUPS
CONV_WIDTH = D_MODEL
CONV_K = 3
N_BRANCHES = 2
FFN_HIDDEN = ((8 * D_MODEL // 3 + 255) // 256) * 256
IN_PROJ_WIDTH = POOL_WIDTH + 3 * CONV_WIDTH + N_BRANCHES * D_MODEL
EPS = 1e-6

kernel_name = "hybrid_pool_shortconv_gated_block"


def rms_norm(x, g):
    xf = x.astype(jnp.float32)
    y = xf * lax.rsqrt(jnp.mean(xf * xf, axis=-1, keepdims=True) + EPS)
    return (y * g.astype(jnp.float32)).astype(x.dtype)


def causal_multiscale_pool(u):
    b, l, _ = u.shape
    ug = u.reshape(b, l, N_POOL_GROUPS, POOL_GROUP).astype(jnp.float32)
    c = jnp.concatenate([jnp.zeros((b, 1, N_POOL_GROUPS, POOL_GROUP), jnp.float32),
                         jnp.cumsum(ug, axis=1)], axis=1)
    t1 = jnp.arange(1, l + 1, dtype=jnp.float32)
    outs = []
    for gi, w in enumerate(POOL_WINDOWS):
        cg = c[:, :, gi]
        c_lag = jnp.pad(cg, ((0, 0), (w - 1, 0), (0, 0)))[:, :l]
        win_sum = cg[:, 1:] - c_lag
        count = jnp.minimum(t1, jnp.float32(w))[None, :, None]
        outs.append(win_sum / count - ug[:, :, gi])
    return jnp.stack(outs, axis=2).astype(u.dtype)


def causal_depthwise_conv(v, w):
    l = v.shape[1]
    vp = jnp.pad(v, ((0, 0), (CONV_K - 1, 0), (0, 0)))
    return sum(w[k][None, None, :] * vp[:, k:k + l] for k in range(CONV_K))


def setup_inputs(seed: int = 0) -> dict:
    key = jax.random.key(seed)
    ks = jax.random.split(key, 16)
    f32 = jnp.float32
    nrm = lambda k, shape, scale: jax.random.normal(k, shape, f32) * scale
    return {
        "x": nrm(ks[0], (BATCH, SEQ, D_MODEL), 1.0),
        "meta_tokens": nrm(ks[1], (N_META, D_MODEL), 1.0),
        "norm_mix_g": 1.0 + nrm(ks[2], (D_MODEL,), 0.02),
        "w_in": nrm(ks[3], (D_MODEL, IN_PROJ_WIDTH), D_MODEL ** -0.5),
        "b_gate": nrm(ks[4], (N_BRANCHES * D_MODEL,), 0.02),
        "pool_w": nrm(ks[5], (N_POOL_GROUPS, POOL_GROUP, POOL_GROUP), POOL_GROUP ** -0.5),
        "pool_scale": 1.0 + nrm(ks[6], (POOL_WIDTH,), 0.02),
        "conv_w": nrm(ks[7], (CONV_K, CONV_WIDTH), CONV_K ** -0.5),
        "conv_out_w": nrm(ks[8], (CONV_WIDTH, D_MODEL), CONV_WIDTH ** -0.5),
        "w_o": nrm(ks[9], (D_MODEL, D_MODEL), D_MODEL ** -0.5),
        "norm_ffn_g": 1.0 + nrm(ks[10], (D_MODEL,), 0.02),
        "w_gate_up": nrm(ks[11], (D_MODEL, 2 * FFN_HIDDEN), D_MODEL ** -0.5),
        "w_down": nrm(ks[12], (FFN_HIDDEN, D_MODEL), FFN_HIDDEN ** -0.5),
        "norm_final_g": 1.0 + nrm(ks[13], (D_MODEL,), 0.02),
    }


def reference(x, meta_tokens, norm_mix_g, w_in, b_gate, pool_w, pool_scale, conv_w,
              conv_out_w, w_o, norm_ffn_g, w_gate_up, w_down, norm_final_g):
    b = x.shape[0]
    meta = jnp.broadcast_to(meta_tokens[None].astype(x.dtype), (b, N_META, D_MODEL))
    h = jnp.concatenate([meta, x], axis=1)

    splits = np.cumsum([POOL_WIDTH, CONV_WIDTH, CONV_WIDTH, CONV_WIDTH, D_MODEL]).tolist()
    for _ in range(DEPTH):
        hn = rms_norm(h, norm_mix_g)
        proj = hn @ w_in
        u, gb_in, gc_in, v_in, ga_lin, gbr_lin = jnp.split(proj, splits, axis=-1)

        pooled = causal_multiscale_pool(u)
        y_a = jnp.einsum("blgc,gcd->blgd", pooled, pool_w).reshape(b, -1, POOL_WIDTH)
        y_a = y_a * pool_scale

        y_b = (gb_in * causal_depthwise_conv(gc_in * v_in, conv_w)) @ conv_out_w

        gates = jax.nn.sigmoid(jnp.concatenate([ga_lin, gbr_lin], axis=-1) + b_gate)
        g_a, g_b = jnp.split(gates, 2, axis=-1)
        h = h + (g_a * y_a + g_b * y_b) @ w_o

        hn = rms_norm(h, norm_ffn_g)
        gate, up = jnp.split(hn @ w_gate_up, 2, axis=-1)
        h = h + (jax.nn.silu(gate) * up) @ w_down

    out = rms_norm(h, norm_final_g)
    return out[:, N_META:]
```

```python
from contextlib import ExitStack

import numpy as np
import ml_dtypes
import concourse.bass as bass
import concourse.mybir as mybir
from concourse.bass_utils import run_bass_kernel_spmd

F32 = mybir.dt.float32
BF16 = mybir.dt.bfloat16
AF = mybir.ActivationFunctionType
ALU = mybir.AluOpType

D = 2048
T = 1040
TR = 1024
HALO = 16
FF = 5632
NPASS = 11
B3 = [(0, 352), (352, 704), (704, 1040)]
B2 = [(16, 528), (528, 1040)]
POOL_W = (2, 4, 8, 16)
EPS = 1e-6
NS = 10
N_CORES = 8

PV_G1, PV_BGA, PV_BGB, PV_PSC, PV_CW, PV_G2 = 0, 16, 32, 48, 64, 112


class Tok:
    __slots__ = ("sem", "val", "key")

    def __init__(self, sem, val, key):
        self.sem, self.val, self.key = sem, val, key


class Eng:
    def __init__(self, name):
        self.name = name
        self.ops = []
        self.cnt = 0
        self.sem = None
        self.seen = {}

    def wait(self, *toks):
        for t in toks:
            if t is None:
                continue
            if isinstance(t, (list, tuple)):
                self.wait(*t)
                continue
            if self.seen.get(t.key, 0) >= t.val:
                continue
            self.seen[t.key] = t.val
            self.ops.append(("wait", t.sem, t.val))

    def do(self, fn, wait=(), sig=True):
        self.wait(*wait)
        if sig:
            self.cnt += 1
            self.ops.append(("op", fn, self.sem, 1))
            return Tok(self.sem, self.cnt, self.name)
        self.ops.append(("op", fn, None, 0))
        return None

    def dma(self, fn, sem, semkey, count, wait=()):
        self.wait(*wait)
        self.ops.append(("op", fn, sem, 16))
        return Tok(sem, 16 * count, semkey)

    def emit(self, e):
        for op in self.ops:
            if op[0] == "wait":
                e.wait_ge(op[1], op[2])
            else:
                ins = op[1](e)
                if op[2] is not None:
                    ins.then_inc(op[2], op[3])


def weight_plan():
    plan = []
    for c in range(16):
        plan += [("in", 4096 + 128 * c), ("in", 6144 + 128 * c), ("in", 2048 + 128 * c)]
    for gi in range(4):
        plan.append(("pw", gi))
        for cc in range(4):
            plan.append(("in", 128 * (4 * gi + cc)))
        for cc in range(4):
            c = 4 * gi + cc
            plan += [("in", 8192 + 128 * c), ("in", 10240 + 128 * c), ("co", 128 * c)]
    for cb in range(4):
        for kq in range(4):
            plan.append(("wo", cb, kq))

    def gu(g):
        out = []
        for jj in range(4):
            j = 4 * g + jj
            out += [("gu", 128 * j), ("gu", FF + 128 * j)]
        return out

    plan += gu(0)
    for g in range(NPASS):
        if g + 1 < NPASS:
            plan += gu(g + 1)
        for cb in range(4):
            plan.append(("wd", g, cb))
    return plan


def build_program(phase_limit=99, debug=False):
    nc = bass.Bass("TRN2", target_bir_lowering=False)
    dram = {}

    def din(name, shape, dt=F32):
        dram[name] = nc.dram_tensor(name, list(shape), dt, kind="ExternalInput").ap()
        return dram[name]

    xh = din("xh", [T, D])
    w_in = din("w_in", [D, 12288])
    pool_w = din("pw2", [512, 2048])
    conv_out_w = din("conv_out_w", [D, D])
    w_o = din("w_o", [D, D])
    w_gu = din("w_gate_up", [D, 2 * FF])
    w_dn = din("w_down", [FF, D])
    pvec = din("pvec", [128, 128])
    g3b = din("g3b", [128, D])
    ident_d = din("ident", [128, 128], BF16)
    out = nc.dram_tensor("out", [TR, D], F32, kind="ExternalOutput").ap()
    dbg = {}
    if debug:
        dbg["hnT"] = nc.dram_tensor("dbg_hnT", [128, 16 * T], BF16, kind="ExternalOutput").ap()
        dbg["zT"] = nc.dram_tensor("dbg_zT", [128, 16 * TR], BF16, kind="ExternalOutput").ap()
        dbg["mT"] = nc.dram_tensor("dbg_mT", [128, 16 * TR], BF16, kind="ExternalOutput").ap()
        dbg["h1"] = nc.dram_tensor("dbg_h1", [128, 8 * D], F32, kind="ExternalOutput").ap()
        dbg["hn1T"] = nc.dram_tensor("dbg_hn1T", [128, 16 * TR], BF16, kind="ExternalOutput").ap()

    w_in_v = w_in.rearrange("(ko p) c -> p ko c", p=128)
    co_v = conv_out_w.rearrange("(ko p) c -> p ko c", p=128)
    wo_v = w_o.rearrange("(ko p) c -> p ko c", p=128)
    gu_v = w_gu.rearrange("(ko p) c -> p ko c", p=128)
    wd_v = w_dn.rearrange("(ko p) c -> p ko c", p=128)
    pw_v = pool_w.rearrange("(kk p) c -> p kk c", p=128)

    with ExitStack() as es:
        def sb(name, n, dt=F32):
            return es.enter_context(nc.sbuf_tensor(name, [128, n], dt))

        R1 = sb("R1", 16640)
        R2 = sb("R2", 8192)
        R3 = sb("R3", 12368)
        WR = sb("WR", NS * 1024)
        PWB = sb("PWB", 2048)
        pv = sb("pv", 128)
        g3 = sb("g3", D)
        ident = sb("ident_sb", 128, BF16)
        stats = sb("stats", 64)
        ps = es.enter_context(nc.psum_tensor("ps", [128, 8, 512], F32))

        def sem(name):
            return es.enter_context(nc.semaphore(name))

        PE, ACT, DVE, POOL, SP = Eng("pe"), Eng("act"), Eng("dve"), Eng("pool"), Eng("sp")
        for e in (PE, ACT, DVE, POOL, SP):
            e.sem = sem("s_" + e.name)
        wsem = [sem(f"w{i}") for i in range(NS)]
        pwsem = [sem(f"pw{i}") for i in range(2)]
        xsem = [sem(f"x{i}") for i in range(9)]
        hsem = [sem(f"h{i}") for i in range(8)]
        osem = [sem(f"o{i}") for i in range(8)]
        osem2 = [sem(f"ob{i}") for i in range(8)]
        csem = sem("const")
        g3sem = sem("g3")
        dsem = sem("dbg")

        hnT = R1[:, 0:8320].bitcast(BF16).rearrange("p (k t) -> p k t", k=16)
        zT = R1[:, 8320:16512].bitcast(BF16).rearrange("p (k t) -> p k t", k=16)
        h1 = R1[:, 0:16384].rearrange("p (t d) -> p t d", t=8)
        mT = R2[:, 0:8192].bitcast(BF16).rearrange("p (k t) -> p k t", k=16)
        junk = R2[:, 4096:5120].bitcast(BF16)
        hnb = R2[:, 5120:7168].bitcast(BF16).rearrange("p (b d) -> p b d", b=2)
        actT = R2[:, 0:4096].bitcast(BF16).rearrange("p (b j t) -> p b j t", b=2, j=4)
        sg = R2[:, 4096:5120].rearrange("p (b n) -> p b n", b=2)
        gcs = R3[:, 0:1040]
        cv = R3[:, 1040:2080]
        ub = R3[:, 2080:3120]
        S1 = R3[:, 3120:4160]
        S2 = R3[:, 4160:5200]
        pooledT = R3[:, 5200:7248].bitcast(BF16).rearrange("p (k t) -> p k t", k=4)
        ybuf = R3[:, 7248:8272].rearrange("p (b n) -> p b n", b=2)
        gA = R3[:, 8272:9296].rearrange("p (b n) -> p b n", b=2)
        gB = R3[:, 9296:10320].rearrange("p (b n) -> p b n", b=2)
        ta = R3[:, 10320:11344].rearrange("p (b n) -> p b n", b=2)
        tb = R3[:, 11344:12368].rearrange("p (b n) -> p b n", b=2)
        hn1T = R3[:, 0:8192].bitcast(BF16).rearrange("p (k t) -> p k t", k=16)
        hn1b = R3[:, 8192:10240].bitcast(BF16).rearrange("p (b d) -> p b d", b=2)
        junk2 = R3[:, 10240:11264].bitcast(BF16)
        wring = WR[:, :].bitcast(BF16).rearrange("p (s n) -> p s n", s=NS)
        pwbuf = PWB[:, :].bitcast(BF16).rearrange("p (s n) -> p s n", s=2)
        ss1, rs1 = stats[:, 0:9], stats[:, 9:18]
        ss2, rs2 = stats[:, 18:26], stats[:, 26:34]
        ss3, rs3 = stats[:, 34:42], stats[:, 42:50]
        eps_ap = stats[:, 50:51]

        ring = {"u": 0, "free": [None] * 8}

        def bank_alloc():
            b = ring["u"] % 8
            ring["u"] += 1
            return b, ring["free"][b]

        def bank_release(b, *toks):
            ring["free"][b] = list(toks)

        def bank_f32(b):
            return ps[:, b, :]

        def bank_bf(b):
            return ps[:, b, :].bitcast(BF16).rearrange("p (k t) -> p k t", k=8)

        plan = weight_plan()
        ring_idx = {}
        ring_at = []
        for i_, key_ in enumerate(plan):
            if key_[0] != "pw":
                ring_idx[i_] = len(ring_at)
                ring_at.append(i_)
        wst = {"next_dma": 0, "cur": -1, "rel": {}, "tok": {}, "uses": [0] * NS, "pwuses": [0, 0]}

        def w_src(key):
            kind = key[0]
            if kind == "in":
                return w_in_v[:, :, key[1]:key[1] + 128], 16
            if kind == "co":
                return co_v[:, :, key[1]:key[1] + 128], 16
            if kind == "gu":
                return gu_v[:, :, key[1]:key[1] + 128], 16
            if kind == "pw":
                return pw_v[:, :, key[1] * 512:(key[1] + 1) * 512], 4
            if kind == "wo":
                return wo_v[:, 4 * key[2]:4 * key[2] + 4, key[1] * 512:(key[1] + 1) * 512], 4
            if kind == "wd":
                return wd_v[:, 4 * key[1]:4 * key[1] + 4, key[2] * 512:(key[2] + 1) * 512], 4
            raise KeyError(key)

        def w_view(i):
            key = plan[i]
            _, kdim = w_src(key)
            if key[0] == "pw":
                return pwbuf[:, key[1] % 2, :].rearrange("p (k c) -> p k c", k=kdim)
            return wring[:, ring_idx[i] % NS, :].rearrange("p (k c) -> p k c", k=kdim)

        def w_pump():
            while wst["next_dma"] < len(plan) and wst["next_dma"] <= wst["cur"] + NS - 1:
                j = wst["next_dma"]
                key = plan[j]
                waits = []
                if key[0] == "pw":
                    if key[1] >= 2:
                        pj = plan.index(("pw", key[1] - 2))
                        if pj not in wst["rel"]:
                            break
                        waits = [wst["rel"][pj]]
                    s = key[1] % 2
                    wst["pwuses"][s] += 1
                    semh, semk, cnt = pwsem[s], f"pw{s}", wst["pwuses"][s]
                else:
                    r = ring_idx[j]
                    if r >= NS:
                        pj = ring_at[r - NS]
                        if pj not in wst["rel"]:
                            break
                        waits = [wst["rel"][pj]]
                    s = r % NS
                    wst["uses"][s] += 1
                    semh, semk, cnt = wsem[s], f"w{s}", wst["uses"][s]
                src, _ = w_src(key)
                dst = w_view(j)
                if j in wst.get("gate", {}):
                    waits = waits + [wst["gate"][j]]
                if key == ("pw", 0):
                    waits = waits + [p0_end["act"]]
                wst["tok"][j] = POOL.dma(
                    (lambda e, dst=dst, src=src: e.dma_start(out=dst, in_=src)),
                    semh, semk, cnt, wait=waits)
                wst["next_dma"] += 1

        p0_end = {}

        class WT:
            pass

        def w_use(key):
            wst["cur"] += 1
            i = wst["cur"]
            assert plan[i] == key, (i, plan[i], key)
            w_pump()
            assert i in wst["tok"], f"weight tile {i} {key} not prefetched (NS too small)"
            wt = WT()
            wt.i, wt.ap, wt.tok = i, w_view(i), wst["tok"][i]
            return wt

        def w_release(wt, tok):
            wst["rel"][wt.i] = tok
            w_pump()

        c1 = SP.dma(lambda e: e.dma_start(out=pv[:, :], in_=pvec), csem, "const", 1)
        c3 = SP.dma(lambda e: e.dma_start(out=ident[:, :], in_=ident_d), csem, "const", 2)
        c4 = DVE.do(lambda e: e.memset(eps_ap, EPS))
        const_tok = c3
        ACT.wait(c4)
        ACT.do(lambda e: e.activation(out=stats[:, 52:53], in_=eps_ap, func=AF.Sqrt), sig=False)
        for e in (PE, ACT, DVE):
            e.wait(const_tok)

        def pe_group(bank, n, K, lhs, rhs, waits, view=None):
            PE.wait(*waits)
            o = view if view is not None else bank_f32(bank)[:, 0:n]
            tok = None
            for k in range(K):
                tok = PE.do(
                    (lambda e, o=o, l=lhs(k), r=rhs(k), st=(k == 0), sp=(k == K - 1):
                     e.matmul(o, lhsT=l, rhs=r, start=st, stop=sp)),
                    sig=(k == K - 1))
            return tok

        def norm_transpose(i, R, src_ap, src_tok, ss, rs, col, bufs, bufi, jk, gcol, dstT, t0, buf_free, extra_wait=()):
            a1 = ACT.do(lambda e: e.activation(out=jk[:R, :], in_=src_ap, func=AF.Square,
                                               accum_out=ss[:R, col:col + 1]),
                        wait=[src_tok, *extra_wait])
            d1 = ACT.do(lambda e: e.activation(out=rs[:R, col:col + 1], in_=ss[:R, col:col + 1], func=AF.Sqrt, scale=1.0 / D,
                                               bias=eps_ap[:R, :]),
                        wait=[a1])
            d2 = DVE.do(lambda e: e.reciprocal(out=rs[:R, col:col + 1], in_=rs[:R, col:col + 1]),
                        wait=[d1])
            hb = bufs[:R, bufi, :]
            a2 = ACT.do(lambda e: e.activation(out=hb, in_=src_ap, func=AF.Copy, scale=rs[:R, col:col + 1]),
                        wait=[d2, buf_free])

            def pe_part():
                last_tr = None
                evs = []
                for half in range(2):
                    b, ftok = bank_alloc()
                    PE.wait(ftok, a2)
                    bv = bank_bf(b)
                    for kk in range(8):
                        k = 8 * half + kk
                        last_tr = PE.do(
                            (lambda e, o=bv[:, kk, 0:R], i_=bufs[:R, bufi, k * 128:(k + 1) * 128]:
                             e.transpose(o, i_, ident[:R, :R])),
                            sig=(kk == 7))
                    gsl = pv[:, gcol + 8 * half:gcol + 8 * half + 8].unsqueeze(2).to_broadcast([128, 8, R])
                    ev = DVE.do(
                        (lambda e, o=dstT[:, 8 * half:8 * half + 8, t0:t0 + R], i0=bv[:, :, 0:R], i1=gsl:
                         e.tensor_tensor(out=o, in0=i0, in1=i1, op=ALU.mult)),
                        wait=[last_tr])
                    bank_release(b, ev)
                    evs.append(ev)
                return last_tr, evs[-1]

            return a2, pe_part

        tr_toks = {}
        a2_toks = {}
        hnT_done = None

        def xt_ap(i, R):
            if i < 6:
                return R3[:R, i * 2048:(i + 1) * 2048]
            if i < 8:
                return R2[:R, (i - 6) * 2048:(i - 5) * 2048]
            return PWB[:R, 0:2048]

        xtok = []
        for i in range(9):
            R = 128 if i < 8 else 16
            xtok.append(SP.dma((lambda e, o=xt_ap(i, R), s=xh[128 * i:128 * i + R, :]: e.dma_start(out=o, in_=s)),
                               xsem[i], f"x{i}", 1))
        POOL.wait(xtok[2])
        wst["gate"] = {3: xtok[7]}
        g3_tok = SP.dma(lambda e: e.dma_start(out=g3[:, :], in_=g3b), g3sem, "g3", 1)

        def p0_stats(i):
            R = 128 if i < 8 else 16
            src = xt_ap(i, R)
            a1 = ACT.do(lambda e: e.activation(out=junk[:R, :], in_=src, func=AF.Square,
                                               accum_out=ss1[:R, i:i + 1]), wait=[xtok[i]])
            d1 = ACT.do(lambda e: e.activation(out=rs1[:R, i:i + 1], in_=ss1[:R, i:i + 1], func=AF.Sqrt, scale=1.0 / D,
                                               bias=eps_ap[:R, :]), wait=[a1])
            return DVE.do(lambda e: e.reciprocal(out=rs1[:R, i:i + 1], in_=rs1[:R, i:i + 1]), wait=[d1])

        def p0_rest(i, rtok):
            R = 128 if i < 8 else 16
            src = xt_ap(i, R)
            buf = i % 2
            hb = hnb[:R, buf, :]
            if i % 2 == 0:
                a2 = ACT.do(lambda e: e.activation(out=hb, in_=src, func=AF.Copy, scale=rs1[:R, i:i + 1]),
                            wait=[rtok, tr_toks.get(i - 2)])
            else:
                a2 = DVE.do(lambda e: e.tensor_scalar(out=hb, in0=src, scalar1=rs1[:R, i:i + 1], scalar2=None,
                                                      op0=ALU.mult), wait=[rtok, tr_toks.get(i - 2)])
            a2_toks[i] = a2
            last_tr = None
            ev = None
            for half in range(2):
                b_, ftok = bank_alloc()
                PE.wait(ftok, a2)
                bv = bank_bf(b_)
                for kk in range(8):
                    k = 8 * half + kk
                    last_tr = PE.do(
                        (lambda e, o=bv[:, kk, 0:R], i_=hnb[:R, buf, k * 128:(k + 1) * 128]:
                         e.transpose(o, i_, ident[:R, :R])),
                        sig=(kk == 7))
                gsl = pv[:, PV_G1 + 8 * half:PV_G1 + 8 * half + 8].unsqueeze(2).to_broadcast([128, 8, R])
                ev = DVE.do(
                    (lambda e, o=hnT[:, 8 * half:8 * half + 8, 128 * i:128 * i + R], i0=bv[:, :, 0:R], i1=gsl:
                     e.tensor_tensor(out=o, in0=i0, in1=i1, op=ALU.mult)),
                    wait=[last_tr])
                bank_release(b_, ev)
            tr_toks[i] = last_tr
            return ev

        zstate = {"last_z": None}
        cprev = {"cvread": [None] * 3, "tap": None, "z": [None, None]}

        def conv_chunk(c):
            prev = cprev
            wt_gc = w_use(("in", 4096 + 128 * c))
            wt_v = w_use(("in", 6144 + 128 * c))
            wt_gb = w_use(("in", 2048 + 128 * c))
            gc_ev, cv_ev, tap = [None] * 3, [None] * 3, [None, None]
            tk = {}

            def gc_unit(bi):
                a, b = B3[bi]
                bank, ftok = bank_alloc()
                tk["gc"] = pe_group(bank, b - a, 16, lambda k: wt_gc.ap[:, k, :], lambda k: hnT[:, k, a:b],
                                    [ftok, wt_gc.tok, hn_tile[(b - 1) // 128]])
                ev = ACT.do((lambda e, o=gcs[:, a:b], i_=bank_f32(bank)[:, 0:b - a]:
                             e.activation(out=o, in_=i_, func=AF.Copy)),
                            wait=[tk["gc"], prev["cvread"][bi]])
                bank_release(bank, ev)
                gc_ev[bi] = ev

            def v_unit(bi):
                a, b = B3[bi]
                bank, ftok = bank_alloc()
                tk["v"] = pe_group(bank, b - a, 16, lambda k: wt_v.ap[:, k, :], lambda k: hnT[:, k, a:b],
                                   [ftok, wt_v.tok, hn_tile[(b - 1) // 128]])
                ev = DVE.do((lambda e, o=cv[:, a:b], i0=bank_f32(bank)[:, 0:b - a], i1=gcs[:, a:b]:
                             e.tensor_tensor(out=o, in0=i0, in1=i1, op=ALU.mult)),
                            wait=[tk["v"], gc_ev[bi], prev["tap"]])
                bank_release(bank, ev)
                cv_ev[bi] = ev
                prev["cvread"][bi] = ev

            def gb_unit(bi):
                a, b = B2[bi]
                y = ybuf[:, bi, :]
                for step in range(3):
                    sh = 2 - step
                    wcol = pv[:, PV_CW + 3 * c + step:PV_CW + 3 * c + step + 1]
                    if step == 0:
                        tap[bi] = DVE.do((lambda e, y=y, i0=cv[:, a - sh:b - sh], w=wcol:
                                          e.tensor_scalar(out=y, in0=i0, scalar1=w, scalar2=None, op0=ALU.mult)),
                                         wait=[cv_ev[bi], cv_ev[bi + 1], prev["z"][bi]])
                    else:
                        tap[bi] = DVE.do((lambda e, y=y, i0=cv[:, a - sh:b - sh], w=wcol:
                                          e.scalar_tensor_tensor(out=y, in0=i0, scalar=w, in1=y,
                                                                 op0=ALU.mult, op1=ALU.add)),
                                         wait=[tap[bi]])
                bank, ftok = bank_alloc()
                tk["gb"] = pe_group(bank, 512, 16, lambda k: wt_gb.ap[:, k, :], lambda k: hnT[:, k, a:b],
                                    [ftok, wt_gb.tok, hn_tile[(b - 1) // 128]])
                ev = DVE.do((lambda e, o=zT[:, c, a - 16:b - 16], i0=bank_f32(bank)[:, 0:512], i1=ybuf[:, bi, :]:
                             e.tensor_tensor(out=o, in0=i0, in1=i1, op=ALU.mult)),
                            wait=[tk["gb"], tap[bi]])
                bank_release(bank, ev)
                prev["z"][bi] = ev
                zstate["last_z"] = ev

            gc_unit(0); yield
            v_unit(0); yield
            gc_unit(1); yield
            v_unit(1); yield
            gb_unit(0); yield
            gc_unit(2)
            w_release(wt_gc, tk["gc"]); yield
            v_unit(2)
            w_release(wt_v, tk["v"]); yield
            gb_unit(1)
            prev["tap"] = tap[1]
            w_release(wt_gb, tk["gb"]); yield

        rt = {}
        hn_tile = {}
        early_gen = iter(())
        for i in range(9):
            rt[i] = p0_stats(i)
            if i >= 1:
                hn_tile[i - 1] = p0_rest(i - 1, rt[i - 1])
                if phase_limit >= 1 and i - 1 == 3:
                    early_gen = conv_chunk(0)
                    next(early_gen)
                if phase_limit >= 1 and i - 1 == 4:
                    next(early_gen)
                if phase_limit >= 1 and i - 1 == 5:
                    next(early_gen)
                if phase_limit >= 1 and i - 1 == 6:
                    next(early_gen)
                if phase_limit >= 1 and i - 1 == 7:
                    next(early_gen)
        hn_tile[8] = p0_rest(8, rt[8])
        hnT_done = hn_tile[8]
        last_p0_pe = tr_toks[8]
        last_p0_act = Tok(ACT.sem, ACT.cnt, "act")
        last_p0_dve = Tok(DVE.sem, DVE.cnt, "dve")
        p0_end["act"] = last_p0_act

        def dump(name, src_ap):
            t = SP.dma((lambda e, o=dbg[name], s=src_ap: e.dma_start(out=o, in_=s)), dsem, "dbg",
                       dump.n + 1, wait=[Tok(DVE.sem, DVE.cnt, "dve"), Tok(PE.sem, PE.cnt, "pe"),
                                         Tok(ACT.sem, ACT.cnt, "act")])
            dump.n += 1
            return t
        dump.n = 0
        final_toks = []

        if debug and phase_limit == 0:
            final_toks.append(dump("hnT", R1[:, 0:8320].bitcast(BF16)))

        last_z = None
        if phase_limit >= 1:
            for _ in early_gen:
                pass
            ACT.wait(last_p0_dve)
            DVE.wait(last_p0_act)
            for c in range(1, 16):
                for _ in conv_chunk(c):
                    pass
            last_z = zstate["last_z"]
            if debug and phase_limit == 1:
                final_toks.append(dump("zT", R1[:, 8320:16512].bitcast(BF16)))

        last_m = None
        last_p1_pe = None
        if phase_limit >= 2:
            prev = {"ubread": None, "s1": None, "s2": None, "ya": None, "ta": [None, None], "tb": [None, None],
                    "m": [None, None]}
            for gi in range(4):
                wwin = POOL_W[gi]
                wt_pw = w_use(("pw", gi))
                pooled_toks = []
                for cc in range(4):
                    c = 4 * gi + cc
                    wt_u = w_use(("in", 128 * c))
                    u_ev = []
                    tok = None
                    for bi, (a, b) in enumerate(B3):
                        bank, ftok = bank_alloc()
                        tok = pe_group(bank, b - a, 16, lambda k: wt_u.ap[:, k, :],
                                       lambda k, a=a, b=b: hnT[:, k, a:b], [ftok, wt_u.tok, hnT_done])
                        ev = ACT.do((lambda e, o=ub[:, a:b], i_=bank_f32(bank)[:, 0:b - a]:
                                     e.activation(out=o, in_=i_, func=AF.Copy)),
                                    wait=[tok, prev["ubread"]])
                        bank_release(bank, ev)
                        u_ev.append(ev)
                    w_release(wt_u, tok)
                    cur = ub
                    curtok = u_ev
                    bufs_ = [(S1, "s1"), (S2, "s2")]
                    lag = 1
                    step = 0
                    while lag < wwin:
                        dst, dkey = bufs_[step % 2]
                        off = 2 * lag - 1
                        tk = DVE.do((lambda e, o=dst[:, off:T], i0=cur[:, off:T], i1=cur[:, off - lag:T - lag]:
                                     e.tensor_tensor(out=o, in0=i0, in1=i1, op=ALU.add)),
                                    wait=[curtok, prev[dkey]])
                        cur, curtok = dst, tk
                        prev[dkey] = tk
                        lag *= 2
                        step += 1
                    pt = DVE.do((lambda e, o=pooledT[:, cc, :], i0=cur[:, HALO:T], i1=ub[:, HALO:T], s=1.0 / wwin:
                                 e.scalar_tensor_tensor(out=o, in0=i0, scalar=s, in1=i1,
                                                        op0=ALU.mult, op1=ALU.subtract)),
                                wait=[curtok, u_ev, prev["ya"]])
                    prev["ubread"] = pt
                    prev["s1"] = pt
                    prev["s2"] = pt
                    pooled_toks.append(pt)
                for cc in range(4):
                    c = 4 * gi + cc
                    wt_ga = w_use(("in", 8192 + 128 * c))
                    wt_gbr = w_use(("in", 10240 + 128 * c))
                    wt_co = w_use(("co", 128 * c))
                    ga_ev, gb_ev = [None, None], [None, None]
                    tok = None
                    for bi, (a, b) in enumerate(B2):
                        bank, ftok = bank_alloc()
                        tok = pe_group(bank, 512, 16, lambda k: wt_ga.ap[:, k, :],
                                       lambda k, a=a, b=b: hnT[:, k, a:b], [ftok, wt_ga.tok])
                        ev = ACT.do((lambda e, o=gA[:, bi, :], i_=bank_f32(bank)[:, 0:512],
                                     bcol=pv[:, PV_BGA + c:PV_BGA + c + 1]:
                                     e.activation(out=o, in_=i_, func=AF.Sigmoid, bias=bcol)),
                                    wait=[tok, prev["ta"][bi]])
                        bank_release(bank, ev)
                        ga_ev[bi] = ev
                    w_release(wt_ga, tok)
                    for bi, (a, b) in enumerate(B2):
                        bank, ftok = bank_alloc()
                        tok = pe_group(bank, 512, 16, lambda k: wt_gbr.ap[:, k, :],
                                       lambda k, a=a, b=b: hnT[:, k, a:b], [ftok, wt_gbr.tok])
                        ev = ACT.do((lambda e, o=gB[:, bi, :], i_=bank_f32(bank)[:, 0:512],
                                     bcol=pv[:, PV_BGB + c:PV_BGB + c + 1]:
                                     e.activation(out=o, in_=i_, func=AF.Sigmoid, bias=bcol)),
                                    wait=[tok, prev["tb"][bi]])
                        bank_release(bank, ev)
                        gb_ev[bi] = ev
                    w_release(wt_gbr, tok)
                    for bi, (a, b) in enumerate(B2):
                        bank, ftok = bank_alloc()
                        tok = pe_group(bank, 512, 4, lambda kk: wt_pw.ap[:, kk, cc * 128:(cc + 1) * 128],
                                       lambda kk, a=a, b=b: pooledT[:, kk, a - 16:b - 16],
                                       [ftok, wt_pw.tok, pooled_toks])
                        prev["ya"] = tok
                        ev = DVE.do((lambda e, o=ta[:, bi, :], i0=bank_f32(bank)[:, 0:512],
                                     s=pv[:, PV_PSC + c:PV_PSC + c + 1], i1=gA[:, bi, :]:
                                     e.scalar_tensor_tensor(out=o, in0=i0, scalar=s, in1=i1,
                                                            op0=ALU.mult, op1=ALU.mult)),
                                    wait=[tok, ga_ev[bi], prev["m"][bi]])
                        bank_release(bank, ev)
                        prev["ta"][bi] = ev
                    if cc == 3:
                        w_release(wt_pw, tok)
                    for bi, (a, b) in enumerate(B2):
                        bank, ftok = bank_alloc()
                        tok = pe_group(bank, 512, 16, lambda k: wt_co.ap[:, k, :],
                                       lambda k, a=a, b=b: zT[:, k, a - 16:b - 16], [ftok, wt_co.tok, last_z])
                        ev = DVE.do((lambda e, o=tb[:, bi, :], i0=bank_f32(bank)[:, 0:512], i1=gB[:, bi, :]:
                                     e.tensor_tensor(out=o, in0=i0, in1=i1, op=ALU.mult)),
                                    wait=[tok, gb_ev[bi], prev["m"][bi]])
                        bank_release(bank, ev)
                        prev["tb"][bi] = ev
                        mv = DVE.do((lambda e, o=mT[:, c, a - 16:b - 16], i0=tb[:, bi, :], i1=ta[:, bi, :]:
                                     e.tensor_tensor(out=o, in0=i0, in1=i1, op=ALU.add)),
                                    wait=[ev, prev["ta"][bi], last_p0_pe, last_p0_act])
                        prev["m"][bi] = mv
                        last_m = mv
                    w_release(wt_co, tok)
                    last_p1_pe = tok
            if debug and phase_limit == 2:
                final_toks.append(dump("mT", R2[:, 0:8192].bitcast(BF16)))

        pending_tr = []
        hn1_done = [None] * 8
        last_p2_pe = None
        if phase_limit >= 3:
            last_p1_act = Tok(ACT.sem, ACT.cnt, "act")
            last_p1_dve = Tok(DVE.sem, DVE.cnt, "dve")
            xr = []
            for t in range(8):
                xr.append(SP.dma((lambda e, o=h1[:, t, :], s=xh[HALO + 128 * t:HALO + 128 * (t + 1), :]:
                                  e.dma_start(out=o, in_=s)),
                                 hsem[t], f"h{t}", 1, wait=[last_p1_pe, last_m]))
            tr2 = {}
            a2s = {}
            for cb in range(4):
                wts = [w_use(("wo", cb, kq)) for kq in range(4)]
                tok = None
                for t in range(8):
                    bank, ftok = bank_alloc()
                    tok = pe_group(bank, 512, 16, lambda k, t=t: mT[:, k, 128 * t:128 * (t + 1)],
                                   lambda k: wts[k // 4].ap[:, k % 4, :],
                                   [ftok, last_m] + [w.tok for w in wts])
                    ev = DVE.do((lambda e, o=h1[:, t, cb * 512:(cb + 1) * 512], i0=bank_f32(bank)[:, 0:512]:
                                 e.tensor_tensor(out=o, in0=i0, in1=o, op=ALU.add)),
                                wait=[tok, xr[t]])
                    bank_release(bank, ev)
                    last_p2_pe = tok
                    if cb == 3:
                        if len(pending_tr) >= 2:
                            tt, pp = pending_tr.pop(0)
                            tr2[tt], hn1_done[tt] = pp()
                        a2, pe_part = norm_transpose(t, 128, h1[:, t, :], ev, ss2, rs2, t, hn1b, t % 2, junk2,
                                                     PV_G2, hn1T, 128 * t, tr2.get(t - 2),
                                                     extra_wait=[last_p1_act, last_p1_dve, last_p1_pe])
                        a2s[t] = a2
                        pending_tr.append((t, pe_part))
                for w in wts:
                    w_release(w, tok)
            if phase_limit == 3:
                while pending_tr:
                    tt, pp = pending_tr.pop(0)
                    tr2[tt], hn1_done[tt] = pp()
                if debug:
                    final_toks.append(dump("h1", R1[:, 0:16384]))
                    final_toks.append(dump("hn1T", R3[:, 0:8192].bitcast(BF16)))

        out_toks = []
        if phase_limit >= 4:
            st = {"sg": [None, None], "sgi": 0, "dn_last": {}, "first": True}

            def flush_tr():
                while pending_tr:
                    tt, pp = pending_tr.pop(0)
                    _, hn1_done[tt] = pp()

            def gu_unit(g, jj, blk, wt_g, wt_u):
                need = [hn1_done[t] for t in range(4 * blk, 4 * blk + 4) if hn1_done[t] is not None]
                bg, ftok = bank_alloc()
                tg = pe_group(bg, 512, 16, lambda k: wt_g.ap[:, k, :],
                              lambda k, blk=blk: hn1T[:, k, 512 * blk:512 * (blk + 1)],
                              [ftok, wt_g.tok, last_p2_pe] + need)
                si = st["sgi"] % 2
                st["sgi"] += 1
                ea = ACT.do((lambda e, o=sg[:, si, :], i_=bank_f32(bg)[:, 0:512]:
                             e.activation(out=o, in_=i_, func=AF.Silu)),
                            wait=[tg, st["sg"][si], last_p2_pe])
                bank_release(bg, ea)
                bu, ftok = bank_alloc()
                tu = pe_group(bu, 512, 16, lambda k: wt_u.ap[:, k, :],
                              lambda k, blk=blk: hn1T[:, k, 512 * blk:512 * (blk + 1)],
                              [ftok, wt_u.tok])
                ed = DVE.do((lambda e, o=actT[:, g % 2, jj, 512 * blk:512 * (blk + 1)],
                             i0=bank_f32(bu)[:, 0:512], i1=sg[:, si, :]:
                             e.tensor_tensor(out=o, in0=i0, in1=i1, op=ALU.mult)),
                            wait=[tu, ea, st["dn_last"].get(g - 2), last_p2_pe])
                bank_release(bu, ed)
                st["sg"][si] = ed
                st["act_done_%d" % g] = ed
                return tg, tu

            def gate_up(g):
                if g == 0:
                    wts = []
                    for jj in range(4):
                        j = 4 * g + jj
                        wts.append((w_use(("gu", 128 * j)), w_use(("gu", FF + 128 * j))))
                    for blk in range(2):
                        for jj in range(4):
                            if blk == 0 and jj == 2 and pending_tr:
                                flush_tr()
                            tg, tu = gu_unit(g, jj, blk, wts[jj][0], wts[jj][1])
                            if blk == 1:
                                w_release(wts[jj][0], tg)
                                w_release(wts[jj][1], tu)
                    return
                for jj in range(4):
                    j = 4 * g + jj
                    wt_g = w_use(("gu", 128 * j))
                    wt_u = w_use(("gu", FF + 128 * j))
                    tg = tu = None
                    for blk in range(2):
                        tg, tu = gu_unit(g, jj, blk, wt_g, wt_u)
                    w_release(wt_g, tg)
                    w_release(wt_u, tu)

            fin = {}
            fin_pool = {}
            fin_scale = {}

            tmpg2 = R2[:, 5120:7168].rearrange("p (b n) -> p b n", b=2)

            def final_out(t, d1):
                ev_t, p3 = fin_pool[t]
                d2 = DVE.do((lambda e, o=rs3[:, t:t + 1]: e.reciprocal(out=o, in_=o)), wait=[d1])
                d3 = DVE.do((lambda e, o=h1[:, t, 0:1024], s=rs3[:, t:t + 1]:
                             e.scalar_tensor_tensor(out=o, in0=o, scalar=s, in1=g3[:, 0:1024],
                                                    op0=ALU.mult, op1=ALU.mult)),
                            wait=[d2, g3_tok])
                a3 = ACT.do((lambda e, o=h1[:, t, 1024:2048], s=rs3[:, t:t + 1], tg=tmpg2[:, t % 2, :]:
                             e.activation(out=o, in_=tg, func=AF.Copy, scale=s)),
                            wait=[d2, p3])
                fin_scale[t] = a3
                out_toks.append(ACT.dma((lambda e, o=out[128 * t:128 * (t + 1), 1024:2048], s=h1[:, t, 1024:2048]:
                                         e.dma_start(out=o, in_=s)),
                                        osem2[t], f"ob{t}", 1, wait=[a3]))
                out_toks.append(ACT.dma((lambda e, o=out[128 * t:128 * (t + 1), 0:1024], s=h1[:, t, 0:1024]:
                                         e.dma_start(out=o, in_=s)),
                                        osem[t], f"o{t}", 1, wait=[d3]))

            def down(g, merged_prev=False):
                gs = [g - 1, g] if merged_prev else [g]
                wts_all = [[w_use(("wd", gg, cb)) for cb in range(4)] for gg in gs]
                wts = [w for ws_ in wts_all for w in ws_]
                nk = 4 * len(gs)
                tok = None
                for t in range(8):
                    ev = None
                    for cb in range(4):
                        bank, ftok = bank_alloc()
                        tok = pe_group(bank, 512, nk,
                                       lambda kk, t=t: actT[:, gs[kk // 4] % 2, kk % 4, 128 * t:128 * (t + 1)],
                                       lambda kk, cb=cb: wts_all[kk // 4][cb].ap[:, kk % 4, :],
                                       [ftok] + [st["act_done_%d" % gg] for gg in gs] + [w.tok for w in wts])
                        ev = DVE.do((lambda e, o=h1[:, t, cb * 512:(cb + 1) * 512], i0=bank_f32(bank)[:, 0:512]:
                                     e.tensor_tensor(out=o, in0=i0, in1=o, op=ALU.add)),
                                    wait=[tok])
                        bank_release(bank, ev)
                    if g == NPASS - 1:
                        a1 = ACT.do((lambda e, i_=h1[:, t, :], acc=ss3[:, t:t + 1]:
                                     e.activation(out=junk2[:, :], in_=i_, func=AF.Square,
                                                  accum_out=acc)),
                                    wait=[ev])
                        d1 = ACT.do((lambda e, o=rs3[:, t:t + 1], i0=ss3[:, t:t + 1]:
                                     e.activation(out=o, in_=i0, func=AF.Sqrt, scale=1.0 / D, bias=eps_ap[:, :])), wait=[a1])
                        if t >= 1:
                            final_out(t - 1, fin[t - 1])
                        fin[t] = d1
                        p3 = POOL.do((lambda e, i0=h1[:, t, 1024:2048], tg=tmpg2[:, t % 2, :]:
                                      e.tensor_tensor(out=tg, in0=i0, in1=g3[:, 1024:2048], op=ALU.mult)),
                                     wait=[ev, g3_tok, fin_scale.get(t - 2)])
                        fin_pool[t] = (ev, p3)
                if g == NPASS - 1:
                    final_out(7, fin[7])
                st["dn_last"][g] = tok
                for w in wts:
                    w_release(w, tok)

            gate_up(0)
            for g in range(NPASS):
                if g + 1 < NPASS:
                    gate_up(g + 1)
                if g == NPASS - 2:
                    continue
                down(g, merged_prev=(g == NPASS - 1))

        ACT.wait(*out_toks)
        SP.wait(*final_toks)
        SP.wait(Tok(PE.sem, PE.cnt, "pe"), Tok(ACT.sem, ACT.cnt, "act"), Tok(DVE.sem, DVE.cnt, "dve"), g3_tok)
        if POOL.cnt:
            SP.wait(Tok(POOL.sem, POOL.cnt, "pool"))
        for s in range(NS):
            if wst["uses"][s]:
                SP.wait(Tok(wsem[s], 16 * wst["uses"][s], f"w{s}"))
        for s in range(2):
            if wst["pwuses"][s]:
                SP.wait(Tok(pwsem[s], 16 * wst["pwuses"][s], f"pw{s}"))

        with nc.Block() as block:
            @block.sync
            def _(e):
                SP.emit(e)

            @block.gpsimd
            def _(e):
                POOL.emit(e)

            @block.tensor
            def _(e):
                PE.emit(e)

            @block.scalar
            def _(e):
                ACT.emit(e)

            @block.vector
            def _(e):
                DVE.emit(e)
    return nc


def make_in_maps(x, meta_tokens, norm_mix_g, w_in, b_gate, pool_w, pool_scale, conv_w, conv_out_w, w_o,
                 norm_ffn_g, w_gate_up, w_down, norm_final_g):
    f = lambda a: np.ascontiguousarray(np.asarray(a, dtype=np.float32))
    x = f(x)
    meta = f(meta_tokens)

    def fm(v):
        return f(v).reshape(16, 128).T

    pvec = np.zeros((128, 128), np.float32)
    pvec[:, PV_G1:PV_G1 + 16] = fm(norm_mix_g)
    bg = f(b_gate)
    pvec[:, PV_BGA:PV_BGA + 16] = fm(bg[:D])
    pvec[:, PV_BGB:PV_BGB + 16] = fm(bg[D:])
    pvec[:, PV_PSC:PV_PSC + 16] = fm(pool_scale)
    cw = f(conv_w)
    for tap in range(3):
        pvec[:, PV_CW + tap:PV_CW + 48:3] = fm(cw[tap])
    pvec[:, PV_G2:PV_G2 + 16] = fm(norm_ffn_g)
    g3b = np.ascontiguousarray(np.broadcast_to(f(norm_final_g)[None, :], (128, D)))
    ident = np.eye(128, dtype=np.float32).astype(ml_dtypes.bfloat16)
    shared = {
        "w_in": f(w_in), "pw2": np.ascontiguousarray(f(pool_w).transpose(1, 0, 2).reshape(512, 2048)), "conv_out_w": f(conv_out_w),
        "w_o": f(w_o), "w_gate_up": f(w_gate_up), "w_down": f(w_down),
        "pvec": pvec, "g3b": g3b, "ident": ident,
    }
    maps = []
    for i in range(N_CORES):
        b, half = i // 2, i % 2
        halo = meta if half == 0 else x[b, TR - HALO:TR]
        xh = np.concatenate([halo, x[b, half * TR:(half + 1) * TR]], axis=0)
        m = dict(shared)
        m["xh"] = np.ascontiguousarray(xh)
        maps.append(m)
    return maps


def kernel(x, meta_tokens, norm_mix_g, w_in, b_gate, pool_w, pool_scale, conv_w, conv_out_w, w_o,
           norm_ffn_g, w_gate_up, w_down, norm_final_g):
    maps = make_in_maps(x, meta_tokens, norm_mix_g, w_in, b_gate, pool_w, pool_scale, conv_w, conv_out_w,
                        w_o, norm_ffn_g, w_gate_up, w_down, norm_final_g)
    nc = build_program()
    res = run_bass_kernel_spmd(nc, maps, core_ids=list(range(N_CORES)))
    outp = np.empty((4, 2 * TR, D), np.float32)
    for i in range(N_CORES):
        b, half = i // 2, i % 2
        outp[b, half * TR:(half + 1) * TR] = res.results[i]["out"]
    return outp
```

```python
from contextlib import ExitStack

import numpy as np
import ml_dtypes
import concourse.bass as bass
import concourse.mybir as mybir
from concourse.bass_utils import run_bass_kernel_spmd

F32 = mybir.dt.float32
BF16 = mybir.dt.bfloat16
AF = mybir.ActivationFunctionType
ALU = mybir.AluOpType

D = 2048
T = 1040
TR = 1024
HALO = 16
FF = 5632
NPASS = 11
B3 = [(0, 352), (352, 704), (704, 1040)]
B2 = [(16, 368), (368, 704), (704, 1040)]
POOL_W = (2, 4, 8, 16)
EPS = 1e-6
NS = 10
N_CORES = 8

PV_G1, PV_BGA, PV_BGB, PV_PSC, PV_CW, PV_G2 = 0, 16, 32, 48, 64, 112


class Tok:
    __slots__ = ("sem", "val", "key")

    def __init__(self, sem, val, key):
        self.sem, self.val, self.key = sem, val, key


class Eng:
    def __init__(self, name):
        self.name = name
        self.ops = []
        self.cnt = 0
        self.sem = None
        self.seen = {}

    def wait(self, *toks):
        for t in toks:
            if t is None:
                continue
            if isinstance(t, (list, tuple)):
                self.wait(*t)
                continue
            if self.seen.get(t.key, 0) >= t.val:
                continue
            self.seen[t.key] = t.val
            self.ops.append(("wait", t.sem, t.val))

    def do(self, fn, wait=(), sig=True):
        self.wait(*wait)
        if sig:
            self.cnt += 1
            self.ops.append(("op", fn, self.sem, 1))
            return Tok(self.sem, self.cnt, self.name)
        self.ops.append(("op", fn, None, 0))
        return None

    def dma(self, fn, sem, semkey, count, wait=()):
        self.wait(*wait)
        self.ops.append(("op", fn, sem, 16))
        return Tok(sem, 16 * count, semkey)

    def emit(self, e):
        for op in self.ops:
            if op[0] == "wait":
                e.wait_ge(op[1], op[2])
            else:
                ins = op[1](e)
                if op[2] is not None:
                    ins.then_inc(op[2], op[3])


def weight_plan():
    plan = []
    for c in range(16):
        plan += [("in", 4096 + 128 * c), ("in", 6144 + 128 * c), ("in", 2048 + 128 * c)]
    for gi in range(4):
        plan.append(("pw", gi))
        for cc in range(4):
            plan.append(("in", 128 * (4 * gi + cc)))
        for cc in range(4):
            c = 4 * gi + cc
            plan += [("in", 8192 + 128 * c), ("in", 10240 + 128 * c), ("co", 128 * c)]
    for cb in range(4):
        for kq in range(4):
            plan.append(("wo", cb, kq))

    def gu(g):
        out = []
        for jj in range(4):
            j = 4 * g + jj
            out += [("gu", 128 * j), ("gu", FF + 128 * j)]
        return out

    plan += gu(0)
    for g in range(NPASS):
        if g + 1 < NPASS:
            plan += gu(g + 1)
        for cb in range(4):
            plan.append(("wd", g, cb))
    return plan


def build_program(phase_limit=99, debug=False):
    nc = bass.Bass("TRN2", target_bir_lowering=False)
    dram = {}

    def din(name, shape, dt=F32):
        dram[name] = nc.dram_tensor(name, list(shape), dt, kind="ExternalInput").ap()
        return dram[name]

    xh = din("xh", [T, D])
    w_in = din("w_in", [D, 12288])
    pool_w = din("pw2", [512, 2048])
    conv_out_w = din("conv_out_w", [D, D])
    w_o = din("w_o", [D, D])
    w_gu = din("w_gate_up", [D, 2 * FF])
    w_dn = din("w_down", [FF, D])
    pvec = din("pvec", [128, 128])
    g3b = din("g3b", [128, D])
    ident_d = din("ident", [128, 128], BF16)
    out = nc.dram_tensor("out", [TR, D], F32, kind="ExternalOutput").ap()
    dbg = {}
    if debug:
        dbg["hnT"] = nc.dram_tensor("dbg_hnT", [128, 16 * T], BF16, kind="ExternalOutput").ap()
        dbg["zT"] = nc.dram_tensor("dbg_zT", [128, 16 * TR], BF16, kind="ExternalOutput").ap()
        dbg["mT"] = nc.dram_tensor("dbg_mT", [128, 16 * TR], BF16, kind="ExternalOutput").ap()
        dbg["h1"] = nc.dram_tensor("dbg_h1", [128, 8 * D], F32, kind="ExternalOutput").ap()
        dbg["hn1T"] = nc.dram_tensor("dbg_hn1T", [128, 16 * TR], BF16, kind="ExternalOutput").ap()

    w_in_v = w_in.rearrange("(ko p) c -> p ko c", p=128)
    co_v = conv_out_w.rearrange("(ko p) c -> p ko c", p=128)
    wo_v = w_o.rearrange("(ko p) c -> p ko c", p=128)
    gu_v = w_gu.rearrange("(ko p) c -> p ko c", p=128)
    wd_v = w_dn.rearrange("(ko p) c -> p ko c", p=128)
    pw_v = pool_w.rearrange("(kk p) c -> p kk c", p=128)

    with ExitStack() as es:
        def sb(name, n, dt=F32):
            return es.enter_context(nc.sbuf_tensor(name, [128, n], dt))

        R1 = sb("R1", 16640)
        R2 = sb("R2", 8192)
        R3 = sb("R3", 12368)
        WR = sb("WR", NS * 1024)
        PWB = sb("PWB", 2048)
        pv = sb("pv", 128)
        g3 = sb("g3", D)
        ident = sb("ident_sb", 128, BF16)
        stats = sb("stats", 64)
        ps = es.enter_context(nc.psum_tensor("ps", [128, 8, 512], F32))

        def sem(name):
            return es.enter_context(nc.semaphore(name))

        PE, ACT, DVE, POOL, SP = Eng("pe"), Eng("act"), Eng("dve"), Eng("pool"), Eng("sp")
        for e in (PE, ACT, DVE, POOL, SP):
            e.sem = sem("s_" + e.name)
        wsem = [sem(f"w{i}") for i in range(NS)]
        pwsem = [sem(f"pw{i}") for i in range(2)]
        xsem = [sem(f"x{i}") for i in range(9)]
        hsem = [sem(f"h{i}") for i in range(8)]
        osem = [sem(f"o{i}") for i in range(8)]
        osem2 = [sem(f"ob{i}") for i in range(8)]
        csem = sem("const")
        g3sem = sem("g3")
        dsem = sem("dbg")

        hnT = R1[:, 0:8320].bitcast(BF16).rearrange("p (k t) -> p k t", k=16)
        zT = R1[:, 8320:16512].bitcast(BF16).rearrange("p (k t) -> p k t", k=16)
        h1 = R1[:, 0:16384].rearrange("p (t d) -> p t d", t=8)
        mT = R2[:, 0:8192].bitcast(BF16).rearrange("p (k t) -> p k t", k=16)
        junk = R2[:, 4096:5120].bitcast(BF16)
        hnb = R2[:, 5120:7168].bitcast(BF16).rearrange("p (b d) -> p b d", b=2)
        actT = R2[:, 0:4096].bitcast(BF16).rearrange("p (b j t) -> p b j t", b=2, j=4)
        sg = R2[:, 4096:5120].rearrange("p (b n) -> p b n", b=2)
        gcs = R3[:, 0:1040]
        cv = R3[:, 1040:2080]
        ub = R3[:, 2080:3120]
        S1 = R3[:, 3120:4160]
        S2 = R3[:, 4160:5200]
        pooledT = R3[:, 5200:7248].bitcast(BF16).rearrange("p (k t) -> p k t", k=4)
        ybuf = R3[:, 7248:8272]
        gA = R3[:, 8272:9296]
        gB = R3[:, 9296:10320]
        ta = R3[:, 10320:11344]
        tb = R3[:, 11344:12368]
        hn1T = R3[:, 0:8192].bitcast(BF16).rearrange("p (k t) -> p k t", k=16)
        hn1b = R3[:, 8192:10240].bitcast(BF16).rearrange("p (b d) -> p b d", b=2)
        junk2 = R3[:, 10240:11264].bitcast(BF16)
        wring = WR[:, :].bitcast(BF16).rearrange("p (s n) -> p s n", s=NS)
        pwbuf = PWB[:, :].bitcast(BF16).rearrange("p (s n) -> p s n", s=2)
        ss1, rs1 = stats[:, 0:9], stats[:, 9:18]
        ss2, rs2 = stats[:, 18:26], stats[:, 26:34]
        ss3, rs3 = stats[:, 34:42], stats[:, 42:50]
        eps_ap = stats[:, 50:51]

        ring = {"u": 0, "free": [None] * 8}

        def bank_alloc():
            b = ring["u"] % 8
            ring["u"] += 1
            return b, ring["free"][b]

        def bank_release(b, *toks):
            ring["free"][b] = list(toks)

        def bank_f32(b):
            return ps[:, b, :]

        def bank_bf(b):
            return ps[:, b, :].bitcast(BF16).rearrange("p (k t) -> p k t", k=8)

        plan = weight_plan()
        ring_idx = {}
        ring_at = []
        for i_, key_ in enumerate(plan):
            if key_[0] != "pw":
                ring_idx[i_] = len(ring_at)
                ring_at.append(i_)
        wst = {"next_dma": 0, "cur": -1, "rel": {}, "tok": {}, "uses": [0] * NS, "pwuses": [0, 0]}

        def w_src(key):
            kind = key[0]
            if kind == "in":
                return w_in_v[:, :, key[1]:key[1] + 128], 16
            if kind == "co":
                return co_v[:, :, key[1]:key[1] + 128], 16
            if kind == "gu":
                return gu_v[:, :, key[1]:key[1] + 128], 16
            if kind == "pw":
                return pw_v[:, :, key[1] * 512:(key[1] + 1) * 512], 4
            if kind == "wo":
                return wo_v[:, 4 * key[2]:4 * key[2] + 4, key[1] * 512:(key[1] + 1) * 512], 4
            if kind == "wd":
                return wd_v[:, 4 * key[1]:4 * key[1] + 4, key[2] * 512:(key[2] + 1) * 512], 4
            raise KeyError(key)

        def w_view(i):
            key = plan[i]
            _, kdim = w_src(key)
            if key[0] == "pw":
                return pwbuf[:, key[1] % 2, :].rearrange("p (k c) -> p k c", k=kdim)
            return wring[:, ring_idx[i] % NS, :].rearrange("p (k c) -> p k c", k=kdim)

        def w_pump():
            while wst["next_dma"] < len(plan) and wst["next_dma"] <= wst["cur"] + NS - 1:
                j = wst["next_dma"]
                key = plan[j]
                waits = []
                if key[0] == "pw":
                    if key[1] >= 2:
                        pj = plan.index(("pw", key[1] - 2))
                        if pj not in wst["rel"]:
                            break
                        waits = [wst["rel"][pj]]
                    s = key[1] % 2
                    wst["pwuses"][s] += 1
                    semh, semk, cnt = pwsem[s], f"pw{s}", wst["pwuses"][s]
                else:
                    r = ring_idx[j]
                    if r >= NS:
                        pj = ring_at[r - NS]
                        if pj not in wst["rel"]:
                            break
                        waits = [wst["rel"][pj]]
                    s = r % NS
                    wst["uses"][s] += 1
                    semh, semk, cnt = wsem[s], f"w{s}", wst["uses"][s]
                src, _ = w_src(key)
                dst = w_view(j)
                wst["tok"][j] = POOL.dma(
                    (lambda e, dst=dst, src=src: e.dma_start(out=dst, in_=src)),
                    semh, semk, cnt, wait=waits)
                wst["next_dma"] += 1

        class WT:
            pass

        def w_use(key):
            wst["cur"] += 1
            i = wst["cur"]
            assert plan[i] == key, (i, plan[i], key)
            w_pump()
            assert i in wst["tok"], f"weight tile {i} {key} not prefetched (NS too small)"
            wt = WT()
            wt.i, wt.ap, wt.tok = i, w_view(i), wst["tok"][i]
            return wt

        def w_release(wt, tok):
            wst["rel"][wt.i] = tok
            w_pump()

        c1 = SP.dma(lambda e: e.dma_start(out=pv[:, :], in_=pvec), csem, "const", 1)
        c3 = SP.dma(lambda e: e.dma_start(out=ident[:, :], in_=ident_d), csem, "const", 2)
        c4 = DVE.do(lambda e: e.memset(eps_ap, EPS))
        const_tok = c3
        ACT.wait(c4)
        ACT.do(lambda e: e.activation(out=stats[:, 52:53], in_=eps_ap, func=AF.Sqrt), sig=False)
        for e in (PE, ACT, DVE):
            e.wait(const_tok)

        def pe_group(bank, n, K, lhs, rhs, waits, view=None):
            PE.wait(*waits)
            o = view if view is not None else bank_f32(bank)[:, 0:n]
            tok = None
            for k in range(K):
                tok = PE.do(
                    (lambda e, o=o, l=lhs(k), r=rhs(k), st=(k == 0), sp=(k == K - 1):
                     e.matmul(o, lhsT=l, rhs=r, start=st, stop=sp)),
                    sig=(k == K - 1))
            return tok

        def norm_transpose(i, R, src_ap, src_tok, ss, rs, col, bufs, bufi, jk, gcol, dstT, t0, buf_free, extra_wait=()):
            a1 = ACT.do(lambda e: e.activation(out=jk[:R, :], in_=src_ap, func=AF.Square,
                                               accum_out=ss[:R, col:col + 1]),
                        wait=[src_tok, *extra_wait])
            d1 = ACT.do(lambda e: e.activation(out=rs[:R, col:col + 1], in_=ss[:R, col:col + 1], func=AF.Sqrt, scale=1.0 / D,
                                               bias=eps_ap[:R, :]),
                        wait=[a1])
            d2 = DVE.do(lambda e: e.reciprocal(out=rs[:R, col:col + 1], in_=rs[:R, col:col + 1]),
                        wait=[d1])
            hb = bufs[:R, bufi, :]
            a2 = ACT.do(lambda e: e.activation(out=hb, in_=src_ap, func=AF.Copy, scale=rs[:R, col:col + 1]),
                        wait=[d2, buf_free])

            def pe_part():
                last_tr = None
                evs = []
                for half in range(2):
                    b, ftok = bank_alloc()
                    PE.wait(ftok, a2)
                    bv = bank_bf(b)
                    for kk in range(8):
                        k = 8 * half + kk
                        last_tr = PE.do(
                            (lambda e, o=bv[:, kk, 0:R], i_=bufs[:R, bufi, k * 128:(k + 1) * 128]:
                             e.transpose(o, i_, ident[:R, :R])),
                            sig=(kk == 7))
                    gsl = pv[:, gcol + 8 * half:gcol + 8 * half + 8].unsqueeze(2).to_broadcast([128, 8, R])
                    ev = DVE.do(
                        (lambda e, o=dstT[:, 8 * half:8 * half + 8, t0:t0 + R], i0=bv[:, :, 0:R], i1=gsl:
                         e.tensor_tensor(out=o, in0=i0, in1=i1, op=ALU.mult)),
                        wait=[last_tr])
                    bank_release(b, ev)
                    evs.append(ev)
                return last_tr, evs[-1]

            return a2, pe_part

        tr_toks = {}
        a2_toks = {}
        hnT_done = None

        def xt_ap(i, R):
            if i < 6:
                return R3[:R, i * 2048:(i + 1) * 2048]
            return R1[:R, 8320 + (i - 6) * 2048:8320 + (i - 5) * 2048]

        xtok = []
        for i in range(9):
            R = 128 if i < 8 else 16
            xtok.append(SP.dma((lambda e, o=xt_ap(i, R), s=xh[128 * i:128 * i + R, :]: e.dma_start(out=o, in_=s)),
                               xsem[i], f"x{i}", 1))
        POOL.wait(xtok[7])
        g3_tok = SP.dma(lambda e: e.dma_start(out=g3[:, :], in_=g3b), g3sem, "g3", 1)

        def p0_stats(i):
            R = 128 if i < 8 else 16
            src = xt_ap(i, R)
            a1 = ACT.do(lambda e: e.activation(out=junk[:R, :], in_=src, func=AF.Square,
                                               accum_out=ss1[:R, i:i + 1]), wait=[xtok[i]])
            d1 = ACT.do(lambda e: e.activation(out=rs1[:R, i:i + 1], in_=ss1[:R, i:i + 1], func=AF.Sqrt, scale=1.0 / D,
                                               bias=eps_ap[:R, :]), wait=[a1])
            return DVE.do(lambda e: e.reciprocal(out=rs1[:R, i:i + 1], in_=rs1[:R, i:i + 1]), wait=[d1])

        def p0_rest(i, rtok):
            R = 128 if i < 8 else 16
            src = xt_ap(i, R)
            buf = i % 2
            hb = hnb[:R, buf, :]
            if i % 2 == 0:
                a2 = ACT.do(lambda e: e.activation(out=hb, in_=src, func=AF.Copy, scale=rs1[:R, i:i + 1]),
                            wait=[rtok, tr_toks.get(i - 2)])
            else:
                a2 = DVE.do(lambda e: e.tensor_scalar(out=hb, in0=src, scalar1=rs1[:R, i:i + 1], scalar2=None,
                                                      op0=ALU.mult), wait=[rtok, tr_toks.get(i - 2)])
            a2_toks[i] = a2
            last_tr = None
            ev = None
            for half in range(2):
                b_, ftok = bank_alloc()
                PE.wait(ftok, a2)
                bv = bank_bf(b_)
                for kk in range(8):
                    k = 8 * half + kk
                    last_tr = PE.do(
                        (lambda e, o=bv[:, kk, 0:R], i_=hnb[:R, buf, k * 128:(k + 1) * 128]:
                         e.transpose(o, i_, ident[:R, :R])),
                        sig=(kk == 7))
                gsl = pv[:, PV_G1 + 8 * half:PV_G1 + 8 * half + 8].unsqueeze(2).to_broadcast([128, 8, R])
                ev = DVE.do(
                    (lambda e, o=hnT[:, 8 * half:8 * half + 8, 128 * i:128 * i + R], i0=bv[:, :, 0:R], i1=gsl:
                     e.tensor_tensor(out=o, in0=i0, in1=i1, op=ALU.mult)),
                    wait=[last_tr])
                bank_release(b_, ev)
            tr_toks[i] = last_tr
            return ev

        rt = {}
        hn_tile = {}
        for i in range(9):
            rt[i] = p0_stats(i)
            if i >= 1:
                hn_tile[i - 1] = p0_rest(i - 1, rt[i - 1])
        hn_tile[8] = p0_rest(8, rt[8])
        hnT_done = hn_tile[8]
        last_p0_pe = tr_toks[8]
        last_p0_act = Tok(ACT.sem, ACT.cnt, "act")
        last_p0_dve = Tok(DVE.sem, DVE.cnt, "dve")

        def dump(name, src_ap):
            t = SP.dma((lambda e, o=dbg[name], s=src_ap: e.dma_start(out=o, in_=s)), dsem, "dbg",
                       dump.n + 1, wait=[Tok(DVE.sem, DVE.cnt, "dve"), Tok(PE.sem, PE.cnt, "pe"),
                                         Tok(ACT.sem, ACT.cnt, "act")])
            dump.n += 1
            return t
        dump.n = 0
        final_toks = []

        if debug and phase_limit == 0:
            final_toks.append(dump("hnT", R1[:, 0:8320].bitcast(BF16)))

        last_z = None
        if phase_limit >= 1:
            prev = {"cvread": [None] * 3, "tap": None, "z": [None] * 3}
            ACT.wait(last_p0_dve)
            DVE.wait(last_p0_act)
            for c in range(16):
                wt_gc = w_use(("in", 4096 + 128 * c))
                wt_v = w_use(("in", 6144 + 128 * c))
                wt_gb = w_use(("in", 2048 + 128 * c))
                gc_ev = []
                tok = None
                for bi, (a, b) in enumerate(B3):
                    bank, ftok = bank_alloc()
                    tok = pe_group(bank, b - a, 16, lambda k: wt_gc.ap[:, k, :], lambda k, a=a, b=b: hnT[:, k, a:b],
                                   [ftok, wt_gc.tok, hnT_done])
                    ev = ACT.do((lambda e, o=gcs[:, a:b], i_=bank_f32(bank)[:, 0:b - a]:
                                 e.activation(out=o, in_=i_, func=AF.Copy)),
                                wait=[tok, prev["cvread"][bi]])
                    bank_release(bank, ev)
                    gc_ev.append(ev)
                w_release(wt_gc, tok)
                cv_ev = []
                for bi, (a, b) in enumerate(B3):
                    bank, ftok = bank_alloc()
                    tok = pe_group(bank, b - a, 16, lambda k: wt_v.ap[:, k, :], lambda k, a=a, b=b: hnT[:, k, a:b],
                                   [ftok, wt_v.tok])
                    ev = DVE.do((lambda e, o=cv[:, a:b], i0=bank_f32(bank)[:, 0:b - a], i1=gcs[:, a:b]:
                                 e.tensor_tensor(out=o, in0=i0, in1=i1, op=ALU.mult)),
                                wait=[tok, gc_ev[bi], prev["tap"]])
                    bank_release(bank, ev)
                    cv_ev.append(ev)
                    prev["cvread"][bi] = ev
                w_release(wt_v, tok)
                tap = [None] * 3
                for step in range(3):
                    for bi, (a, b) in enumerate(B2):
                        y = ybuf[:, a - 16:b - 16]
                        sh = 2 - step
                        wcol = pv[:, PV_CW + 3 * c + step:PV_CW + 3 * c + step + 1]
                        if step == 0:
                            tap[bi] = DVE.do((lambda e, y=y, i0=cv[:, a - sh:b - sh], w=wcol:
                                              e.tensor_scalar(out=y, in0=i0, scalar1=w, scalar2=None, op0=ALU.mult)),
                                             wait=[cv_ev, prev["z"][bi]])
                        else:
                            tap[bi] = DVE.do((lambda e, y=y, i0=cv[:, a - sh:b - sh], w=wcol:
                                              e.scalar_tensor_tensor(out=y, in0=i0, scalar=w, in1=y,
                                                                     op0=ALU.mult, op1=ALU.add)),
                                             wait=[tap[bi]])
                prev["tap"] = tap[2]
                for bi, (a, b) in enumerate(B2):
                    bank, ftok = bank_alloc()
                    tok = pe_group(bank, b - a, 16, lambda k: wt_gb.ap[:, k, :], lambda k, a=a, b=b: hnT[:, k, a:b],
                                   [ftok, wt_gb.tok])
                    ev = DVE.do((lambda e, o=zT[:, c, a - 16:b - 16], i0=bank_f32(bank)[:, 0:b - a], i1=ybuf[:, a - 16:b - 16]:
                                 e.tensor_tensor(out=o, in0=i0, in1=i1, op=ALU.mult)),
                                wait=[tok, tap[bi]])
                    bank_release(bank, ev)
                    prev["z"][bi] = ev
                    last_z = ev
                w_release(wt_gb, tok)
            if debug and phase_limit == 1:
                final_toks.append(dump("zT", R1[:, 8320:16512].bitcast(BF16)))

        last_m = None
        last_p1_pe = None
        if phase_limit >= 2:
            prev = {"ubread": None, "s1": None, "s2": None, "ya": None, "ta": [None] * 3, "tb": [None] * 3,
                    "m": [None] * 3}
            for gi in range(4):
                wwin = POOL_W[gi]
                wt_pw = w_use(("pw", gi))
                pooled_toks = []
                for cc in range(4):
                    c = 4 * gi + cc
                    wt_u = w_use(("in", 128 * c))
                    u_ev = []
                    tok = None
                    for bi, (a, b) in enumerate(B3):
                        bank, ftok = bank_alloc()
                        tok = pe_group(bank, b - a, 16, lambda k: wt_u.ap[:, k, :],
                                       lambda k, a=a, b=b: hnT[:, k, a:b], [ftok, wt_u.tok, hnT_done])
                        ev = ACT.do((lambda e, o=ub[:, a:b], i_=bank_f32(bank)[:, 0:b - a]:
                                     e.activation(out=o, in_=i_, func=AF.Copy)),
                                    wait=[tok, prev["ubread"]])
                        bank_release(bank, ev)
                        u_ev.append(ev)
                    w_release(wt_u, tok)
                    cur = ub
                    curtok = u_ev
                    bufs_ = [(S1, "s1"), (S2, "s2")]
                    lag = 1
                    step = 0
                    while lag < wwin:
                        dst, dkey = bufs_[step % 2]
                        off = 2 * lag - 1
                        tk = DVE.do((lambda e, o=dst[:, off:T], i0=cur[:, off:T], i1=cur[:, off - lag:T - lag]:
                                     e.tensor_tensor(out=o, in0=i0, in1=i1, op=ALU.add)),
                                    wait=[curtok, prev[dkey]])
                        cur, curtok = dst, tk
                        prev[dkey] = tk
                        lag *= 2
                        step += 1
                    pt = DVE.do((lambda e, o=pooledT[:, cc, :], i0=cur[:, HALO:T], i1=ub[:, HALO:T], s=1.0 / wwin:
                                 e.scalar_tensor_tensor(out=o, in0=i0, scalar=s, in1=i1,
                                                        op0=ALU.mult, op1=ALU.subtract)),
                                wait=[curtok, u_ev, prev["ya"]])
                    prev["ubread"] = pt
                    prev["s1"] = pt
                    prev["s2"] = pt
                    pooled_toks.append(pt)
                for cc in range(4):
                    c = 4 * gi + cc
                    wt_ga = w_use(("in", 8192 + 128 * c))
                    wt_gbr = w_use(("in", 10240 + 128 * c))
                    wt_co = w_use(("co", 128 * c))
                    ga_ev, gb_ev = [None] * 3, [None] * 3
                    tok = None
                    for bi, (a, b) in enumerate(B2):
                        bank, ftok = bank_alloc()
                        tok = pe_group(bank, b - a, 16, lambda k: wt_ga.ap[:, k, :],
                                       lambda k, a=a, b=b: hnT[:, k, a:b], [ftok, wt_ga.tok])
                        ev = ACT.do((lambda e, o=gA[:, a - 16:b - 16], i_=bank_f32(bank)[:, 0:b - a],
                                     bcol=pv[:, PV_BGA + c:PV_BGA + c + 1]:
                                     e.activation(out=o, in_=i_, func=AF.Sigmoid, bias=bcol)),
                                    wait=[tok, prev["ta"][bi]])
                        bank_release(bank, ev)
                        ga_ev[bi] = ev
                    w_release(wt_ga, tok)
                    for bi, (a, b) in enumerate(B2):
                        bank, ftok = bank_alloc()
                        tok = pe_group(bank, b - a, 16, lambda k: wt_gbr.ap[:, k, :],
                                       lambda k, a=a, b=b: hnT[:, k, a:b], [ftok, wt_gbr.tok])
                        ev = ACT.do((lambda e, o=gB[:, a - 16:b - 16], i_=bank_f32(bank)[:, 0:b - a],
                                     bcol=pv[:, PV_BGB + c:PV_BGB + c + 1]:
                                     e.activation(out=o, in_=i_, func=AF.Sigmoid, bias=bcol)),
                                    wait=[tok, prev["tb"][bi]])
                        bank_release(bank, ev)
                        gb_ev[bi] = ev
                    w_release(wt_gbr, tok)
                    for bi, (a, b) in enumerate(B2):
                        bank, ftok = bank_alloc()
                        tok = pe_group(bank, b - a, 4, lambda kk: wt_pw.ap[:, kk, cc * 128:(cc + 1) * 128],
                                       lambda kk, a=a, b=b: pooledT[:, kk, a - 16:b - 16],
                                       [ftok, wt_pw.tok, pooled_toks])
                        prev["ya"] = tok
                        ev = DVE.do((lambda e, o=ta[:, a - 16:b - 16], i0=bank_f32(bank)[:, 0:b - a],
                                     s=pv[:, PV_PSC + c:PV_PSC + c + 1], i1=gA[:, a - 16:b - 16]:
                                     e.scalar_tensor_tensor(out=o, in0=i0, scalar=s, in1=i1,
                                                            op0=ALU.mult, op1=ALU.mult)),
                                    wait=[tok, ga_ev[bi], prev["m"][bi]])
                        bank_release(bank, ev)
                        prev["ta"][bi] = ev
                    if cc == 3:
                        w_release(wt_pw, tok)
                    for bi, (a, b) in enumerate(B2):
                        bank, ftok = bank_alloc()
                        tok = pe_group(bank, b - a, 16, lambda k: wt_co.ap[:, k, :],
                                       lambda k, a=a, b=b: zT[:, k, a - 16:b - 16], [ftok, wt_co.tok, last_z])
                        ev = DVE.do((lambda e, o=tb[:, a - 16:b - 16], i0=bank_f32(bank)[:, 0:b - a], i1=gB[:, a - 16:b - 16]:
                                     e.tensor_tensor(out=o, in0=i0, in1=i1, op=ALU.mult)),
                                    wait=[tok, gb_ev[bi], prev["m"][bi]])
                        bank_release(bank, ev)
                        prev["tb"][bi] = ev
                        mv = DVE.do((lambda e, o=mT[:, c, a - 16:b - 16], i0=tb[:, a - 16:b - 16], i1=ta[:, a - 16:b - 16]:
                                     e.tensor_tensor(out=o, in0=i0, in1=i1, op=ALU.add)),
                                    wait=[ev, prev["ta"][bi], last_p0_pe, last_p0_act])
                        prev["m"][bi] = mv
                        last_m = mv
                    w_release(wt_co, tok)
                    last_p1_pe = tok
            if debug and phase_limit == 2:
                final_toks.append(dump("mT", R2[:, 0:8192].bitcast(BF16)))

        pending_tr = []
        hn1_done = [None] * 8
        last_p2_pe = None
        if phase_limit >= 3:
            last_p1_act = Tok(ACT.sem, ACT.cnt, "act")
            last_p1_dve = Tok(DVE.sem, DVE.cnt, "dve")
            xr = []
            for t in range(8):
                xr.append(SP.dma((lambda e, o=h1[:, t, :], s=xh[HALO + 128 * t:HALO + 128 * (t + 1), :]:
                                  e.dma_start(out=o, in_=s)),
                                 hsem[t], f"h{t}", 1, wait=[last_p1_pe, last_m]))
            tr2 = {}
            a2s = {}
            for cb in range(4):
                wts = [w_use(("wo", cb, kq)) for kq in range(4)]
                tok = None
                for t in range(8):
                    bank, ftok = bank_alloc()
                    tok = pe_group(bank, 512, 16, lambda k, t=t: mT[:, k, 128 * t:128 * (t + 1)],
                                   lambda k: wts[k // 4].ap[:, k % 4, :],
                                   [ftok, last_m] + [w.tok for w in wts])
                    ev = DVE.do((lambda e, o=h1[:, t, cb * 512:(cb + 1) * 512], i0=bank_f32(bank)[:, 0:512]:
                                 e.tensor_tensor(out=o, in0=i0, in1=o, op=ALU.add)),
                                wait=[tok, xr[t]])
                    bank_release(bank, ev)
                    last_p2_pe = tok
                    if cb == 3:
                        if len(pending_tr) >= 2:
                            tt, pp = pending_tr.pop(0)
                            tr2[tt], hn1_done[tt] = pp()
                        a2, pe_part = norm_transpose(t, 128, h1[:, t, :], ev, ss2, rs2, t, hn1b, t % 2, junk2,
                                                     PV_G2, hn1T, 128 * t, tr2.get(t - 2),
                                                     extra_wait=[last_p1_act, last_p1_dve, last_p1_pe])
                        a2s[t] = a2
                        pending_tr.append((t, pe_part))
                for w in wts:
                    w_release(w, tok)
            if phase_limit == 3:
                while pending_tr:
                    tt, pp = pending_tr.pop(0)
                    tr2[tt], hn1_done[tt] = pp()
                if debug:
                    final_toks.append(dump("h1", R1[:, 0:16384]))
                    final_toks.append(dump("hn1T", R3[:, 0:8192].bitcast(BF16)))

        out_toks = []
        if phase_limit >= 4:
            st = {"sg": [None, None], "sgi": 0, "dn_last": {}, "first": True}

            def flush_tr():
                while pending_tr:
                    tt, pp = pending_tr.pop(0)
                    _, hn1_done[tt] = pp()

            def gu_unit(g, jj, blk, wt_g, wt_u):
                a, b = B2[blk][0] - HALO, B2[blk][1] - HALO
                n = b - a
                need = [hn1_done[t] for t in range(a // 128, (b - 1) // 128 + 1) if hn1_done[t] is not None]
                bg, ftok = bank_alloc()
                tg = pe_group(bg, n, 16, lambda k: wt_g.ap[:, k, :],
                              lambda k: hn1T[:, k, a:b],
                              [ftok, wt_g.tok, last_p2_pe] + need)
                si = st["sgi"] % 2
                st["sgi"] += 1
                ea = ACT.do((lambda e, o=sg[:, si, 0:n], i_=bank_f32(bg)[:, 0:n]:
                             e.activation(out=o, in_=i_, func=AF.Silu)),
                            wait=[tg, st["sg"][si], last_p2_pe])
                bank_release(bg, ea)
                bu, ftok = bank_alloc()
                tu = pe_group(bu, n, 16, lambda k: wt_u.ap[:, k, :],
                              lambda k: hn1T[:, k, a:b],
                              [ftok, wt_u.tok])
                ed = DVE.do((lambda e, o=actT[:, g % 2, jj, a:b],
                             i0=bank_f32(bu)[:, 0:n], i1=sg[:, si, 0:n]:
                             e.tensor_tensor(out=o, in0=i0, in1=i1, op=ALU.mult)),
                            wait=[tu, ea, st["dn_last"].get(g - 2), last_p2_pe])
                bank_release(bu, ed)
                st["sg"][si] = ed
                st["act_done_%d" % g] = ed
                return tg, tu

            def gate_up(g):
                if g == 0:
                    wts = []
                    for jj in range(4):
                        j = 4 * g + jj
                        wts.append((w_use(("gu", 128 * j)), w_use(("gu", FF + 128 * j))))
                    for blk in range(len(B2)):
                        for jj in range(4):
                            if blk == 0 and jj == 2 and pending_tr:
                                flush_tr()
                            tg, tu = gu_unit(g, jj, blk, wts[jj][0], wts[jj][1])
                            if blk == len(B2) - 1:
                                w_release(wts[jj][0], tg)
                                w_release(wts[jj][1], tu)
                    return
                for jj in range(4):
                    j = 4 * g + jj
                    wt_g = w_use(("gu", 128 * j))
                    wt_u = w_use(("gu", FF + 128 * j))
                    tg = tu = None
                    for blk in range(len(B2)):
                        tg, tu = gu_unit(g, jj, blk, wt_g, wt_u)
                    w_release(wt_g, tg)
                    w_release(wt_u, tu)

            fin = {}
            fin_pool = {}
            fin_scale = {}

            tmpg2 = R2[:, 5120:7168].rearrange("p (b n) -> p b n", b=2)

            def final_out(t, d1):
                ev_t, p3 = fin_pool[t]
                d2 = DVE.do((lambda e, o=rs3[:, t:t + 1]: e.reciprocal(out=o, in_=o)), wait=[d1])
                d3 = DVE.do((lambda e, o=h1[:, t, 0:1024], s=rs3[:, t:t + 1]:
                             e.scalar_tensor_tensor(out=o, in0=o, scalar=s, in1=g3[:, 0:1024],
                                                    op0=ALU.mult, op1=ALU.mult)),
                            wait=[d2, g3_tok])
                a3 = ACT.do((lambda e, o=h1[:, t, 1024:2048], s=rs3[:, t:t + 1], tg=tmpg2[:, t % 2, :]:
                             e.activation(out=o, in_=tg, func=AF.Copy, scale=s)),
                            wait=[d2, p3])
                fin_scale[t] = a3
                out_toks.append(ACT.dma((lambda e, o=out[128 * t:128 * (t + 1), 1024:2048], s=h1[:, t, 1024:2048]:
                                         e.dma_start(out=o, in_=s)),
                                        osem2[t], f"ob{t}", 1, wait=[a3]))
                out_toks.append(ACT.dma((lambda e, o=out[128 * t:128 * (t + 1), 0:1024], s=h1[:, t, 0:1024]:
                                         e.dma_start(out=o, in_=s)),
                                        osem[t], f"o{t}", 1, wait=[d3]))

            def down(g, merged_prev=False):
                gs = [g - 1, g] if merged_prev else [g]
                wts_all = [[w_use(("wd", gg, cb)) for cb in range(4)] for gg in gs]
                wts = [w for ws_ in wts_all for w in ws_]
                nk = 4 * len(gs)
                tok = None
                for t in range(8):
                    ev = None
                    for cb in range(4):
                        bank, ftok = bank_alloc()
                        tok = pe_group(bank, 512, nk,
                                       lambda kk, t=t: actT[:, gs[kk // 4] % 2, kk % 4, 128 * t:128 * (t + 1)],
                                       lambda kk, cb=cb: wts_all[kk // 4][cb].ap[:, kk % 4, :],
                                       [ftok] + [st["act_done_%d" % gg] for gg in gs] + [w.tok for w in wts])
                        ev = DVE.do((lambda e, o=h1[:, t, cb * 512:(cb + 1) * 512], i0=bank_f32(bank)[:, 0:512]:
                                     e.tensor_tensor(out=o, in0=i0, in1=o, op=ALU.add)),
                                    wait=[tok])
                        bank_release(bank, ev)
                    if g == NPASS - 1:
                        a1 = ACT.do((lambda e, i_=h1[:, t, :], acc=ss3[:, t:t + 1]:
                                     e.activation(out=junk2[:, :], in_=i_, func=AF.Square,
                                                  accum_out=acc)),
                                    wait=[ev])
                        d1 = ACT.do((lambda e, o=rs3[:, t:t + 1], i0=ss3[:, t:t + 1]:
                                     e.activation(out=o, in_=i0, func=AF.Sqrt, scale=1.0 / D, bias=eps_ap[:, :])), wait=[a1])
                        if t >= 1:
                            final_out(t - 1, fin[t - 1])
                        fin[t] = d1
                        p3 = POOL.do((lambda e, i0=h1[:, t, 1024:2048], tg=tmpg2[:, t % 2, :]:
                                      e.tensor_tensor(out=tg, in0=i0, in1=g3[:, 1024:2048], op=ALU.mult)),
                                     wait=[ev, g3_tok, fin_scale.get(t - 2)])
                        fin_pool[t] = (ev, p3)
                if g == NPASS - 1:
                    final_out(7, fin[7])
                st["dn_last"][g] = tok
                for w in wts:
                    w_release(w, tok)

            gate_up(0)
            for g in range(NPASS):
                if g + 1 < NPASS:
                    gate_up(g + 1)
                if g == NPASS - 2:
                    continue
                down(g, merged_prev=(g == NPASS - 1))

        ACT.wait(*out_toks)
        SP.wait(*final_toks)
        SP.wait(Tok(PE.sem, PE.cnt, "pe"), Tok(ACT.sem, ACT.cnt, "act"), Tok(DVE.sem, DVE.cnt, "dve"), g3_tok)
        if POOL.cnt:
            SP.wait(Tok(POOL.sem, POOL.cnt, "pool"))
        for s in range(NS):
            if wst["uses"][s]:
                SP.wait(Tok(wsem[s], 16 * wst["uses"][s], f"w{s}"))
        for s in range(2):
            if wst["pwuses"][s]:
                SP.wait(Tok(pwsem[s], 16 * wst["pwuses"][s], f"pw{s}"))

        with nc.Block() as block:
            @block.sync
            def _(e):
                SP.emit(e)

            @block.gpsimd
            def _(e):
                POOL.emit(e)

            @block.tensor
            def _(e):
                PE.emit(e)

            @block.scalar
            def _(e):
                ACT.emit(e)

            @block.vector
            def _(e):
                DVE.emit(e)
    return nc


def make_in_maps(x, meta_tokens, norm_mix_g, w_in, b_gate, pool_w, pool_scale, conv_w, conv_out_w, w_o,
                 norm_ffn_g, w_gate_up, w_down, norm_final_g):
    f = lambda a: np.ascontiguousarray(np.asarray(a, dtype=np.float32))
    x = f(x)
    meta = f(meta_tokens)

    def fm(v):
        return f(v).reshape(16, 128).T

    pvec = np.zeros((128, 128), np.float32)
    pvec[:, PV_G1:PV_G1 + 16] = fm(norm_mix_g)
    bg = f(b_gate)
    pvec[:, PV_BGA:PV_BGA + 16] = fm(bg[:D])
    pvec[:, PV_BGB:PV_BGB + 16] = fm(bg[D:])
    pvec[:, PV_PSC:PV_PSC + 16] = fm(pool_scale)
    cw = f(conv_w)
    for tap in range(3):
        pvec[:, PV_CW + tap:PV_CW + 48:3] = fm(cw[tap])
    pvec[:, PV_G2:PV_G2 + 16] = fm(norm_ffn_g)
    g3b = np.ascontiguousarray(np.broadcast_to(f(norm_final_g)[None, :], (128, D)))
    ident = np.eye(128, dtype=np.float32).astype(ml_dtypes.bfloat16)
    shared = {
        "w_in": f(w_in), "pw2": np.ascontiguousarray(f(pool_w).transpose(1, 0, 2).reshape(512, 2048)), "conv_out_w": f(conv_out_w),
        "w_o": f(w_o), "w_gate_up": f(w_gate_up), "w_down": f(w_down),
        "pvec": pvec, "g3b": g3b, "ident": ident,
    }
    maps = []
    for i in range(N_CORES):
        b, half = i // 2, i % 2
        halo = meta if half == 0 else x[b, TR - HALO:TR]
        xh = np.concatenate([halo, x[b, half * TR:(half + 1) * TR]], axis=0)
        m = dict(shared)
        m["xh"] = np.ascontiguousarray(xh)
        maps.append(m)
    return maps


def kernel(x, meta_tokens, norm_mix_g, w_in, b_gate, pool_w, pool_scale, conv_w, conv_out_w, w_o,
           norm_ffn_g, w_gate_up, w_down, norm_final_g):
    maps = make_in_maps(x, meta_tokens, norm_mix_g, w_in, b_gate, pool_w, pool_scale, conv_w, conv_out_w,
                        w_o, norm_ffn_g, w_gate_up, w_down, norm_final_g)
    nc = build_program()
    res = run_bass_kernel_spmd(nc, maps, core_ids=list(range(N_CORES)))
    outp = np.empty((4, 2 * TR, D), np.float32)
    for i in range(N_CORES):
        b, half = i // 2, i % 2
        outp[b, half * TR:(half + 1) * TR] = res.results[i]["out"]
    return outp
```

```python
from contextlib import ExitStack

import numpy as np
import ml_dtypes
import concourse.bass as bass
import concourse.mybir as mybir
from concourse.bass_utils import run_bass_kernel_spmd

F32 = mybir.dt.float32
BF16 = mybir.dt.bfloat16
AF = mybir.ActivationFunctionType
ALU = mybir.AluOpType

D = 2048
T = 1040
TR = 1024
HALO = 16
FF = 5632
NPASS = 11
B3 = [(0, 336), (336, 688), (688, 1040)]
B2 = [(16, 528), (528, 1040)]
POOL_W = (2, 4, 8, 16)
EPS = 1e-6
NS = 10
N_CORES = 8

PV_G1, PV_BGA, PV_BGB, PV_PSC, PV_CW, PV_G2 = 0, 16, 32, 48, 64, 112


class Tok:
    __slots__ = ("sem", "val", "key")

    def __init__(self, sem, val, key):
        self.sem, self.val, self.key = sem, val, key


class Eng:
    def __init__(self, name):
        self.name = name
        self.ops = []
        self.cnt = 0
        self.sem = None
        self.seen = {}

    def wait(self, *toks):
        for t in toks:
            if t is None:
                continue
            if isinstance(t, (list, tuple)):
                self.wait(*t)
                continue
            if self.seen.get(t.key, 0) >= t.val:
                continue
            self.seen[t.key] = t.val
            self.ops.append(("wait", t.sem, t.val))

    def do(self, fn, wait=(), sig=True):
        self.wait(*wait)
        if sig:
            self.cnt += 1
            self.ops.append(("op", fn, self.sem, 1))
            return Tok(self.sem, self.cnt, self.name)
        self.ops.append(("op", fn, None, 0))
        return None

    def dma(self, fn, sem, semkey, count, wait=()):
        self.wait(*wait)
        self.ops.append(("op", fn, sem, 16))
        return Tok(sem, 16 * count, semkey)

    def emit(self, e):
        for op in self.ops:
            if op[0] == "wait":
                e.wait_ge(op[1], op[2])
            else:
                ins = op[1](e)
                if op[2] is not None:
                    ins.then_inc(op[2], op[3])


def weight_plan():
    plan = []
    for c in range(16):
        plan += [("in", 4096 + 128 * c), ("in", 6144 + 128 * c), ("in", 2048 + 128 * c)]
    for gi in range(4):
        plan.append(("pw", gi))
        for cc in range(4):
            plan.append(("in", 128 * (4 * gi + cc)))
        for cc in range(4):
            c = 4 * gi + cc
            plan += [("in", 8192 + 128 * c), ("in", 10240 + 128 * c), ("co", 128 * c)]
    for cb in range(4):
        for kq in range(4):
            plan.append(("wo", cb, kq))

    def gu(g):
        out = []
        for jj in range(4):
            j = 4 * g + jj
            out += [("gu", 128 * j), ("gu", FF + 128 * j)]
        return out

    plan += gu(0)
    for g in range(NPASS):
        if g + 1 < NPASS:
            plan += gu(g + 1)
        for cb in range(4):
            plan.append(("wd", g, cb))
    return plan


def build_program(phase_limit=99, debug=False):
    nc = bass.Bass("TRN2", target_bir_lowering=False)
    dram = {}

    def din(name, shape, dt=F32):
        dram[name] = nc.dram_tensor(name, list(shape), dt, kind="ExternalInput").ap()
        return dram[name]

    xh = din("xh", [T, D])
    w_in = din("w_in", [D, 12288])
    pool_w = din("pw2", [512, 2048])
    conv_out_w = din("conv_out_w", [D, D])
    w_o = din("w_o", [D, D])
    w_gu = din("w_gate_up", [D, 2 * FF])
    w_dn = din("w_down", [FF, D])
    pvec = din("pvec", [128, 128])
    g3b = din("g3b", [128, D])
    ident_d = din("ident", [128, 128], BF16)
    out = nc.dram_tensor("out", [TR, D], F32, kind="ExternalOutput").ap()
    dbg = {}
    if debug:
        dbg["hnT"] = nc.dram_tensor("dbg_hnT", [128, 16 * T], BF16, kind="ExternalOutput").ap()
        dbg["zT"] = nc.dram_tensor("dbg_zT", [128, 16 * TR], BF16, kind="ExternalOutput").ap()
        dbg["mT"] = nc.dram_tensor("dbg_mT", [128, 16 * TR], BF16, kind="ExternalOutput").ap()
        dbg["h1"] = nc.dram_tensor("dbg_h1", [128, 8 * D], F32, kind="ExternalOutput").ap()
        dbg["hn1T"] = nc.dram_tensor("dbg_hn1T", [128, 16 * TR], BF16, kind="ExternalOutput").ap()

    w_in_v = w_in.rearrange("(ko p) c -> p ko c", p=128)
    co_v = conv_out_w.rearrange("(ko p) c -> p ko c", p=128)
    wo_v = w_o.rearrange("(ko p) c -> p ko c", p=128)
    gu_v = w_gu.rearrange("(ko p) c -> p ko c", p=128)
    wd_v = w_dn.rearrange("(ko p) c -> p ko c", p=128)
    pw_v = pool_w.rearrange("(kk p) c -> p kk c", p=128)

    with ExitStack() as es:
        def sb(name, n, dt=F32):
            return es.enter_context(nc.sbuf_tensor(name, [128, n], dt))

        R1 = sb("R1", 16640)
        R2 = sb("R2", 8192)
        R3 = sb("R3", 12368)
        WR = sb("WR", NS * 1024)
        PWB = sb("PWB", 2048)
        pv = sb("pv", 128)
        g3 = sb("g3", D)
        ident = sb("ident_sb", 128, BF16)
        stats = sb("stats", 64)
        ps = es.enter_context(nc.psum_tensor("ps", [128, 8, 512], F32))

        def sem(name):
            return es.enter_context(nc.semaphore(name))

        PE, ACT, DVE, POOL, SP = Eng("pe"), Eng("act"), Eng("dve"), Eng("pool"), Eng("sp")
        for e in (PE, ACT, DVE, POOL, SP):
            e.sem = sem("s_" + e.name)
        wsem = [sem(f"w{i}") for i in range(NS)]
        pwsem = [sem(f"pw{i}") for i in range(2)]
        xsem = [sem(f"x{i}") for i in range(9)]
        hsem = [sem(f"h{i}") for i in range(8)]
        osem = [sem(f"o{i}") for i in range(8)]
        osem2 = [sem(f"ob{i}") for i in range(8)]
        csem = sem("const")
        g3sem = sem("g3")
        dsem = sem("dbg")

        hnT = R1[:, 0:8320].bitcast(BF16).rearrange("p (k t) -> p k t", k=16)
        zT = R1[:, 8320:16512].bitcast(BF16).rearrange("p (k t) -> p k t", k=16)
        h1 = R1[:, 0:16384].rearrange("p (t d) -> p t d", t=8)
        mT = R2[:, 0:8192].bitcast(BF16).rearrange("p (k t) -> p k t", k=16)
        junk = R2[:, 4096:5120].bitcast(BF16)
        hnb = R2[:, 5120:7168].bitcast(BF16).rearrange("p (b d) -> p b d", b=2)
        actT = R2[:, 0:4096].bitcast(BF16).rearrange("p (b j t) -> p b j t", b=2, j=4)
        sg = R2[:, 4096:5120].rearrange("p (b n) -> p b n", b=2)
        gcs = R3[:, 0:1040]
        cv = R3[:, 1040:2080]
        ub = R3[:, 2080:3120]
        S1 = R3[:, 3120:4160]
        S2 = R3[:, 4160:5200]
        pooledT = R3[:, 5200:7248].bitcast(BF16).rearrange("p (k t) -> p k t", k=4)
        ybuf = R3[:, 7248:8272].rearrange("p (b n) -> p b n", b=2)
        gA = R3[:, 8272:9296].rearrange("p (b n) -> p b n", b=2)
        gB = R3[:, 9296:10320].rearrange("p (b n) -> p b n", b=2)
        ta = R3[:, 10320:11344].rearrange("p (b n) -> p b n", b=2)
        tb = R3[:, 11344:12368].rearrange("p (b n) -> p b n", b=2)
        hn1T = R3[:, 0:8192].bitcast(BF16).rearrange("p (k t) -> p k t", k=16)
        hn1b = R3[:, 8192:10240].bitcast(BF16).rearrange("p (b d) -> p b d", b=2)
        junk2 = R3[:, 10240:11264].bitcast(BF16)
        wring = WR[:, :].bitcast(BF16).rearrange("p (s n) -> p s n", s=NS)
        pwbuf = PWB[:, :].bitcast(BF16).rearrange("p (s n) -> p s n", s=2)
        ss1, rs1 = stats[:, 0:9], stats[:, 9:18]
        ss2, rs2 = stats[:, 18:26], stats[:, 26:34]
        ss3, rs3 = stats[:, 34:42], stats[:, 42:50]
        eps_ap = stats[:, 50:51]

        ring = {"u": 0, "free": [None] * 8}

        def bank_alloc():
            b = ring["u"] % 8
            ring["u"] += 1
            return b, ring["free"][b]

        def bank_release(b, *toks):
            ring["free"][b] = list(toks)

        def bank_f32(b):
            return ps[:, b, :]

        def bank_bf(b):
            return ps[:, b, :].bitcast(BF16).rearrange("p (k t) -> p k t", k=8)

        plan = weight_plan()
        ring_idx = {}
        ring_at = []
        for i_, key_ in enumerate(plan):
            if key_[0] != "pw":
                ring_idx[i_] = len(ring_at)
                ring_at.append(i_)
        wst = {"next_dma": 0, "cur": -1, "rel": {}, "tok": {}, "uses": [0] * NS, "pwuses": [0, 0]}

        def w_src(key):
            kind = key[0]
            if kind == "in":
                return w_in_v[:, :, key[1]:key[1] + 128], 16
            if kind == "co":
                return co_v[:, :, key[1]:key[1] + 128], 16
            if kind == "gu":
                return gu_v[:, :, key[1]:key[1] + 128], 16
            if kind == "pw":
                return pw_v[:, :, key[1] * 512:(key[1] + 1) * 512], 4
            if kind == "wo":
                return wo_v[:, 4 * key[2]:4 * key[2] + 4, key[1] * 512:(key[1] + 1) * 512], 4
            if kind == "wd":
                return wd_v[:, 4 * key[1]:4 * key[1] + 4, key[2] * 512:(key[2] + 1) * 512], 4
            raise KeyError(key)

        def w_view(i):
            key = plan[i]
            _, kdim = w_src(key)
            if key[0] == "pw":
                return pwbuf[:, key[1] % 2, :].rearrange("p (k c) -> p k c", k=kdim)
            return wring[:, ring_idx[i] % NS, :].rearrange("p (k c) -> p k c", k=kdim)

        def w_pump():
            while wst["next_dma"] < len(plan) and wst["next_dma"] <= wst["cur"] + NS - 1:
                j = wst["next_dma"]
                key = plan[j]
                waits = []
                if key[0] == "pw":
                    if key[1] >= 2:
                        pj = plan.index(("pw", key[1] - 2))
                        if pj not in wst["rel"]:
                            break
                        waits = [wst["rel"][pj]]
                    s = key[1] % 2
                    wst["pwuses"][s] += 1
                    semh, semk, cnt = pwsem[s], f"pw{s}", wst["pwuses"][s]
                else:
                    r = ring_idx[j]
                    if r >= NS:
                        pj = ring_at[r - NS]
                        if pj not in wst["rel"]:
                            break
                        waits = [wst["rel"][pj]]
                    s = r % NS
                    wst["uses"][s] += 1
                    semh, semk, cnt = wsem[s], f"w{s}", wst["uses"][s]
                src, _ = w_src(key)
                dst = w_view(j)
                wst["tok"][j] = POOL.dma(
                    (lambda e, dst=dst, src=src: e.dma_start(out=dst, in_=src)),
                    semh, semk, cnt, wait=waits)
                wst["next_dma"] += 1

        class WT:
            pass

        def w_use(key):
            wst["cur"] += 1
            i = wst["cur"]
            assert plan[i] == key, (i, plan[i], key)
            w_pump()
            assert i in wst["tok"], f"weight tile {i} {key} not prefetched (NS too small)"
            wt = WT()
            wt.i, wt.ap, wt.tok = i, w_view(i), wst["tok"][i]
            return wt

        def w_release(wt, tok):
            wst["rel"][wt.i] = tok
            w_pump()

        c1 = SP.dma(lambda e: e.dma_start(out=pv[:, :], in_=pvec), csem, "const", 1)
        c3 = SP.dma(lambda e: e.dma_start(out=ident[:, :], in_=ident_d), csem, "const", 2)
        c4 = DVE.do(lambda e: e.memset(eps_ap, EPS))
        const_tok = c3
        ACT.wait(c4)
        ACT.do(lambda e: e.activation(out=stats[:, 52:53], in_=eps_ap, func=AF.Sqrt), sig=False)
        for e in (PE, ACT, DVE):
            e.wait(const_tok)

        def pe_group(bank, n, K, lhs, rhs, waits, view=None):
            PE.wait(*waits)
            o = view if view is not None else bank_f32(bank)[:, 0:n]
            tok = None
            for k in range(K):
                tok = PE.do(
                    (lambda e, o=o, l=lhs(k), r=rhs(k), st=(k == 0), sp=(k == K - 1):
                     e.matmul(o, lhsT=l, rhs=r, start=st, stop=sp)),
                    sig=(k == K - 1))
            return tok

        def norm_transpose(i, R, src_ap, src_tok, ss, rs, col, bufs, bufi, jk, gcol, dstT, t0, buf_free, extra_wait=()):
            a1 = ACT.do(lambda e: e.activation(out=jk[:R, :], in_=src_ap, func=AF.Square,
                                               accum_out=ss[:R, col:col + 1]),
                        wait=[src_tok, *extra_wait])
            d1 = ACT.do(lambda e: e.activation(out=rs[:R, col:col + 1], in_=ss[:R, col:col + 1], func=AF.Sqrt, scale=1.0 / D,
                                               bias=eps_ap[:R, :]),
                        wait=[a1])
            d2 = DVE.do(lambda e: e.reciprocal(out=rs[:R, col:col + 1], in_=rs[:R, col:col + 1]),
                        wait=[d1])
            hb = bufs[:R, bufi, :]
            a2 = ACT.do(lambda e: e.activation(out=hb, in_=src_ap, func=AF.Copy, scale=rs[:R, col:col + 1]),
                        wait=[d2, buf_free])

            def pe_part():
                last_tr = None
                evs = []
                for half in range(2):
                    b, ftok = bank_alloc()
                    PE.wait(ftok, a2)
                    bv = bank_bf(b)
                    for kk in range(8):
                        k = 8 * half + kk
                        last_tr = PE.do(
                            (lambda e, o=bv[:, kk, 0:R], i_=bufs[:R, bufi, k * 128:(k + 1) * 128]:
                             e.transpose(o, i_, ident[:R, :R])),
                            sig=(kk == 7))
                    gsl = pv[:, gcol + 8 * half:gcol + 8 * half + 8].unsqueeze(2).to_broadcast([128, 8, R])
                    ev = DVE.do(
                        (lambda e, o=dstT[:, 8 * half:8 * half + 8, t0:t0 + R], i0=bv[:, :, 0:R], i1=gsl:
                         e.tensor_tensor(out=o, in0=i0, in1=i1, op=ALU.mult)),
                        wait=[last_tr])
                    bank_release(b, ev)
                    evs.append(ev)
                return last_tr, evs[-1]

            return a2, pe_part

        tr_toks = {}
        a2_toks = {}
        hnT_done = None

        def xt_ap(i, R):
            if i < 6:
                return R3[:R, i * 2048:(i + 1) * 2048]
            return R1[:R, 8320 + (i - 6) * 2048:8320 + (i - 5) * 2048]

        xtok = []
        for i in range(9):
            R = 128 if i < 8 else 16
            xtok.append(SP.dma((lambda e, o=xt_ap(i, R), s=xh[128 * i:128 * i + R, :]: e.dma_start(out=o, in_=s)),
                               xsem[i], f"x{i}", 1))
        POOL.wait(xtok[6])
        g3_tok = SP.dma(lambda e: e.dma_start(out=g3[:, :], in_=g3b), g3sem, "g3", 1)

        def p0_stats(i):
            R = 128 if i < 8 else 16
            src = xt_ap(i, R)
            a1 = ACT.do(lambda e: e.activation(out=junk[:R, :], in_=src, func=AF.Square,
                                               accum_out=ss1[:R, i:i + 1]), wait=[xtok[i]])
            d1 = ACT.do(lambda e: e.activation(out=rs1[:R, i:i + 1], in_=ss1[:R, i:i + 1], func=AF.Sqrt, scale=1.0 / D,
                                               bias=eps_ap[:R, :]), wait=[a1])
            return DVE.do(lambda e: e.reciprocal(out=rs1[:R, i:i + 1], in_=rs1[:R, i:i + 1]), wait=[d1])

        def p0_rest(i, rtok):
            R = 128 if i < 8 else 16
            src = xt_ap(i, R)
            buf = i % 2
            hb = hnb[:R, buf, :]
            if i % 2 == 0:
                a2 = ACT.do(lambda e: e.activation(out=hb, in_=src, func=AF.Copy, scale=rs1[:R, i:i + 1]),
                            wait=[rtok, tr_toks.get(i - 2)])
            else:
                a2 = DVE.do(lambda e: e.tensor_scalar(out=hb, in0=src, scalar1=rs1[:R, i:i + 1], scalar2=None,
                                                      op0=ALU.mult), wait=[rtok, tr_toks.get(i - 2)])
            a2_toks[i] = a2
            last_tr = None
            ev = None
            for half in range(2):
                b_, ftok = bank_alloc()
                PE.wait(ftok, a2)
                bv = bank_bf(b_)
                for kk in range(8):
                    k = 8 * half + kk
                    last_tr = PE.do(
                        (lambda e, o=bv[:, kk, 0:R], i_=hnb[:R, buf, k * 128:(k + 1) * 128]:
                         e.transpose(o, i_, ident[:R, :R])),
                        sig=(kk == 7))
                gsl = pv[:, PV_G1 + 8 * half:PV_G1 + 8 * half + 8].unsqueeze(2).to_broadcast([128, 8, R])
                ev = DVE.do(
                    (lambda e, o=hnT[:, 8 * half:8 * half + 8, 128 * i:128 * i + R], i0=bv[:, :, 0:R], i1=gsl:
                     e.tensor_tensor(out=o, in0=i0, in1=i1, op=ALU.mult)),
                    wait=[last_tr])
                bank_release(b_, ev)
            tr_toks[i] = last_tr
            return ev

        rt = {}
        hn_tile = {}
        for i in range(9):
            rt[i] = p0_stats(i)
            if i >= 1:
                hn_tile[i - 1] = p0_rest(i - 1, rt[i - 1])
        hn_tile[8] = p0_rest(8, rt[8])
        hnT_done = hn_tile[8]
        last_p0_pe = tr_toks[8]
        last_p0_act = Tok(ACT.sem, ACT.cnt, "act")
        last_p0_dve = Tok(DVE.sem, DVE.cnt, "dve")

        def dump(name, src_ap):
            t = SP.dma((lambda e, o=dbg[name], s=src_ap: e.dma_start(out=o, in_=s)), dsem, "dbg",
                       dump.n + 1, wait=[Tok(DVE.sem, DVE.cnt, "dve"), Tok(PE.sem, PE.cnt, "pe"),
                                         Tok(ACT.sem, ACT.cnt, "act")])
            dump.n += 1
            return t
        dump.n = 0
        final_toks = []

        if debug and phase_limit == 0:
            final_toks.append(dump("hnT", R1[:, 0:8320].bitcast(BF16)))

        last_z = None
        if phase_limit >= 1:
            prev = {"cvread": [None] * 3, "tap": None, "z": [None, None]}
            ACT.wait(last_p0_dve)
            DVE.wait(last_p0_act)
            for c in range(16):
                wt_gc = w_use(("in", 4096 + 128 * c))
                wt_v = w_use(("in", 6144 + 128 * c))
                wt_gb = w_use(("in", 2048 + 128 * c))
                gc_ev = []
                tok = None
                for bi, (a, b) in enumerate(B3):
                    bank, ftok = bank_alloc()
                    tok = pe_group(bank, b - a, 16, lambda k: wt_gc.ap[:, k, :], lambda k, a=a, b=b: hnT[:, k, a:b],
                                   [ftok, wt_gc.tok, hnT_done])
                    ev = ACT.do((lambda e, o=gcs[:, a:b], i_=bank_f32(bank)[:, 0:b - a]:
                                 e.activation(out=o, in_=i_, func=AF.Copy)),
                                wait=[tok, prev["cvread"][bi]])
                    bank_release(bank, ev)
                    gc_ev.append(ev)
                w_release(wt_gc, tok)
                cv_ev = []
                for bi, (a, b) in enumerate(B3):
                    bank, ftok = bank_alloc()
                    tok = pe_group(bank, b - a, 16, lambda k: wt_v.ap[:, k, :], lambda k, a=a, b=b: hnT[:, k, a:b],
                                   [ftok, wt_v.tok])
                    ev = DVE.do((lambda e, o=cv[:, a:b], i0=bank_f32(bank)[:, 0:b - a], i1=gcs[:, a:b]:
                                 e.tensor_tensor(out=o, in0=i0, in1=i1, op=ALU.mult)),
                                wait=[tok, gc_ev[bi], prev["tap"]])
                    bank_release(bank, ev)
                    cv_ev.append(ev)
                    prev["cvread"][bi] = ev
                w_release(wt_v, tok)
                tap = [None, None]
                for step in range(3):
                    for bi, (a, b) in enumerate(B2):
                        y = ybuf[:, bi, :]
                        sh = 2 - step
                        wcol = pv[:, PV_CW + 3 * c + step:PV_CW + 3 * c + step + 1]
                        if step == 0:
                            tap[bi] = DVE.do((lambda e, y=y, i0=cv[:, a - sh:b - sh], w=wcol:
                                              e.tensor_scalar(out=y, in0=i0, scalar1=w, scalar2=None, op0=ALU.mult)),
                                             wait=[cv_ev, prev["z"][bi]])
                        else:
                            tap[bi] = DVE.do((lambda e, y=y, i0=cv[:, a - sh:b - sh], w=wcol:
                                              e.scalar_tensor_tensor(out=y, in0=i0, scalar=w, in1=y,
                                                                     op0=ALU.mult, op1=ALU.add)),
                                             wait=[tap[bi]])
                prev["tap"] = tap[1]
                for bi, (a, b) in enumerate(B2):
                    bank, ftok = bank_alloc()
                    tok = pe_group(bank, 512, 16, lambda k: wt_gb.ap[:, k, :], lambda k, a=a, b=b: hnT[:, k, a:b],
                                   [ftok, wt_gb.tok])
                    ev = DVE.do((lambda e, o=zT[:, c, a - 16:b - 16], i0=bank_f32(bank)[:, 0:512], i1=ybuf[:, bi, :]:
                                 e.tensor_tensor(out=o, in0=i0, in1=i1, op=ALU.mult)),
                                wait=[tok, tap[bi]])
                    bank_release(bank, ev)
                    prev["z"][bi] = ev
                    last_z = ev
                w_release(wt_gb, tok)
            if debug and phase_limit == 1:
                final_toks.append(dump("zT", R1[:, 8320:16512].bitcast(BF16)))

        last_m = None
        last_p1_pe = None
        if phase_limit >= 2:
            prev = {"ubread": None, "s1": None, "s2": None, "ya": None, "ta": [None, None], "tb": [None, None],
                    "m": [None, None]}
            for gi in range(4):
                wwin = POOL_W[gi]
                wt_pw = w_use(("pw", gi))
                pooled_toks = []
                for cc in range(4):
                    c = 4 * gi + cc
                    wt_u = w_use(("in", 128 * c))
                    u_ev = []
                    tok = None
                    for bi, (a, b) in enumerate(B3):
                        bank, ftok = bank_alloc()
                        tok = pe_group(bank, b - a, 16, lambda k: wt_u.ap[:, k, :],
                                       lambda k, a=a, b=b: hnT[:, k, a:b], [ftok, wt_u.tok, hnT_done])
                        ev = ACT.do((lambda e, o=ub[:, a:b], i_=bank_f32(bank)[:, 0:b - a]:
                                     e.activation(out=o, in_=i_, func=AF.Copy)),
                                    wait=[tok, prev["ubread"]])
                        bank_release(bank, ev)
                        u_ev.append(ev)
                    w_release(wt_u, tok)
                    cur = ub
                    curtok = u_ev
                    bufs_ = [(S1, "s1"), (S2, "s2")]
                    lag = 1
                    step = 0
                    while lag < wwin:
                        dst, dkey = bufs_[step % 2]
                        off = 2 * lag - 1
                        tk = DVE.do((lambda e, o=dst[:, off:T], i0=cur[:, off:T], i1=cur[:, off - lag:T - lag]:
                                     e.tensor_tensor(out=o, in0=i0, in1=i1, op=ALU.add)),
                                    wait=[curtok, prev[dkey]])
                        cur, curtok = dst, tk
                        prev[dkey] = tk
                        lag *= 2
                        step += 1
                    pt = DVE.do((lambda e, o=pooledT[:, cc, :], i0=cur[:, HALO:T], i1=ub[:, HALO:T], s=1.0 / wwin:
                                 e.scalar_tensor_tensor(out=o, in0=i0, scalar=s, in1=i1,
                                                        op0=ALU.mult, op1=ALU.subtract)),
                                wait=[curtok, u_ev, prev["ya"]])
                    prev["ubread"] = pt
                    prev["s1"] = pt
                    prev["s2"] = pt
                    pooled_toks.append(pt)
                for cc in range(4):
                    c = 4 * gi + cc
                    wt_ga = w_use(("in", 8192 + 128 * c))
                    wt_gbr = w_use(("in", 10240 + 128 * c))
                    wt_co = w_use(("co", 128 * c))
                    ga_ev, gb_ev = [None, None], [None, None]
                    tok = None
                    for bi, (a, b) in enumerate(B2):
                        bank, ftok = bank_alloc()
                        tok = pe_group(bank, 512, 16, lambda k: wt_ga.ap[:, k, :],
                                       lambda k, a=a, b=b: hnT[:, k, a:b], [ftok, wt_ga.tok])
                        ev = ACT.do((lambda e, o=gA[:, bi, :], i_=bank_f32(bank)[:, 0:512],
                                     bcol=pv[:, PV_BGA + c:PV_BGA + c + 1]:
                                     e.activation(out=o, in_=i_, func=AF.Sigmoid, bias=bcol)),
                                    wait=[tok, prev["ta"][bi]])
                        bank_release(bank, ev)
                        ga_ev[bi] = ev
                    w_release(wt_ga, tok)
                    for bi, (a, b) in enumerate(B2):
                        bank, ftok = bank_alloc()
                        tok = pe_group(bank, 512, 16, lambda k: wt_gbr.ap[:, k, :],
                                       lambda k, a=a, b=b: hnT[:, k, a:b], [ftok, wt_gbr.tok])
                        ev = ACT.do((lambda e, o=gB[:, bi, :], i_=bank_f32(bank)[:, 0:512],
                                     bcol=pv[:, PV_BGB + c:PV_BGB + c + 1]:
                                     e.activation(out=o, in_=i_, func=AF.Sigmoid, bias=bcol)),
                                    wait=[tok, prev["tb"][bi]])
                        bank_release(bank, ev)
                        gb_ev[bi] = ev
                    w_release(wt_gbr, tok)
                    for bi, (a, b) in enumerate(B2):
                        bank, ftok = bank_alloc()
                        tok = pe_group(bank, 512, 4, lambda kk: wt_pw.ap[:, kk, cc * 128:(cc + 1) * 128],
                                       lambda kk, a=a, b=b: pooledT[:, kk, a - 16:b - 16],
                                       [ftok, wt_pw.tok, pooled_toks])
                        prev["ya"] = tok
                        ev = DVE.do((lambda e, o=ta[:, bi, :], i0=bank_f32(bank)[:, 0:512],
                                     s=pv[:, PV_PSC + c:PV_PSC + c + 1], i1=gA[:, bi, :]:
                                     e.scalar_tensor_tensor(out=o, in0=i0, scalar=s, in1=i1,
                                                            op0=ALU.mult, op1=ALU.mult)),
                                    wait=[tok, ga_ev[bi], prev["m"][bi]])
                        bank_release(bank, ev)
                        prev["ta"][bi] = ev
                    if cc == 3:
                        w_release(wt_pw, tok)
                    for bi, (a, b) in enumerate(B2):
                        bank, ftok = bank_alloc()
                        tok = pe_group(bank, 512, 16, lambda k: wt_co.ap[:, k, :],
                                       lambda k, a=a, b=b: zT[:, k, a - 16:b - 16], [ftok, wt_co.tok, last_z])
                        ev = DVE.do((lambda e, o=tb[:, bi, :], i0=bank_f32(bank)[:, 0:512], i1=gB[:, bi, :]:
                                     e.tensor_tensor(out=o, in0=i0, in1=i1, op=ALU.mult)),
                                    wait=[tok, gb_ev[bi], prev["m"][bi]])
                        bank_release(bank, ev)
                        prev["tb"][bi] = ev
                        mv = DVE.do((lambda e, o=mT[:, c, a - 16:b - 16], i0=tb[:, bi, :], i1=ta[:, bi, :]:
                                     e.tensor_tensor(out=o, in0=i0, in1=i1, op=ALU.add)),
                                    wait=[ev, prev["ta"][bi], last_p0_pe, last_p0_act])
                        prev["m"][bi] = mv
                        last_m = mv
                    w_release(wt_co, tok)
                    last_p1_pe = tok
            if debug and phase_limit == 2:
                final_toks.append(dump("mT", R2[:, 0:8192].bitcast(BF16)))

        pending_tr = []
        hn1_done = [None] * 8
        last_p2_pe = None
        if phase_limit >= 3:
            last_p1_act = Tok(ACT.sem, ACT.cnt, "act")
            last_p1_dve = Tok(DVE.sem, DVE.cnt, "dve")
            xr = []
            for t in range(8):
                xr.append(SP.dma((lambda e, o=h1[:, t, :], s=xh[HALO + 128 * t:HALO + 128 * (t + 1), :]:
                                  e.dma_start(out=o, in_=s)),
                                 hsem[t], f"h{t}", 1, wait=[last_p1_pe, last_m]))
            tr2 = {}
            a2s = {}
            for cb in range(4):
                wts = [w_use(("wo", cb, kq)) for kq in range(4)]
                tok = None
                for t in range(8):
                    bank, ftok = bank_alloc()
                    tok = pe_group(bank, 512, 16, lambda k, t=t: mT[:, k, 128 * t:128 * (t + 1)],
                                   lambda k: wts[k // 4].ap[:, k % 4, :],
                                   [ftok, last_m] + [w.tok for w in wts])
                    ev = DVE.do((lambda e, o=h1[:, t, cb * 512:(cb + 1) * 512], i0=bank_f32(bank)[:, 0:512]:
                                 e.tensor_tensor(out=o, in0=i0, in1=o, op=ALU.add)),
                                wait=[tok, xr[t]])
                    bank_release(bank, ev)
                    last_p2_pe = tok
                    if cb == 3:
                        if len(pending_tr) >= 2:
                            tt, pp = pending_tr.pop(0)
                            tr2[tt], hn1_done[tt] = pp()
                        a2, pe_part = norm_transpose(t, 128, h1[:, t, :], ev, ss2, rs2, t, hn1b, t % 2, junk2,
                                                     PV_G2, hn1T, 128 * t, tr2.get(t - 2),
                                                     extra_wait=[last_p1_act, last_p1_dve, last_p1_pe])
                        a2s[t] = a2
                        pending_tr.append((t, pe_part))
                for w in wts:
                    w_release(w, tok)
            if phase_limit == 3:
                while pending_tr:
                    tt, pp = pending_tr.pop(0)
                    tr2[tt], hn1_done[tt] = pp()
                if debug:
                    final_toks.append(dump("h1", R1[:, 0:16384]))
                    final_toks.append(dump("hn1T", R3[:, 0:8192].bitcast(BF16)))

        out_toks = []
        if phase_limit >= 4:
            st = {"sg": [None, None], "sgi": 0, "dn_last": {}, "first": True}

            def flush_tr():
                while pending_tr:
                    tt, pp = pending_tr.pop(0)
                    _, hn1_done[tt] = pp()

            def gu_unit(g, jj, blk, wt_g, wt_u):
                need = [hn1_done[t] for t in range(4 * blk, 4 * blk + 4) if hn1_done[t] is not None]
                bg, ftok = bank_alloc()
                tg = pe_group(bg, 512, 16, lambda k: wt_g.ap[:, k, :],
                              lambda k, blk=blk: hn1T[:, k, 512 * blk:512 * (blk + 1)],
                              [ftok, wt_g.tok, last_p2_pe] + need)
                si = st["sgi"] % 2
                st["sgi"] += 1
                ea = ACT.do((lambda e, o=sg[:, si, :], i_=bank_f32(bg)[:, 0:512]:
                             e.activation(out=o, in_=i_, func=AF.Silu)),
                            wait=[tg, st["sg"][si], last_p2_pe])
                bank_release(bg, ea)
                bu, ftok = bank_alloc()
                tu = pe_group(bu, 512, 16, lambda k: wt_u.ap[:, k, :],
                              lambda k, blk=blk: hn1T[:, k, 512 * blk:512 * (blk + 1)],
                              [ftok, wt_u.tok])
                ed = DVE.do((lambda e, o=actT[:, g % 2, jj, 512 * blk:512 * (blk + 1)],
                             i0=bank_f32(bu)[:, 0:512], i1=sg[:, si, :]:
                             e.tensor_tensor(out=o, in0=i0, in1=i1, op=ALU.mult)),
                            wait=[tu, ea, st["dn_last"].get(g - 2), last_p2_pe])
                bank_release(bu, ed)
                st["sg"][si] = ed
                st["act_done_%d" % g] = ed
                return tg, tu

            def gate_up(g):
                if g == 0:
                    wts = []
                    for jj in range(4):
                        j = 4 * g + jj
                        wts.append((w_use(("gu", 128 * j)), w_use(("gu", FF + 128 * j))))
                    for blk in range(2):
                        for jj in range(4):
                            if blk == 0 and jj == 2 and pending_tr:
                                flush_tr()
                            tg, tu = gu_unit(g, jj, blk, wts[jj][0], wts[jj][1])
                            if blk == 1:
                                w_release(wts[jj][0], tg)
                                w_release(wts[jj][1], tu)
                    return
                for jj in range(4):
                    j = 4 * g + jj
                    wt_g = w_use(("gu", 128 * j))
                    wt_u = w_use(("gu", FF + 128 * j))
                    tg = tu = None
                    for blk in range(2):
                        tg, tu = gu_unit(g, jj, blk, wt_g, wt_u)
                    w_release(wt_g, tg)
                    w_release(wt_u, tu)

            fin = {}
            fin_pool = {}
            fin_scale = {}

            tmpg2 = R2[:, 5120:7168].rearrange("p (b n) -> p b n", b=2)

            def final_out(t, d1):
                ev_t, p3 = fin_pool[t]
                d2 = DVE.do((lambda e, o=rs3[:, t:t + 1]: e.reciprocal(out=o, in_=o)), wait=[d1])
                d3 = DVE.do((lambda e, o=h1[:, t, 0:1024], s=rs3[:, t:t + 1]:
                             e.scalar_tensor_tensor(out=o, in0=o, scalar=s, in1=g3[:, 0:1024],
                                                    op0=ALU.mult, op1=ALU.mult)),
                            wait=[d2, g3_tok])
                a3 = ACT.do((lambda e, o=h1[:, t, 1024:2048], s=rs3[:, t:t + 1], tg=tmpg2[:, t % 2, :]:
                             e.activation(out=o, in_=tg, func=AF.Copy, scale=s)),
                            wait=[d2, p3])
                fin_scale[t] = a3
                out_toks.append(ACT.dma((lambda e, o=out[128 * t:128 * (t + 1), 1024:2048], s=h1[:, t, 1024:2048]:
                                         e.dma_start(out=o, in_=s)),
                                        osem2[t], f"ob{t}", 1, wait=[a3]))
                out_toks.append(ACT.dma((lambda e, o=out[128 * t:128 * (t + 1), 0:1024], s=h1[:, t, 0:1024]:
                                         e.dma_start(out=o, in_=s)),
                                        osem[t], f"o{t}", 1, wait=[d3]))

            def down(g, merged_prev=False):
                gs = [g - 1, g] if merged_prev else [g]
                wts_all = [[w_use(("wd", gg, cb)) for cb in range(4)] for gg in gs]
                wts = [w for ws_ in wts_all for w in ws_]
                nk = 4 * len(gs)
                tok = None
                for t in range(8):
                    ev = None
                    for cb in range(4):
                        bank, ftok = bank_alloc()
                        tok = pe_group(bank, 512, nk,
                                       lambda kk, t=t: actT[:, gs[kk // 4] % 2, kk % 4, 128 * t:128 * (t + 1)],
                                       lambda kk, cb=cb: wts_all[kk // 4][cb].ap[:, kk % 4, :],
                                       [ftok] + [st["act_done_%d" % gg] for gg in gs] + [w.tok for w in wts])
                        ev = DVE.do((lambda e, o=h1[:, t, cb * 512:(cb + 1) * 512], i0=bank_f32(bank)[:, 0:512]:
                                     e.tensor_tensor(out=o, in0=i0, in1=o, op=ALU.add)),
                                    wait=[tok])
                        bank_release(bank, ev)
                    if g == NPASS - 1:
                        a1 = ACT.do((lambda e, i_=h1[:, t, :], acc=ss3[:, t:t + 1]:
                                     e.activation(out=junk2[:, :], in_=i_, func=AF.Square,
                                                  accum_out=acc)),
                                    wait=[ev])
                        d1 = ACT.do((lambda e, o=rs3[:, t:t + 1], i0=ss3[:, t:t + 1]:
                                     e.activation(out=o, in_=i0, func=AF.Sqrt, scale=1.0 / D, bias=eps_ap[:, :])), wait=[a1])
                        if t >= 1:
                            final_out(t - 1, fin[t - 1])
                        fin[t] = d1
                        p3 = POOL.do((lambda e, i0=h1[:, t, 1024:2048], tg=tmpg2[:, t % 2, :]:
                                      e.tensor_tensor(out=tg, in0=i0, in1=g3[:, 1024:2048], op=ALU.mult)),
                                     wait=[ev, g3_tok, fin_scale.get(t - 2)])
                        fin_pool[t] = (ev, p3)
                if g == NPASS - 1:
                    final_out(7, fin[7])
                st["dn_last"][g] = tok
                for w in wts:
                    w_release(w, tok)

            gate_up(0)
            for g in range(NPASS):
                if g + 1 < NPASS:
                    gate_up(g + 1)
                if g == NPASS - 2:
                    continue
                down(g, merged_prev=(g == NPASS - 1))

        ACT.wait(*out_toks)
        SP.wait(*final_toks)
        SP.wait(Tok(PE.sem, PE.cnt, "pe"), Tok(ACT.sem, ACT.cnt, "act"), Tok(DVE.sem, DVE.cnt, "dve"), g3_tok)
        if POOL.cnt:
            SP.wait(Tok(POOL.sem, POOL.cnt, "pool"))
        for s in range(NS):
            if wst["uses"][s]:
                SP.wait(Tok(wsem[s], 16 * wst["uses"][s], f"w{s}"))
        for s in range(2):
            if wst["pwuses"][s]:
                SP.wait(Tok(pwsem[s], 16 * wst["pwuses"][s], f"pw{s}"))

        with nc.Block() as block:
            @block.sync
            def _(e):
                SP.emit(e)

            @block.gpsimd
            def _(e):
                POOL.emit(e)

            @block.tensor
            def _(e):
                PE.emit(e)

            @block.scalar
            def _(e):
                ACT.emit(e)

            @block.vector
            def _(e):
                DVE.emit(e)
    return nc


def make_in_maps(x, meta_tokens, norm_mix_g, w_in, b_gate, pool_w, pool_scale, conv_w, conv_out_w, w_o,
                 norm_ffn_g, w_gate_up, w_down, norm_final_g):
    f = lambda a: np.ascontiguousarray(np.asarray(a, dtype=np.float32))
    x = f(x)
    meta = f(meta_tokens)

    def fm(v):
        return f(v).reshape(16, 128).T

    pvec = np.zeros((128, 128), np.float32)
    pvec[:, PV_G1:PV_G1 + 16] = fm(norm_mix_g)
    bg = f(b_gate)
    pvec[:, PV_BGA:PV_BGA + 16] = fm(bg[:D])
    pvec[:, PV_BGB:PV_BGB + 16] = fm(bg[D:])
    pvec[:, PV_PSC:PV_PSC + 16] = fm(pool_scale)
    cw = f(conv_w)
    for tap in range(3):
        pvec[:, PV_CW + tap:PV_CW + 48:3] = fm(cw[tap])
    pvec[:, PV_G2:PV_G2 + 16] = fm(norm_ffn_g)
    g3b = np.ascontiguousarray(np.broadcast_to(f(norm_final_g)[None, :], (128, D)))
    ident = np.eye(128, dtype=np.float32).astype(ml_dtypes.bfloat16)
    shared = {
        "w_in": f(w_in), "pw2": np.ascontiguousarray(f(pool_w).transpose(1, 0, 2).reshape(512, 2048)), "conv_out_w": f(conv_out_w),
        "w_o": f(w_o), "w_gate_up": f(w_gate_up), "w_down": f(w_down),
        "pvec": pvec, "g3b": g3b, "ident": ident,
    }
    maps = []
    for i in range(N_CORES):
        b, half = i // 2, i % 2
        halo = meta if half == 0 else x[b, TR - HALO:TR]
        xh = np.concatenate([halo, x[b, half * TR:(half + 1) * TR]], axis=0)
        m = dict(shared)
        m["xh"] = np.ascontiguousarray(xh)
        maps.append(m)
    return maps


def kernel(x, meta_tokens, norm_mix_g, w_in, b_gate, pool_w, pool_scale, conv_w, conv_out_w, w_o,
           norm_ffn_g, w_gate_up, w_down, norm_final_g):
    maps = make_in_maps(x, meta_tokens, norm_mix_g, w_in, b_gate, pool_w, pool_scale, conv_w, conv_out_w,
                        w_o, norm_ffn_g, w_gate_up, w_down, norm_final_g)
    nc = build_program()
    res = run_bass_kernel_spmd(nc, maps, core_ids=list(range(N_CORES)))
    outp = np.empty((4, 2 * TR, D), np.float32)
    for i in range(N_CORES):
        b, half = i // 2, i % 2
        outp[b, half * TR:(half + 1) * TR] = res.results[i]["out"]
    return outp
```

```python
from contextlib import ExitStack

import numpy as np
import ml_dtypes
import concourse.bass as bass
import concourse.mybir as mybir
from concourse.bass_utils import run_bass_kernel_spmd

F32 = mybir.dt.float32
BF16 = mybir.dt.bfloat16
AF = mybir.ActivationFunctionType
ALU = mybir.AluOpType

D = 2048
T = 1040
TR = 1024
HALO = 16
FF = 5632
NPASS = 11
B3 = [(0, 336), (336, 688), (688, 1040)]
B2 = [(16, 528), (528, 1040)]
POOL_W = (2, 4, 8, 16)
EPS = 1e-6
NS = 10
N_CORES = 8

PV_G1, PV_BGA, PV_BGB, PV_PSC, PV_CW, PV_G2 = 0, 16, 32, 48, 64, 112


class Tok:
    __slots__ = ("sem", "val", "key")

    def __init__(self, sem, val, key):
        self.sem, self.val, self.key = sem, val, key


class Eng:
    def __init__(self, name):
        self.name = name
        self.ops = []
        self.cnt = 0
        self.sem = None
        self.seen = {}

    def wait(self, *toks):
        for t in toks:
            if t is None:
                continue
            if isinstance(t, (list, tuple)):
                self.wait(*t)
                continue
            if self.seen.get(t.key, 0) >= t.val:
                continue
            self.seen[t.key] = t.val
            self.ops.append(("wait", t.sem, t.val))

    def do(self, fn, wait=(), sig=True):
        self.wait(*wait)
        if sig:
            self.cnt += 1
            self.ops.append(("op", fn, self.sem, 1))
            return Tok(self.sem, self.cnt, self.name)
        self.ops.append(("op", fn, None, 0))
        return None

    def dma(self, fn, sem, semkey, count, wait=()):
        self.wait(*wait)
        self.ops.append(("op", fn, sem, 16))
        return Tok(sem, 16 * count, semkey)

    def emit(self, e):
        for op in self.ops:
            if op[0] == "wait":
                e.wait_ge(op[1], op[2])
            else:
                ins = op[1](e)
                if op[2] is not None:
                    ins.then_inc(op[2], op[3])


def weight_plan():
    plan = []
    for c in range(16):
        plan += [("in", 4096 + 128 * c), ("in", 6144 + 128 * c), ("in", 2048 + 128 * c)]
    for gi in range(4):
        plan.append(("pw", gi))
        for cc in range(4):
            plan.append(("in", 128 * (4 * gi + cc)))
        for cc in range(4):
            c = 4 * gi + cc
            plan += [("in", 8192 + 128 * c), ("in", 10240 + 128 * c), ("co", 128 * c)]
    for cb in range(4):
        for kq in range(4):
            plan.append(("wo", cb, kq))

    def gu(g):
        out = []
        for jj in range(4):
            j = 4 * g + jj
            out += [("gu", 128 * j), ("gu", FF + 128 * j)]
        return out

    plan += gu(0)
    for g in range(NPASS):
        if g + 1 < NPASS:
            plan += gu(g + 1)
        for cb in range(4):
            plan.append(("wd", g, cb))
    return plan


def build_program(phase_limit=99, debug=False):
    nc = bass.Bass("TRN2", target_bir_lowering=False)
    dram = {}

    def din(name, shape, dt=F32):
        dram[name] = nc.dram_tensor(name, list(shape), dt, kind="ExternalInput").ap()
        return dram[name]

    xh = din("xh", [T, D])
    w_in = din("w_in", [D, 12288])
    pool_w = din("pw2", [512, 2048])
    conv_out_w = din("conv_out_w", [D, D])
    w_o = din("w_o", [D, D])
    w_gu = din("w_gate_up", [D, 2 * FF])
    w_dn = din("w_down", [FF, D])
    pvec = din("pvec", [128, 128])
    g3b = din("g3b", [128, D])
    ident_d = din("ident", [128, 128], BF16)
    out = nc.dram_tensor("out", [TR, D], F32, kind="ExternalOutput").ap()
    dbg = {}
    if debug:
        dbg["hnT"] = nc.dram_tensor("dbg_hnT", [128, 16 * T], BF16, kind="ExternalOutput").ap()
        dbg["zT"] = nc.dram_tensor("dbg_zT", [128, 16 * TR], BF16, kind="ExternalOutput").ap()
        dbg["mT"] = nc.dram_tensor("dbg_mT", [128, 16 * TR], BF16, kind="ExternalOutput").ap()
        dbg["h1"] = nc.dram_tensor("dbg_h1", [128, 8 * D], F32, kind="ExternalOutput").ap()
        dbg["hn1T"] = nc.dram_tensor("dbg_hn1T", [128, 16 * TR], BF16, kind="ExternalOutput").ap()

    w_in_v = w_in.rearrange("(ko p) c -> p ko c", p=128)
    co_v = conv_out_w.rearrange("(ko p) c -> p ko c", p=128)
    wo_v = w_o.rearrange("(ko p) c -> p ko c", p=128)
    gu_v = w_gu.rearrange("(ko p) c -> p ko c", p=128)
    wd_v = w_dn.rearrange("(ko p) c -> p ko c", p=128)
    pw_v = pool_w.rearrange("(kk p) c -> p kk c", p=128)

    with ExitStack() as es:
        def sb(name, n, dt=F32):
            return es.enter_context(nc.sbuf_tensor(name, [128, n], dt))

        R1 = sb("R1", 16640)
        R2 = sb("R2", 8192)
        R3 = sb("R3", 12368)
        WR = sb("WR", NS * 1024)
        PWB = sb("PWB", 2048)
        pv = sb("pv", 128)
        g3 = sb("g3", D)
        ident = sb("ident_sb", 128, BF16)
        stats = sb("stats", 64)
        ps = es.enter_context(nc.psum_tensor("ps", [128, 8, 512], F32))

        def sem(name):
            return es.enter_context(nc.semaphore(name))

        PE, ACT, DVE, POOL, SP = Eng("pe"), Eng("act"), Eng("dve"), Eng("pool"), Eng("sp")
        for e in (PE, ACT, DVE, POOL, SP):
            e.sem = sem("s_" + e.name)
        wsem = [sem(f"w{i}") for i in range(NS)]
        pwsem = [sem(f"pw{i}") for i in range(2)]
        xsem = [sem(f"x{i}") for i in range(9)]
        hsem = [sem(f"h{i}") for i in range(8)]
        osem = [sem(f"o{i}") for i in range(8)]
        osem2 = [sem(f"ob{i}") for i in range(8)]
        csem = sem("const")
        g3sem = sem("g3")
        dsem = sem("dbg")

        hnT = R1[:, 0:8320].bitcast(BF16).rearrange("p (k t) -> p k t", k=16)
        zT = R1[:, 8320:16512].bitcast(BF16).rearrange("p (k t) -> p k t", k=16)
        h1 = R1[:, 0:16384].rearrange("p (t d) -> p t d", t=8)
        mT = R2[:, 0:8192].bitcast(BF16).rearrange("p (k t) -> p k t", k=16)
        junk = R2[:, 4096:5120].bitcast(BF16)
        hnb = R2[:, 5120:7168].bitcast(BF16).rearrange("p (b d) -> p b d", b=2)
        actT = R2[:, 0:4096].bitcast(BF16).rearrange("p (b j t) -> p b j t", b=2, j=4)
        sg = R2[:, 4096:5120].rearrange("p (b n) -> p b n", b=2)
        gcs = R3[:, 0:1040]
        cv = R3[:, 1040:2080]
        ub = R3[:, 2080:3120]
        S1 = R3[:, 3120:4160]
        S2 = R3[:, 4160:5200]
        pooledT = R3[:, 5200:7248].bitcast(BF16).rearrange("p (k t) -> p k t", k=4)
        ybuf = R3[:, 7248:8272].rearrange("p (b n) -> p b n", b=2)
        gA = R3[:, 8272:9296].rearrange("p (b n) -> p b n", b=2)
        gB = R3[:, 9296:10320].rearrange("p (b n) -> p b n", b=2)
        ta = R3[:, 10320:11344].rearrange("p (b n) -> p b n", b=2)
        tb = R3[:, 11344:12368].rearrange("p (b n) -> p b n", b=2)
        hn1T = R3[:, 0:8192].bitcast(BF16).rearrange("p (k t) -> p k t", k=16)
        hn1b = R3[:, 8192:10240].bitcast(BF16).rearrange("p (b d) -> p b d", b=2)
        junk2 = R3[:, 10240:11264].bitcast(BF16)
        wring = WR[:, :].bitcast(BF16).rearrange("p (s n) -> p s n", s=NS)
        pwbuf = PWB[:, :].bitcast(BF16).rearrange("p (s n) -> p s n", s=2)
        ss1, rs1 = stats[:, 0:9], stats[:, 9:18]
        ss2, rs2 = stats[:, 18:26], stats[:, 26:34]
        ss3, rs3 = stats[:, 34:42], stats[:, 42:50]
        eps_ap = stats[:, 50:51]

        ring = {"u": 0, "free": [None] * 8}

        def bank_alloc():
            b = ring["u"] % 8
            ring["u"] += 1
            return b, ring["free"][b]

        def bank_release(b, *toks):
            ring["free"][b] = list(toks)

        def bank_f32(b):
            return ps[:, b, :]

        def bank_bf(b):
            return ps[:, b, :].bitcast(BF16).rearrange("p (k t) -> p k t", k=8)

        plan = weight_plan()
        ring_idx = {}
        ring_at = []
        for i_, key_ in enumerate(plan):
            if key_[0] != "pw":
                ring_idx[i_] = len(ring_at)
                ring_at.append(i_)
        wst = {"next_dma": 0, "cur": -1, "rel": {}, "tok": {}, "uses": [0] * NS, "pwuses": [0, 0]}

        def w_src(key):
            kind = key[0]
            if kind == "in":
                return w_in_v[:, :, key[1]:key[1] + 128], 16
            if kind == "co":
                return co_v[:, :, key[1]:key[1] + 128], 16
            if kind == "gu":
                return gu_v[:, :, key[1]:key[1] + 128], 16
            if kind == "pw":
                return pw_v[:, :, key[1] * 512:(key[1] + 1) * 512], 4
            if kind == "wo":
                return wo_v[:, 4 * key[2]:4 * key[2] + 4, key[1] * 512:(key[1] + 1) * 512], 4
            if kind == "wd":
                return wd_v[:, 4 * key[1]:4 * key[1] + 4, key[2] * 512:(key[2] + 1) * 512], 4
            raise KeyError(key)

        def w_view(i):
            key = plan[i]
            _, kdim = w_src(key)
            if key[0] == "pw":
                return pwbuf[:, key[1] % 2, :].rearrange("p (k c) -> p k c", k=kdim)
            return wring[:, ring_idx[i] % NS, :].rearrange("p (k c) -> p k c", k=kdim)

        def w_pump():
            while wst["next_dma"] < len(plan) and wst["next_dma"] <= wst["cur"] + NS - 1:
                j = wst["next_dma"]
                key = plan[j]
                waits = []
                if key[0] == "pw":
                    if key[1] >= 2:
                        pj = plan.index(("pw", key[1] - 2))
                        if pj not in wst["rel"]:
                            break
                        waits = [wst["rel"][pj]]
                    s = key[1] % 2
                    wst["pwuses"][s] += 1
                    semh, semk, cnt = pwsem[s], f"pw{s}", wst["pwuses"][s]
                else:
                    r = ring_idx[j]
                    if r >= NS:
                        pj = ring_at[r - NS]
                        if pj not in wst["rel"]:
                            break
                        waits = [wst["rel"][pj]]
                    s = r % NS
                    wst["uses"][s] += 1
                    semh, semk, cnt = wsem[s], f"w{s}", wst["uses"][s]
                src, _ = w_src(key)
                dst = w_view(j)
                wst["tok"][j] = POOL.dma(
                    (lambda e, dst=dst, src=src: e.dma_start(out=dst, in_=src)),
                    semh, semk, cnt, wait=waits)
                wst["next_dma"] += 1

        class WT:
            pass

        def w_use(key):
            wst["cur"] += 1
            i = wst["cur"]
            assert plan[i] == key, (i, plan[i], key)
            w_pump()
            assert i in wst["tok"], f"weight tile {i} {key} not prefetched (NS too small)"
            wt = WT()
            wt.i, wt.ap, wt.tok = i, w_view(i), wst["tok"][i]
            return wt

        def w_release(wt, tok):
            wst["rel"][wt.i] = tok
            w_pump()

        c1 = SP.dma(lambda e: e.dma_start(out=pv[:, :], in_=pvec), csem, "const", 1)
        c3 = SP.dma(lambda e: e.dma_start(out=ident[:, :], in_=ident_d), csem, "const", 2)
        c4 = DVE.do(lambda e: e.memset(eps_ap, EPS))
        const_tok = c3
        ACT.wait(c4)
        ACT.do(lambda e: e.activation(out=stats[:, 52:53], in_=eps_ap, func=AF.Sqrt), sig=False)
        for e in (PE, ACT, DVE):
            e.wait(const_tok)

        def pe_group(bank, n, K, lhs, rhs, waits, view=None):
            PE.wait(*waits)
            o = view if view is not None else bank_f32(bank)[:, 0:n]
            tok = None
            for k in range(K):
                tok = PE.do(
                    (lambda e, o=o, l=lhs(k), r=rhs(k), st=(k == 0), sp=(k == K - 1):
                     e.matmul(o, lhsT=l, rhs=r, start=st, stop=sp)),
                    sig=(k == K - 1))
            return tok

        def norm_transpose(i, R, src_ap, src_tok, ss, rs, col, bufs, bufi, jk, gcol, dstT, t0, buf_free, extra_wait=()):
            a1 = ACT.do(lambda e: e.activation(out=jk[:R, :], in_=src_ap, func=AF.Square,
                                               accum_out=ss[:R, col:col + 1]),
                        wait=[src_tok, *extra_wait])
            d1 = ACT.do(lambda e: e.activation(out=rs[:R, col:col + 1], in_=ss[:R, col:col + 1], func=AF.Sqrt, scale=1.0 / D,
                                               bias=eps_ap[:R, :]),
                        wait=[a1])
            d2 = DVE.do(lambda e: e.reciprocal(out=rs[:R, col:col + 1], in_=rs[:R, col:col + 1]),
                        wait=[d1])
            hb = bufs[:R, bufi, :]
            a2 = ACT.do(lambda e: e.activation(out=hb, in_=src_ap, func=AF.Copy, scale=rs[:R, col:col + 1]),
                        wait=[d2, buf_free])

            def pe_part():
                last_tr = None
                evs = []
                for half in range(2):
                    b, ftok = bank_alloc()
                    PE.wait(ftok, a2)
                    bv = bank_bf(b)
                    for kk in range(8):
                        k = 8 * half + kk
                        last_tr = PE.do(
                            (lambda e, o=bv[:, kk, 0:R], i_=bufs[:R, bufi, k * 128:(k + 1) * 128]:
                             e.transpose(o, i_, ident[:R, :R])),
                            sig=(kk == 7))
                    gsl = pv[:, gcol + 8 * half:gcol + 8 * half + 8].unsqueeze(2).to_broadcast([128, 8, R])
                    ev = DVE.do(
                        (lambda e, o=dstT[:, 8 * half:8 * half + 8, t0:t0 + R], i0=bv[:, :, 0:R], i1=gsl:
                         e.tensor_tensor(out=o, in0=i0, in1=i1, op=ALU.mult)),
                        wait=[last_tr])
                    bank_release(b, ev)
                    evs.append(ev)
                return last_tr, evs[-1]

            return a2, pe_part

        tr_toks = {}
        a2_toks = {}
        hnT_done = None

        def xt_ap(i, R):
            if i < 6:
                return R3[:R, i * 2048:(i + 1) * 2048]
            return R1[:R, 8320 + (i - 6) * 2048:8320 + (i - 5) * 2048]

        xtok = []
        for i in range(9):
            R = 128 if i < 8 else 16
            xtok.append(SP.dma((lambda e, o=xt_ap(i, R), s=xh[128 * i:128 * i + R, :]: e.dma_start(out=o, in_=s)),
                               xsem[i], f"x{i}", 1))
        POOL.wait(xtok[6])
        g3_tok = SP.dma(lambda e: e.dma_start(out=g3[:, :], in_=g3b), g3sem, "g3", 1)

        def p0_stats(i):
            R = 128 if i < 8 else 16
            src = xt_ap(i, R)
            a1 = ACT.do(lambda e: e.activation(out=junk[:R, :], in_=src, func=AF.Square,
                                               accum_out=ss1[:R, i:i + 1]), wait=[xtok[i]])
            d1 = ACT.do(lambda e: e.activation(out=rs1[:R, i:i + 1], in_=ss1[:R, i:i + 1], func=AF.Sqrt, scale=1.0 / D,
                                               bias=eps_ap[:R, :]), wait=[a1])
            return DVE.do(lambda e: e.reciprocal(out=rs1[:R, i:i + 1], in_=rs1[:R, i:i + 1]), wait=[d1])

        def p0_rest(i, rtok):
            R = 128 if i < 8 else 16
            src = xt_ap(i, R)
            buf = i % 2
            hb = hnb[:R, buf, :]
            if i % 2 == 1:
                a2 = ACT.do(lambda e: e.activation(out=hb, in_=src, func=AF.Copy, scale=rs1[:R, i:i + 1]),
                            wait=[rtok, tr_toks.get(i - 2)])
            else:
                a2 = DVE.do(lambda e: e.tensor_scalar(out=hb, in0=src, scalar1=rs1[:R, i:i + 1], scalar2=None,
                                                      op0=ALU.mult), wait=[rtok, tr_toks.get(i - 2)])
            a2_toks[i] = a2
            last_tr = None
            ev = None
            for half in range(2):
                b_, ftok = bank_alloc()
                PE.wait(ftok, a2)
                bv = bank_bf(b_)
                for kk in range(8):
                    k = 8 * half + kk
                    last_tr = PE.do(
                        (lambda e, o=bv[:, kk, 0:R], i_=hnb[:R, buf, k * 128:(k + 1) * 128]:
                         e.transpose(o, i_, ident[:R, :R])),
                        sig=(kk == 7))
                gsl = pv[:, PV_G1 + 8 * half:PV_G1 + 8 * half + 8].unsqueeze(2).to_broadcast([128, 8, R])
                ev = DVE.do(
                    (lambda e, o=hnT[:, 8 * half:8 * half + 8, 128 * i:128 * i + R], i0=bv[:, :, 0:R], i1=gsl:
                     e.tensor_tensor(out=o, in0=i0, in1=i1, op=ALU.mult)),
                    wait=[last_tr])
                bank_release(b_, ev)
            tr_toks[i] = last_tr
            return ev

        rt = {}
        hn_tile = {}
        for i in range(9):
            rt[i] = p0_stats(i)
            if i >= 1:
                hn_tile[i - 1] = p0_rest(i - 1, rt[i - 1])
        hn_tile[8] = p0_rest(8, rt[8])
        hnT_done = hn_tile[8]
        last_p0_pe = tr_toks[8]
        last_p0_act = Tok(ACT.sem, ACT.cnt, "act")
        last_p0_dve = Tok(DVE.sem, DVE.cnt, "dve")

        def dump(name, src_ap):
            t = SP.dma((lambda e, o=dbg[name], s=src_ap: e.dma_start(out=o, in_=s)), dsem, "dbg",
                       dump.n + 1, wait=[Tok(DVE.sem, DVE.cnt, "dve"), Tok(PE.sem, PE.cnt, "pe"),
                                         Tok(ACT.sem, ACT.cnt, "act")])
            dump.n += 1
            return t
        dump.n = 0
        final_toks = []

        if debug and phase_limit == 0:
            final_toks.append(dump("hnT", R1[:, 0:8320].bitcast(BF16)))

        last_z = None
        if phase_limit >= 1:
            prev = {"cvread": [None] * 3, "tap": None, "z": [None, None]}
            ACT.wait(last_p0_dve)
            DVE.wait(last_p0_act)
            for c in range(16):
                wt_gc = w_use(("in", 4096 + 128 * c))
                wt_v = w_use(("in", 6144 + 128 * c))
                wt_gb = w_use(("in", 2048 + 128 * c))
                gc_ev = []
                tok = None
                for bi, (a, b) in enumerate(B3):
                    bank, ftok = bank_alloc()
                    tok = pe_group(bank, b - a, 16, lambda k: wt_gc.ap[:, k, :], lambda k, a=a, b=b: hnT[:, k, a:b],
                                   [ftok, wt_gc.tok, hnT_done])
                    ev = ACT.do((lambda e, o=gcs[:, a:b], i_=bank_f32(bank)[:, 0:b - a]:
                                 e.activation(out=o, in_=i_, func=AF.Copy)),
                                wait=[tok, prev["cvread"][bi]])
                    bank_release(bank, ev)
                    gc_ev.append(ev)
                w_release(wt_gc, tok)
                cv_ev = []
                for bi, (a, b) in enumerate(B3):
                    bank, ftok = bank_alloc()
                    tok = pe_group(bank, b - a, 16, lambda k: wt_v.ap[:, k, :], lambda k, a=a, b=b: hnT[:, k, a:b],
                                   [ftok, wt_v.tok])
                    ev = DVE.do((lambda e, o=cv[:, a:b], i0=bank_f32(bank)[:, 0:b - a], i1=gcs[:, a:b]:
                                 e.tensor_tensor(out=o, in0=i0, in1=i1, op=ALU.mult)),
                                wait=[tok, gc_ev[bi], prev["tap"]])
                    bank_release(bank, ev)
                    cv_ev.append(ev)
                    prev["cvread"][bi] = ev
                w_release(wt_v, tok)
                tap = [None, None]
                for step in range(3):
                    for bi, (a, b) in enumerate(B2):
                        y = ybuf[:, bi, :]
                        sh = 2 - step
                        wcol = pv[:, PV_CW + 3 * c + step:PV_CW + 3 * c + step + 1]
                        if step == 0:
                            tap[bi] = DVE.do((lambda e, y=y, i0=cv[:, a - sh:b - sh], w=wcol:
                                              e.tensor_scalar(out=y, in0=i0, scalar1=w, scalar2=None, op0=ALU.mult)),
                                             wait=[cv_ev, prev["z"][bi]])
                        else:
                            tap[bi] = DVE.do((lambda e, y=y, i0=cv[:, a - sh:b - sh], w=wcol:
                                              e.scalar_tensor_tensor(out=y, in0=i0, scalar=w, in1=y,
                                                                     op0=ALU.mult, op1=ALU.add)),
                                             wait=[tap[bi]])
                prev["tap"] = tap[1]
                for bi, (a, b) in enumerate(B2):
                    bank, ftok = bank_alloc()
                    tok = pe_group(bank, 512, 16, lambda k: wt_gb.ap[:, k, :], lambda k, a=a, b=b: hnT[:, k, a:b],
                                   [ftok, wt_gb.tok])
                    ev = DVE.do((lambda e, o=zT[:, c, a - 16:b - 16], i0=bank_f32(bank)[:, 0:512], i1=ybuf[:, bi, :]:
                                 e.tensor_tensor(out=o, in0=i0, in1=i1, op=ALU.mult)),
                                wait=[tok, tap[bi]])
                    bank_release(bank, ev)
                    prev["z"][bi] = ev
                    last_z = ev
                w_release(wt_gb, tok)
            if debug and phase_limit == 1:
                final_toks.append(dump("zT", R1[:, 8320:16512].bitcast(BF16)))

        last_m = None
        last_p1_pe = None
        if phase_limit >= 2:
            prev = {"ubread": None, "s1": None, "s2": None, "ya": None, "ta": [None, None], "tb": [None, None],
                    "m": [None, None]}
            for gi in range(4):
                wwin = POOL_W[gi]
                wt_pw = w_use(("pw", gi))
                pooled_toks = []
                for cc in range(4):
                    c = 4 * gi + cc
                    wt_u = w_use(("in", 128 * c))
                    u_ev = []
                    tok = None
                    for bi, (a, b) in enumerate(B3):
                        bank, ftok = bank_alloc()
                        tok = pe_group(bank, b - a, 16, lambda k: wt_u.ap[:, k, :],
                                       lambda k, a=a, b=b: hnT[:, k, a:b], [ftok, wt_u.tok, hnT_done])
                        ev = ACT.do((lambda e, o=ub[:, a:b], i_=bank_f32(bank)[:, 0:b - a]:
                                     e.activation(out=o, in_=i_, func=AF.Copy)),
                                    wait=[tok, prev["ubread"]])
                        bank_release(bank, ev)
                        u_ev.append(ev)
                    w_release(wt_u, tok)
                    cur = ub
                    curtok = u_ev
                    bufs_ = [(S1, "s1"), (S2, "s2")]
                    lag = 1
                    step = 0
                    while lag < wwin:
                        dst, dkey = bufs_[step % 2]
                        off = 2 * lag - 1
                        tk = DVE.do((lambda e, o=dst[:, off:T], i0=cur[:, off:T], i1=cur[:, off - lag:T - lag]:
                                     e.tensor_tensor(out=o, in0=i0, in1=i1, op=ALU.add)),
                                    wait=[curtok, prev[dkey]])
                        cur, curtok = dst, tk
                        prev[dkey] = tk
                        lag *= 2
                        step += 1
                    pt = DVE.do((lambda e, o=pooledT[:, cc, :], i0=cur[:, HALO:T], i1=ub[:, HALO:T], s=1.0 / wwin:
                                 e.scalar_tensor_tensor(out=o, in0=i0, scalar=s, in1=i1,
                                                        op0=ALU.mult, op1=ALU.subtract)),
                                wait=[curtok, u_ev, prev["ya"]])
                    prev["ubread"] = pt
                    prev["s1"] = pt
                    prev["s2"] = pt
                    pooled_toks.append(pt)
                for cc in range(4):
                    c = 4 * gi + cc
                    wt_ga = w_use(("in", 8192 + 128 * c))
                    wt_gbr = w_use(("in", 10240 + 128 * c))
                    wt_co = w_use(("co", 128 * c))
                    ga_ev, gb_ev = [None, None], [None, None]
                    tok = None
                    for bi, (a, b) in enumerate(B2):
                        bank, ftok = bank_alloc()
                        tok = pe_group(bank, 512, 16, lambda k: wt_ga.ap[:, k, :],
                                       lambda k, a=a, b=b: hnT[:, k, a:b], [ftok, wt_ga.tok])
                        ev = ACT.do((lambda e, o=gA[:, bi, :], i_=bank_f32(bank)[:, 0:512],
                                     bcol=pv[:, PV_BGA + c:PV_BGA + c + 1]:
                                     e.activation(out=o, in_=i_, func=AF.Sigmoid, bias=bcol)),
                                    wait=[tok, prev["ta"][bi]])
                        bank_release(bank, ev)
                        ga_ev[bi] = ev
                    w_release(wt_ga, tok)
                    for bi, (a, b) in enumerate(B2):
                        bank, ftok = bank_alloc()
                        tok = pe_group(bank, 512, 16, lambda k: wt_gbr.ap[:, k, :],
                                       lambda k, a=a, b=b: hnT[:, k, a:b], [ftok, wt_gbr.tok])
                        ev = ACT.do((lambda e, o=gB[:, bi, :], i_=bank_f32(bank)[:, 0:512],
                                     bcol=pv[:, PV_BGB + c:PV_BGB + c + 1]:
                                     e.activation(out=o, in_=i_, func=AF.Sigmoid, bias=bcol)),
                                    wait=[tok, prev["tb"][bi]])
                        bank_release(bank, ev)
                        gb_ev[bi] = ev
                    w_release(wt_gbr, tok)
                    for bi, (a, b) in enumerate(B2):
                        bank, ftok = bank_alloc()
                        tok = pe_group(bank, 512, 4, lambda kk: wt_pw.ap[:, kk, cc * 128:(cc + 1) * 128],
                                       lambda kk, a=a, b=b: pooledT[:, kk, a - 16:b - 16],
                                       [ftok, wt_pw.tok, pooled_toks])
                        prev["ya"] = tok
                        ev = DVE.do((lambda e, o=ta[:, bi, :], i0=bank_f32(bank)[:, 0:512],
                                     s=pv[:, PV_PSC + c:PV_PSC + c + 1], i1=gA[:, bi, :]:
                                     e.scalar_tensor_tensor(out=o, in0=i0, scalar=s, in1=i1,
                                                            op0=ALU.mult, op1=ALU.mult)),
                                    wait=[tok, ga_ev[bi], prev["m"][bi]])
                        bank_release(bank, ev)
                        prev["ta"][bi] = ev
                    if cc == 3:
                        w_release(wt_pw, tok)
                    for bi, (a, b) in enumerate(B2):
                        bank, ftok = bank_alloc()
                        tok = pe_group(bank, 512, 16, lambda k: wt_co.ap[:, k, :],
                                       lambda k, a=a, b=b: zT[:, k, a - 16:b - 16], [ftok, wt_co.tok, last_z])
                        ev = DVE.do((lambda e, o=tb[:, bi, :], i0=bank_f32(bank)[:, 0:512], i1=gB[:, bi, :]:
                                     e.tensor_tensor(out=o, in0=i0, in1=i1, op=ALU.mult)),
                                    wait=[tok, gb_ev[bi], prev["m"][bi]])
                        bank_release(bank, ev)
                        prev["tb"][bi] = ev
                        mv = DVE.do((lambda e, o=mT[:, c, a - 16:b - 16], i0=tb[:, bi, :], i1=ta[:, bi, :]:
                                     e.tensor_tensor(out=o, in0=i0, in1=i1, op=ALU.add)),
                                    wait=[ev, prev["ta"][bi], last_p0_pe, last_p0_act])
                        prev["m"][bi] = mv
                        last_m = mv
                    w_release(wt_co, tok)
                    last_p1_pe = tok
            if debug and phase_limit == 2:
                final_toks.append(dump("mT", R2[:, 0:8192].bitcast(BF16)))

        pending_tr = []
        hn1_done = [None] * 8
        last_p2_pe = None
        if phase_limit >= 3:
            last_p1_act = Tok(ACT.sem, ACT.cnt, "act")
            last_p1_dve = Tok(DVE.sem, DVE.cnt, "dve")
            xr = []
            for t in range(8):
                xr.append(SP.dma((lambda e, o=h1[:, t, :], s=xh[HALO + 128 * t:HALO + 128 * (t + 1), :]:
                                  e.dma_start(out=o, in_=s)),
                                 hsem[t], f"h{t}", 1, wait=[last_p1_pe, last_m]))
            tr2 = {}
            a2s = {}
            for cb in range(4):
                wts = [w_use(("wo", cb, kq)) for kq in range(4)]
                tok = None
                for t in range(8):
                    bank, ftok = bank_alloc()
                    tok = pe_group(bank, 512, 16, lambda k, t=t: mT[:, k, 128 * t:128 * (t + 1)],
                                   lambda k: wts[k // 4].ap[:, k % 4, :],
                                   [ftok, last_m] + [w.tok for w in wts])
                    ev = DVE.do((lambda e, o=h1[:, t, cb * 512:(cb + 1) * 512], i0=bank_f32(bank)[:, 0:512]:
                                 e.tensor_tensor(out=o, in0=i0, in1=o, op=ALU.add)),
                                wait=[tok, xr[t]])
                    bank_release(bank, ev)
                    last_p2_pe = tok
                    if cb == 3:
                        if len(pending_tr) >= 2:
                            tt, pp = pending_tr.pop(0)
                            tr2[tt], hn1_done[tt] = pp()
                        a2, pe_part = norm_transpose(t, 128, h1[:, t, :], ev, ss2, rs2, t, hn1b, t % 2, junk2,
                                                     PV_G2, hn1T, 128 * t, tr2.get(t - 2),
                                                     extra_wait=[last_p1_act, last_p1_dve, last_p1_pe])
                        a2s[t] = a2
                        pending_tr.append((t, pe_part))
                for w in wts:
                    w_release(w, tok)
            if phase_limit == 3:
                while pending_tr:
                    tt, pp = pending_tr.pop(0)
                    tr2[tt], hn1_done[tt] = pp()
                if debug:
                    final_toks.append(dump("h1", R1[:, 0:16384]))
                    final_toks.append(dump("hn1T", R3[:, 0:8192].bitcast(BF16)))

        out_toks = []
        if phase_limit >= 4:
            st = {"sg": [None, None], "sgi": 0, "dn_last": {}, "first": True}

            def flush_tr():
                while pending_tr:
                    tt, pp = pending_tr.pop(0)
                    _, hn1_done[tt] = pp()

            def gu_unit(g, jj, blk, wt_g, wt_u):
                need = [hn1_done[t] for t in range(4 * blk, 4 * blk + 4) if hn1_done[t] is not None]
                bg, ftok = bank_alloc()
                tg = pe_group(bg, 512, 16, lambda k: wt_g.ap[:, k, :],
                              lambda k, blk=blk: hn1T[:, k, 512 * blk:512 * (blk + 1)],
                              [ftok, wt_g.tok, last_p2_pe] + need)
                si = st["sgi"] % 2
                st["sgi"] += 1
                ea = ACT.do((lambda e, o=sg[:, si, :], i_=bank_f32(bg)[:, 0:512]:
                             e.activation(out=o, in_=i_, func=AF.Silu)),
                            wait=[tg, st["sg"][si], last_p2_pe])
                bank_release(bg, ea)
                bu, ftok = bank_alloc()
                tu = pe_group(bu, 512, 16, lambda k: wt_u.ap[:, k, :],
                              lambda k, blk=blk: hn1T[:, k, 512 * blk:512 * (blk + 1)],
                              [ftok, wt_u.tok])
                ed = DVE.do((lambda e, o=actT[:, g % 2, jj, 512 * blk:512 * (blk + 1)],
                             i0=bank_f32(bu)[:, 0:512], i1=sg[:, si, :]:
                             e.tensor_tensor(out=o, in0=i0, in1=i1, op=ALU.mult)),
                            wait=[tu, ea, st["dn_last"].get(g - 2), last_p2_pe])
                bank_release(bu, ed)
                st["sg"][si] = ed
                st["act_done_%d" % g] = ed
                return tg, tu

            def gate_up(g):
                if g == 0:
                    wts = []
                    for jj in range(4):
                        j = 4 * g + jj
                        wts.append((w_use(("gu", 128 * j)), w_use(("gu", FF + 128 * j))))
                    for blk in range(2):
                        for jj in range(4):
                            if blk == 0 and jj == 2 and pending_tr:
                                flush_tr()
                            tg, tu = gu_unit(g, jj, blk, wts[jj][0], wts[jj][1])
                            if blk == 1:
                                w_release(wts[jj][0], tg)
                                w_release(wts[jj][1], tu)
                    return
                for jj in range(4):
                    j = 4 * g + jj
                    wt_g = w_use(("gu", 128 * j))
                    wt_u = w_use(("gu", FF + 128 * j))
                    tg = tu = None
                    for blk in range(2):
                        tg, tu = gu_unit(g, jj, blk, wt_g, wt_u)
                    w_release(wt_g, tg)
                    w_release(wt_u, tu)

            fin = {}
            fin_pool = {}
            fin_scale = {}

            tmpg2 = R2[:, 5120:7168].rearrange("p (b n) -> p b n", b=2)

            def final_out(t, d1):
                ev_t, p3 = fin_pool[t]
                d2 = DVE.do((lambda e, o=rs3[:, t:t + 1]: e.reciprocal(out=o, in_=o)), wait=[d1])
                d3 = DVE.do((lambda e, o=h1[:, t, 0:1024], s=rs3[:, t:t + 1]:
                             e.scalar_tensor_tensor(out=o, in0=o, scalar=s, in1=g3[:, 0:1024],
                                                    op0=ALU.mult, op1=ALU.mult)),
                            wait=[d2, g3_tok])
                a3 = ACT.do((lambda e, o=h1[:, t, 1024:2048], s=rs3[:, t:t + 1], tg=tmpg2[:, t % 2, :]:
                             e.activation(out=o, in_=tg, func=AF.Copy, scale=s)),
                            wait=[d2, p3])
                fin_scale[t] = a3
                out_toks.append(ACT.dma((lambda e, o=out[128 * t:128 * (t + 1), 1024:2048], s=h1[:, t, 1024:2048]:
                                         e.dma_start(out=o, in_=s)),
                                        osem2[t], f"ob{t}", 1, wait=[a3]))
                out_toks.append(ACT.dma((lambda e, o=out[128 * t:128 * (t + 1), 0:1024], s=h1[:, t, 0:1024]:
                                         e.dma_start(out=o, in_=s)),
                                        osem[t], f"o{t}", 1, wait=[d3]))

            def down(g, merged_prev=False):
                gs = [g - 1, g] if merged_prev else [g]
                wts_all = [[w_use(("wd", gg, cb)) for cb in range(4)] for gg in gs]
                wts = [w for ws_ in wts_all for w in ws_]
                nk = 4 * len(gs)
                tok = None
                for t in range(8):
                    ev = None
                    for cb in range(4):
                        bank, ftok = bank_alloc()
                        tok = pe_group(bank, 512, nk,
                                       lambda kk, t=t: actT[:, gs[kk // 4] % 2, kk % 4, 128 * t:128 * (t + 1)],
                                       lambda kk, cb=cb: wts_all[kk // 4][cb].ap[:, kk % 4, :],
                                       [ftok] + [st["act_done_%d" % gg] for gg in gs] + [w.tok for w in wts])
                        ev = DVE.do((lambda e, o=h1[:, t, cb * 512:(cb + 1) * 512], i0=bank_f32(bank)[:, 0:512]:
                                     e.tensor_tensor(out=o, in0=i0, in1=o, op=ALU.add)),
                                    wait=[tok])
                        bank_release(bank, ev)
                    if g == NPASS - 1:
                        a1 = ACT.do((lambda e, i_=h1[:, t, :], acc=ss3[:, t:t + 1]:
                                     e.activation(out=junk2[:, :], in_=i_, func=AF.Square,
                                                  accum_out=acc)),
                                    wait=[ev])
                        d1 = ACT.do((lambda e, o=rs3[:, t:t + 1], i0=ss3[:, t:t + 1]:
                                     e.activation(out=o, in_=i0, func=AF.Sqrt, scale=1.0 / D, bias=eps_ap[:, :])), wait=[a1])
                        if t >= 1:
                            final_out(t - 1, fin[t - 1])
                        fin[t] = d1
                        p3 = POOL.do((lambda e, i0=h1[:, t, 1024:2048], tg=tmpg2[:, t % 2, :]:
                                      e.tensor_tensor(out=tg, in0=i0, in1=g3[:, 1024:2048], op=ALU.mult)),
                                     wait=[ev, g3_tok, fin_scale.get(t - 2)])
                        fin_pool[t] = (ev, p3)
                if g == NPASS - 1:
                    final_out(7, fin[7])
                st["dn_last"][g] = tok
                for w in wts:
                    w_release(w, tok)

            gate_up(0)
            for g in range(NPASS):
                if g + 1 < NPASS:
                    gate_up(g + 1)
                if g == NPASS - 2:
                    continue
                down(g, merged_prev=(g == NPASS - 1))

        ACT.wait(*out_toks)
        SP.wait(*final_toks)
        SP.wait(Tok(PE.sem, PE.cnt, "pe"), Tok(ACT.sem, ACT.cnt, "act"), Tok(DVE.sem, DVE.cnt, "dve"), g3_tok)
        if POOL.cnt:
            SP.wait(Tok(POOL.sem, POOL.cnt, "pool"))
        for s in range(NS):
            if wst["uses"][s]:
                SP.wait(Tok(wsem[s], 16 * wst["uses"][s], f"w{s}"))
        for s in range(2):
            if wst["pwuses"][s]:
                SP.wait(Tok(pwsem[s], 16 * wst["pwuses"][s], f"pw{s}"))

        with nc.Block() as block:
            @block.sync
            def _(e):
                SP.emit(e)

            @block.gpsimd
            def _(e):
                POOL.emit(e)

            @block.tensor
            def _(e):
                PE.emit(e)

            @block.scalar
            def _(e):
                ACT.emit(e)

            @block.vector
            def _(e):
                DVE.emit(e)
    return nc


def make_in_maps(x, meta_tokens, norm_mix_g, w_in, b_gate, pool_w, pool_scale, conv_w, conv_out_w, w_o,
                 norm_ffn_g, w_gate_up, w_down, norm_final_g):
    f = lambda a: np.ascontiguousarray(np.asarray(a, dtype=np.float32))
    x = f(x)
    meta = f(meta_tokens)

    def fm(v):
        return f(v).reshape(16, 128).T

    pvec = np.zeros((128, 128), np.float32)
    pvec[:, PV_G1:PV_G1 + 16] = fm(norm_mix_g)
    bg = f(b_gate)
    pvec[:, PV_BGA:PV_BGA + 16] = fm(bg[:D])
    pvec[:, PV_BGB:PV_BGB + 16] = fm(bg[D:])
    pvec[:, PV_PSC:PV_PSC + 16] = fm(pool_scale)
    cw = f(conv_w)
    for tap in range(3):
        pvec[:, PV_CW + tap:PV_CW + 48:3] = fm(cw[tap])
    pvec[:, PV_G2:PV_G2 + 16] = fm(norm_ffn_g)
    g3b = np.ascontiguousarray(np.broadcast_to(f(norm_final_g)[None, :], (128, D)))
    ident = np.eye(128, dtype=np.float32).astype(ml_dtypes.bfloat16)
    shared = {
        "w_in": f(w_in), "pw2": np.ascontiguousarray(f(pool_w).transpose(1, 0, 2).reshape(512, 2048)), "conv_out_w": f(conv_out_w),
        "w_o": f(w_o), "w_gate_up": f(w_gate_up), "w_down": f(w_down),
        "pvec": pvec, "g3b": g3b, "ident": ident,
    }
    maps = []
    for i in range(N_CORES):
        b, half = i // 2, i % 2
        halo = meta if half == 0 else x[b, TR - HALO:TR]
        xh = np.concatenate([halo, x[b, half * TR:(half + 1) * TR]], axis=0)
        m = dict(shared)
        m["xh"] = np.ascontiguousarray(xh)
        maps.append(m)
    return maps


def kernel(x, meta_tokens, norm_mix_g, w_in, b_gate, pool_w, pool_scale, conv_w, conv_out_w, w_o,
           norm_ffn_g, w_gate_up, w_down, norm_final_g):
    maps = make_in_maps(x, meta_tokens, norm_mix_g, w_in, b_gate, pool_w, pool_scale, conv_w, conv_out_w,
                        w_o, norm_ffn_g, w_gate_up, w_down, norm_final_g)
    nc = build_program()
    res = run_bass_kernel_spmd(nc, maps, core_ids=list(range(N_CORES)))
    outp = np.empty((4, 2 * TR, D), np.float32)
    for i in range(N_CORES):
        b, half = i // 2, i % 2
        outp[b, half * TR:(half + 1) * TR] = res.results[i]["out"]
    return outp
```

```python
from contextlib import ExitStack

import numpy as np
import ml_dtypes
import concourse.bass as bass
import concourse.mybir as mybir
from concourse.bass_utils import run_bass_kernel_spmd

F32 = mybir.dt.float32
BF16 = mybir.dt.bfloat16
AF = mybir.ActivationFunctionType
ALU = mybir.AluOpType

D = 2048
T = 1040
TR = 1024
HALO = 16
FF = 5632
NPASS = 11
B3 = [(0, 336), (336, 688), (688, 1040)]
B2 = [(16, 528), (528, 1040)]
POOL_W = (2, 4, 8, 16)
EPS = 1e-6
NS = 10
N_CORES = 8

PV_G1, PV_BGA, PV_BGB, PV_PSC, PV_CW, PV_G2 = 0, 16, 32, 48, 64, 112


class Tok:
    __slots__ = ("sem", "val", "key")

    def __init__(self, sem, val, key):
        self.sem, self.val, self.key = sem, val, key


class Eng:
    def __init__(self, name):
        self.name = name
        self.ops = []
        self.cnt = 0
        self.sem = None
        self.seen = {}

    def wait(self, *toks):
        for t in toks:
            if t is None:
                continue
            if isinstance(t, (list, tuple)):
                self.wait(*t)
                continue
            if self.seen.get(t.key, 0) >= t.val:
                continue
            self.seen[t.key] = t.val
            self.ops.append(("wait", t.sem, t.val))

    def do(self, fn, wait=(), sig=True):
        self.wait(*wait)
        if sig:
            self.cnt += 1
            self.ops.append(("op", fn, self.sem, 1))
            return Tok(self.sem, self.cnt, self.name)
        self.ops.append(("op", fn, None, 0))
        return None

    def dma(self, fn, sem, semkey, count, wait=()):
        self.wait(*wait)
        self.ops.append(("op", fn, sem, 16))
        return Tok(sem, 16 * count, semkey)

    def emit(self, e):
        for op in self.ops:
            if op[0] == "wait":
                e.wait_ge(op[1], op[2])
            else:
                ins = op[1](e)
                if op[2] is not None:
                    ins.then_inc(op[2], op[3])


def weight_plan():
    plan = []
    for c in range(16):
        plan += [("in", 4096 + 128 * c), ("in", 6144 + 128 * c), ("in", 2048 + 128 * c)]
    for gi in range(4):
        plan.append(("pw", gi))
        for cc in range(4):
            plan.append(("in", 128 * (4 * gi + cc)))
        for cc in range(4):
            c = 4 * gi + cc
            plan += [("in", 8192 + 128 * c), ("in", 10240 + 128 * c), ("co", 128 * c)]
    for cb in range(4):
        for kq in range(4):
            plan.append(("wo", cb, kq))

    def gu(g):
        out = []
        for jj in range(4):
            j = 4 * g + jj
            out += [("gu", 128 * j), ("gu", FF + 128 * j)]
        return out

    plan += gu(0)
    for g in range(NPASS):
        if g + 1 < NPASS:
            plan += gu(g + 1)
        for cb in range(4):
            plan.append(("wd", g, cb))
    return plan


def build_program(phase_limit=99, debug=False):
    nc = bass.Bass("TRN2", target_bir_lowering=False)
    dram = {}

    def din(name, shape, dt=F32):
        dram[name] = nc.dram_tensor(name, list(shape), dt, kind="ExternalInput").ap()
        return dram[name]

    xh = din("xh", [T, D])
    w_in = din("w_in", [D, 12288])
    pool_w = din("pw2", [512, 2048])
    conv_out_w = din("conv_out_w", [D, D])
    w_o = din("w_o", [D, D])
    w_gu = din("w_gate_up", [D, 2 * FF])
    w_dn = din("w_down", [FF, D])
    pvec = din("pvec", [128, 128])
    g3b = din("g3b", [128, D])
    ident_d = din("ident", [128, 128], BF16)
    out = nc.dram_tensor("out", [TR, D], F32, kind="ExternalOutput").ap()
    dbg = {}
    if debug:
        dbg["hnT"] = nc.dram_tensor("dbg_hnT", [128, 16 * T], BF16, kind="ExternalOutput").ap()
        dbg["zT"] = nc.dram_tensor("dbg_zT", [128, 16 * TR], BF16, kind="ExternalOutput").ap()
        dbg["mT"] = nc.dram_tensor("dbg_mT", [128, 16 * TR], BF16, kind="ExternalOutput").ap()
        dbg["h1"] = nc.dram_tensor("dbg_h1", [128, 8 * D], F32, kind="ExternalOutput").ap()
        dbg["hn1T"] = nc.dram_tensor("dbg_hn1T", [128, 16 * TR], BF16, kind="ExternalOutput").ap()

    w_in_v = w_in.rearrange("(ko p) c -> p ko c", p=128)
    co_v = conv_out_w.rearrange("(ko p) c -> p ko c", p=128)
    wo_v = w_o.rearrange("(ko p) c -> p ko c", p=128)
    gu_v = w_gu.rearrange("(ko p) c -> p ko c", p=128)
    wd_v = w_dn.rearrange("(ko p) c -> p ko c", p=128)
    pw_v = pool_w.rearrange("(kk p) c -> p kk c", p=128)

    with ExitStack() as es:
        def sb(name, n, dt=F32):
            return es.enter_context(nc.sbuf_tensor(name, [128, n], dt))

        R1 = sb("R1", 16640)
        R2 = sb("R2", 8192)
        R3 = sb("R3", 12368)
        WR = sb("WR", NS * 1024)
        PWB = sb("PWB", 2048)
        pv = sb("pv", 128)
        g3 = sb("g3", D)
        ident = sb("ident_sb", 128, BF16)
        stats = sb("stats", 64)
        ps = es.enter_context(nc.psum_tensor("ps", [128, 8, 512], F32))

        def sem(name):
            return es.enter_context(nc.semaphore(name))

        PE, ACT, DVE, POOL, SP = Eng("pe"), Eng("act"), Eng("dve"), Eng("pool"), Eng("sp")
        for e in (PE, ACT, DVE, POOL, SP):
            e.sem = sem("s_" + e.name)
        wsem = [sem(f"w{i}") for i in range(NS)]
        pwsem = [sem(f"pw{i}") for i in range(2)]
        xsem = [sem(f"x{i}") for i in range(9)]
        hsem = [sem(f"h{i}") for i in range(8)]
        osem = [sem(f"o{i}") for i in range(8)]
        osem2 = [sem(f"ob{i}") for i in range(8)]
        csem = sem("const")
        g3sem = sem("g3")
        dsem = sem("dbg")

        hnT = R1[:, 0:8320].bitcast(BF16).rearrange("p (k t) -> p k t", k=16)
        zT = R1[:, 8320:16512].bitcast(BF16).rearrange("p (k t) -> p k t", k=16)
        h1 = R1[:, 0:16384].rearrange("p (t d) -> p t d", t=8)
        mT = R2[:, 0:8192].bitcast(BF16).rearrange("p (k t) -> p k t", k=16)
        junk = R2[:, 4096:5120].bitcast(BF16)
        hnb = R2[:, 5120:7168].bitcast(BF16).rearrange("p (b d) -> p b d", b=2)
        actT = R2[:, 0:4096].bitcast(BF16).rearrange("p (b j t) -> p b j t", b=2, j=4)
        sg = R2[:, 4096:5120].rearrange("p (b n) -> p b n", b=2)
        gcs = R3[:, 0:1040]
        cv = R3[:, 1040:2080]
        ub = R3[:, 2080:3120]
        S1 = R3[:, 3120:4160]
        S2 = R3[:, 4160:5200]
        pooledT = R3[:, 5200:7248].bitcast(BF16).rearrange("p (k t) -> p k t", k=4)
        ybuf = R3[:, 7248:8272].rearrange("p (b n) -> p b n", b=2)
        gA = R3[:, 8272:9296].rearrange("p (b n) -> p b n", b=2)
        gB = R3[:, 9296:10320].rearrange("p (b n) -> p b n", b=2)
        ta = R3[:, 10320:11344].rearrange("p (b n) -> p b n", b=2)
        tb = R3[:, 11344:12368].rearrange("p (b n) -> p b n", b=2)
        hn1T = R3[:, 0:8192].bitcast(BF16).rearrange("p (k t) -> p k t", k=16)
        hn1b = R3[:, 8192:10240].bitcast(BF16).rearrange("p (b d) -> p b d", b=2)
        junk2 = R3[:, 10240:11264].bitcast(BF16)
        wring = WR[:, :].bitcast(BF16).rearrange("p (s n) -> p s n", s=NS)
        pwbuf = PWB[:, :].bitcast(BF16).rearrange("p (s n) -> p s n", s=2)
        ss1, rs1 = stats[:, 0:9], stats[:, 9:18]
        ss2, rs2 = stats[:, 18:26], stats[:, 26:34]
        ss3, rs3 = stats[:, 34:42], stats[:, 42:50]
        eps_ap = stats[:, 50:51]

        ring = {"u": 0, "free": [None] * 8}

        def bank_alloc():
            b = ring["u"] % 8
            ring["u"] += 1
            return b, ring["free"][b]

        def bank_release(b, *toks):
            ring["free"][b] = list(toks)

        def bank_f32(b):
            return ps[:, b, :]

        def bank_bf(b):
            return ps[:, b, :].bitcast(BF16).rearrange("p (k t) -> p k t", k=8)

        plan = weight_plan()
        ring_idx = {}
        ring_at = []
        for i_, key_ in enumerate(plan):
            if key_[0] != "pw":
                ring_idx[i_] = len(ring_at)
                ring_at.append(i_)
        wst = {"next_dma": 0, "cur": -1, "rel": {}, "tok": {}, "uses": [0] * NS, "pwuses": [0, 0]}

        def w_src(key):
            kind = key[0]
            if kind == "in":
                return w_in_v[:, :, key[1]:key[1] + 128], 16
            if kind == "co":
                return co_v[:, :, key[1]:key[1] + 128], 16
            if kind == "gu":
                return gu_v[:, :, key[1]:key[1] + 128], 16
            if kind == "pw":
                return pw_v[:, :, key[1] * 512:(key[1] + 1) * 512], 4
            if kind == "wo":
                return wo_v[:, 4 * key[2]:4 * key[2] + 4, key[1] * 512:(key[1] + 1) * 512], 4
            if kind == "wd":
                return wd_v[:, 4 * key[1]:4 * key[1] + 4, key[2] * 512:(key[2] + 1) * 512], 4
            raise KeyError(key)

        def w_view(i):
            key = plan[i]
            _, kdim = w_src(key)
            if key[0] == "pw":
                return pwbuf[:, key[1] % 2, :].rearrange("p (k c) -> p k c", k=kdim)
            return wring[:, ring_idx[i] % NS, :].rearrange("p (k c) -> p k c", k=kdim)

        def w_pump():
            while wst["next_dma"] < len(plan) and wst["next_dma"] <= wst["cur"] + NS - 1:
                j = wst["next_dma"]
                key = plan[j]
                waits = []
                if key[0] == "pw":
                    if key[1] >= 2:
                        pj = plan.index(("pw", key[1] - 2))
                        if pj not in wst["rel"]:
                            break
                        waits = [wst["rel"][pj]]
                    s = key[1] % 2
                    wst["pwuses"][s] += 1
                    semh, semk, cnt = pwsem[s], f"pw{s}", wst["pwuses"][s]
                else:
                    r = ring_idx[j]
                    if r >= NS:
                        pj = ring_at[r - NS]
                        if pj not in wst["rel"]:
                            break
                        waits = [wst["rel"][pj]]
                    s = r % NS
                    wst["uses"][s] += 1
                    semh, semk, cnt = wsem[s], f"w{s}", wst["uses"][s]
                src, _ = w_src(key)
                dst = w_view(j)
                wst["tok"][j] = POOL.dma(
                    (lambda e, dst=dst, src=src: e.dma_start(out=dst, in_=src)),
                    semh, semk, cnt, wait=waits)
                wst["next_dma"] += 1

        class WT:
            pass

        def w_use(key):
            wst["cur"] += 1
            i = wst["cur"]
            assert plan[i] == key, (i, plan[i], key)
            w_pump()
            assert i in wst["tok"], f"weight tile {i} {key} not prefetched (NS too small)"
            wt = WT()
            wt.i, wt.ap, wt.tok = i, w_view(i), wst["tok"][i]
            return wt

        def w_release(wt, tok):
            wst["rel"][wt.i] = tok
            w_pump()

        c1 = SP.dma(lambda e: e.dma_start(out=pv[:, :], in_=pvec), csem, "const", 1)
        c3 = SP.dma(lambda e: e.dma_start(out=ident[:, :], in_=ident_d), csem, "const", 2)
        c4 = DVE.do(lambda e: e.memset(eps_ap, EPS))
        const_tok = c3
        ACT.wait(c4)
        ACT.do(lambda e: e.activation(out=stats[:, 52:53], in_=eps_ap, func=AF.Sqrt), sig=False)
        for e in (PE, ACT, DVE):
            e.wait(const_tok)

        def pe_group(bank, n, K, lhs, rhs, waits, view=None, mid_wait=None):
            PE.wait(*waits)
            o = view if view is not None else bank_f32(bank)[:, 0:n]
            tok = None
            for k in range(K):
                if mid_wait is not None and k == mid_wait[0]:
                    PE.wait(*mid_wait[1])
                tok = PE.do(
                    (lambda e, o=o, l=lhs(k), r=rhs(k), st=(k == 0), sp=(k == K - 1):
                     e.matmul(o, lhsT=l, rhs=r, start=st, stop=sp)),
                    sig=(k == K - 1))
            return tok

        def norm_transpose(i, R, src_ap, src_tok, ss, rs, col, bufs, bufi, jk, gcol, dstT, t0, buf_free, extra_wait=()):
            a1 = ACT.do(lambda e: e.activation(out=jk[:R, :], in_=src_ap, func=AF.Square,
                                               accum_out=ss[:R, col:col + 1]),
                        wait=[src_tok, *extra_wait])
            d1 = ACT.do(lambda e: e.activation(out=rs[:R, col:col + 1], in_=ss[:R, col:col + 1], func=AF.Sqrt, scale=1.0 / D,
                                               bias=eps_ap[:R, :]),
                        wait=[a1])
            d2 = DVE.do(lambda e: e.reciprocal(out=rs[:R, col:col + 1], in_=rs[:R, col:col + 1]),
                        wait=[d1])
            hb = bufs[:R, bufi, :]
            a2 = ACT.do(lambda e: e.activation(out=hb, in_=src_ap, func=AF.Copy, scale=rs[:R, col:col + 1]),
                        wait=[d2, buf_free])

            def pe_part():
                last_tr = None
                evs = []
                for half in range(2):
                    b, ftok = bank_alloc()
                    PE.wait(ftok, a2)
                    bv = bank_bf(b)
                    for kk in range(8):
                        k = 8 * half + kk
                        last_tr = PE.do(
                            (lambda e, o=bv[:, kk, 0:R], i_=bufs[:R, bufi, k * 128:(k + 1) * 128]:
                             e.transpose(o, i_, ident[:R, :R])),
                            sig=(kk == 7))
                    gsl = pv[:, gcol + 8 * half:gcol + 8 * half + 8].unsqueeze(2).to_broadcast([128, 8, R])
                    ev = DVE.do(
                        (lambda e, o=dstT[:, 8 * half:8 * half + 8, t0:t0 + R], i0=bv[:, :, 0:R], i1=gsl:
                         e.tensor_tensor(out=o, in0=i0, in1=i1, op=ALU.mult)),
                        wait=[last_tr])
                    bank_release(b, ev)
                    evs.append(ev)
                return last_tr, evs[-1]

            return a2, pe_part

        tr_toks = {}
        a2_toks = {}
        hnT_done = None

        def xt_ap(i, R):
            if i < 6:
                return R3[:R, i * 2048:(i + 1) * 2048]
            return R1[:R, 8320 + (i - 6) * 2048:8320 + (i - 5) * 2048]

        xtok = []
        for i in range(9):
            R = 128 if i < 8 else 16
            xtok.append(SP.dma((lambda e, o=xt_ap(i, R), s=xh[128 * i:128 * i + R, :]: e.dma_start(out=o, in_=s)),
                               xsem[i], f"x{i}", 1))
        POOL.wait(xtok[6])
        g3_tok = SP.dma(lambda e: e.dma_start(out=g3[:, :], in_=g3b), g3sem, "g3", 1)

        def p0_stats(i):
            R = 128 if i < 8 else 16
            src = xt_ap(i, R)
            a1 = ACT.do(lambda e: e.activation(out=junk[:R, :], in_=src, func=AF.Square,
                                               accum_out=ss1[:R, i:i + 1]), wait=[xtok[i]])
            d1 = ACT.do(lambda e: e.activation(out=rs1[:R, i:i + 1], in_=ss1[:R, i:i + 1], func=AF.Sqrt, scale=1.0 / D,
                                               bias=eps_ap[:R, :]), wait=[a1])
            return DVE.do(lambda e: e.reciprocal(out=rs1[:R, i:i + 1], in_=rs1[:R, i:i + 1]), wait=[d1])

        def p0_rest(i, rtok):
            R = 128 if i < 8 else 16
            src = xt_ap(i, R)
            buf = i % 2
            hb = hnb[:R, buf, :]
            if i % 2 == 1:
                a2 = ACT.do(lambda e: e.activation(out=hb, in_=src, func=AF.Copy, scale=rs1[:R, i:i + 1]),
                            wait=[rtok, tr_toks.get(i - 2)])
            else:
                a2 = DVE.do(lambda e: e.tensor_scalar(out=hb, in0=src, scalar1=rs1[:R, i:i + 1], scalar2=None,
                                                      op0=ALU.mult), wait=[rtok, tr_toks.get(i - 2)])
            a2_toks[i] = a2
            last_tr = None
            ev = None
            for half in range(2):
                b_, ftok = bank_alloc()
                PE.wait(ftok, a2)
                bv = bank_bf(b_)
                for kk in range(8):
                    k = 8 * half + kk
                    last_tr = PE.do(
                        (lambda e, o=bv[:, kk, 0:R], i_=hnb[:R, buf, k * 128:(k + 1) * 128]:
                         e.transpose(o, i_, ident[:R, :R])),
                        sig=(kk == 7))
                gsl = pv[:, PV_G1 + 8 * half:PV_G1 + 8 * half + 8].unsqueeze(2).to_broadcast([128, 8, R])
                ev = DVE.do(
                    (lambda e, o=hnT[:, 8 * half:8 * half + 8, 128 * i:128 * i + R], i0=bv[:, :, 0:R], i1=gsl:
                     e.tensor_tensor(out=o, in0=i0, in1=i1, op=ALU.mult)),
                    wait=[last_tr])
                bank_release(b_, ev)
            tr_toks[i] = last_tr
            return ev

        rt = {}
        hn_tile = {}
        for i in range(9):
            rt[i] = p0_stats(i)
            if i >= 1:
                hn_tile[i - 1] = p0_rest(i - 1, rt[i - 1])
        hn_tile[8] = p0_rest(8, rt[8])
        hnT_done = hn_tile[8]
        last_p0_pe = tr_toks[8]
        last_p0_act = Tok(ACT.sem, ACT.cnt, "act")
        last_p0_dve = Tok(DVE.sem, DVE.cnt, "dve")

        def dump(name, src_ap):
            t = SP.dma((lambda e, o=dbg[name], s=src_ap: e.dma_start(out=o, in_=s)), dsem, "dbg",
                       dump.n + 1, wait=[Tok(DVE.sem, DVE.cnt, "dve"), Tok(PE.sem, PE.cnt, "pe"),
                                         Tok(ACT.sem, ACT.cnt, "act")])
            dump.n += 1
            return t
        dump.n = 0
        final_toks = []

        if debug and phase_limit == 0:
            final_toks.append(dump("hnT", R1[:, 0:8320].bitcast(BF16)))

        last_z = None
        if phase_limit >= 1:
            prev = {"cvread": [None] * 3, "tap": None, "z": [None, None]}
            ACT.wait(last_p0_dve)
            DVE.wait(last_p0_act)
            for c in range(16):
                wt_gc = w_use(("in", 4096 + 128 * c))
                wt_v = w_use(("in", 6144 + 128 * c))
                wt_gb = w_use(("in", 2048 + 128 * c))
                gc_ev = []
                tok = None
                for bi, (a, b) in enumerate(B3):
                    bank, ftok = bank_alloc()
                    tok = pe_group(bank, b - a, 16, lambda k: wt_gc.ap[:, k, :], lambda k, a=a, b=b: hnT[:, k, a:b],
                                   [ftok, wt_gc.tok, hnT_done])
                    ev = ACT.do((lambda e, o=gcs[:, a:b], i_=bank_f32(bank)[:, 0:b - a]:
                                 e.activation(out=o, in_=i_, func=AF.Copy)),
                                wait=[tok, prev["cvread"][bi]])
                    bank_release(bank, ev)
                    gc_ev.append(ev)
                w_release(wt_gc, tok)
                cv_ev = []
                for bi, (a, b) in enumerate(B3):
                    bank, ftok = bank_alloc()
                    tok = pe_group(bank, b - a, 16, lambda k: wt_v.ap[:, k, :], lambda k, a=a, b=b: hnT[:, k, a:b],
                                   [ftok, wt_v.tok])
                    ev = DVE.do((lambda e, o=cv[:, a:b], i0=bank_f32(bank)[:, 0:b - a], i1=gcs[:, a:b]:
                                 e.tensor_tensor(out=o, in0=i0, in1=i1, op=ALU.mult)),
                                wait=[tok, gc_ev[bi], prev["tap"]])
                    bank_release(bank, ev)
                    cv_ev.append(ev)
                    prev["cvread"][bi] = ev
                w_release(wt_v, tok)
                tap = [None, None]
                for step in range(3):
                    for bi, (a, b) in enumerate(B2):
                        y = ybuf[:, bi, :]
                        sh = 2 - step
                        wcol = pv[:, PV_CW + 3 * c + step:PV_CW + 3 * c + step + 1]
                        if step == 0:
                            tap[bi] = DVE.do((lambda e, y=y, i0=cv[:, a - sh:b - sh], w=wcol:
                                              e.tensor_scalar(out=y, in0=i0, scalar1=w, scalar2=None, op0=ALU.mult)),
                                             wait=[cv_ev, prev["z"][bi]])
                        else:
                            tap[bi] = DVE.do((lambda e, y=y, i0=cv[:, a - sh:b - sh], w=wcol:
                                              e.scalar_tensor_tensor(out=y, in0=i0, scalar=w, in1=y,
                                                                     op0=ALU.mult, op1=ALU.add)),
                                             wait=[tap[bi]])
                prev["tap"] = tap[1]
                for bi, (a, b) in enumerate(B2):
                    bank, ftok = bank_alloc()
                    tok = pe_group(bank, 512, 16, lambda k: wt_gb.ap[:, k, :], lambda k, a=a, b=b: hnT[:, k, a:b],
                                   [ftok, wt_gb.tok])
                    ev = DVE.do((lambda e, o=zT[:, c, a - 16:b - 16], i0=bank_f32(bank)[:, 0:512], i1=ybuf[:, bi, :]:
                                 e.tensor_tensor(out=o, in0=i0, in1=i1, op=ALU.mult)),
                                wait=[tok, tap[bi]])
                    bank_release(bank, ev)
                    prev["z"][bi] = ev
                    last_z = ev
                w_release(wt_gb, tok)
            if debug and phase_limit == 1:
                final_toks.append(dump("zT", R1[:, 8320:16512].bitcast(BF16)))

        last_m = None
        last_p1_pe = None
        if phase_limit >= 2:
            prev = {"ubread": None, "s1": None, "s2": None, "ya": None, "ta": [None, None], "tb": [None, None],
                    "m": [None, None]}
            for gi in range(4):
                wwin = POOL_W[gi]
                wt_pw = w_use(("pw", gi))
                pooled_toks = []
                for cc in range(4):
                    c = 4 * gi + cc
                    wt_u = w_use(("in", 128 * c))
                    u_ev = []
                    tok = None
                    for bi, (a, b) in enumerate(B3):
                        bank, ftok = bank_alloc()
                        tok = pe_group(bank, b - a, 16, lambda k: wt_u.ap[:, k, :],
                                       lambda k, a=a, b=b: hnT[:, k, a:b], [ftok, wt_u.tok, hnT_done])
                        ev = ACT.do((lambda e, o=ub[:, a:b], i_=bank_f32(bank)[:, 0:b - a]:
                                     e.activation(out=o, in_=i_, func=AF.Copy)),
                                    wait=[tok, prev["ubread"]])
                        bank_release(bank, ev)
                        u_ev.append(ev)
                    w_release(wt_u, tok)
                    cur = ub
                    curtok = u_ev
                    bufs_ = [(S1, "s1"), (S2, "s2")]
                    lag = 1
                    step = 0
                    while lag < wwin:
                        dst, dkey = bufs_[step % 2]
                        off = 2 * lag - 1
                        tk = DVE.do((lambda e, o=dst[:, off:T], i0=cur[:, off:T], i1=cur[:, off - lag:T - lag]:
                                     e.tensor_tensor(out=o, in0=i0, in1=i1, op=ALU.add)),
                                    wait=[curtok, prev[dkey]])
                        cur, curtok = dst, tk
                        prev[dkey] = tk
                        lag *= 2
                        step += 1
                    pt = DVE.do((lambda e, o=pooledT[:, cc, :], i0=cur[:, HALO:T], i1=ub[:, HALO:T], s=1.0 / wwin:
                                 e.scalar_tensor_tensor(out=o, in0=i0, scalar=s, in1=i1,
                                                        op0=ALU.mult, op1=ALU.subtract)),
                                wait=[curtok, u_ev, prev["ya"]])
                    prev["ubread"] = pt
                    prev["s1"] = pt
                    prev["s2"] = pt
                    pooled_toks.append(pt)
                for cc in range(4):
                    c = 4 * gi + cc
                    wt_ga = w_use(("in", 8192 + 128 * c))
                    wt_gbr = w_use(("in", 10240 + 128 * c))
                    wt_co = w_use(("co", 128 * c))
                    ga_ev, gb_ev = [None, None], [None, None]
                    tok = None
                    for bi, (a, b) in enumerate(B2):
                        bank, ftok = bank_alloc()
                        tok = pe_group(bank, 512, 16, lambda k: wt_ga.ap[:, k, :],
                                       lambda k, a=a, b=b: hnT[:, k, a:b], [ftok, wt_ga.tok])
                        ev = ACT.do((lambda e, o=gA[:, bi, :], i_=bank_f32(bank)[:, 0:512],
                                     bcol=pv[:, PV_BGA + c:PV_BGA + c + 1]:
                                     e.activation(out=o, in_=i_, func=AF.Sigmoid, bias=bcol)),
                                    wait=[tok, prev["ta"][bi]])
                        bank_release(bank, ev)
                        ga_ev[bi] = ev
                    w_release(wt_ga, tok)
                    for bi, (a, b) in enumerate(B2):
                        bank, ftok = bank_alloc()
                        tok = pe_group(bank, 512, 16, lambda k: wt_gbr.ap[:, k, :],
                                       lambda k, a=a, b=b: hnT[:, k, a:b], [ftok, wt_gbr.tok])
                        ev = ACT.do((lambda e, o=gB[:, bi, :], i_=bank_f32(bank)[:, 0:512],
                                     bcol=pv[:, PV_BGB + c:PV_BGB + c + 1]:
                                     e.activation(out=o, in_=i_, func=AF.Sigmoid, bias=bcol)),
                                    wait=[tok, prev["tb"][bi]])
                        bank_release(bank, ev)
                        gb_ev[bi] = ev
                    w_release(wt_gbr, tok)
                    for bi, (a, b) in enumerate(B2):
                        bank, ftok = bank_alloc()
                        tok = pe_group(bank, 512, 4, lambda kk: wt_pw.ap[:, kk, cc * 128:(cc + 1) * 128],
                                       lambda kk, a=a, b=b: pooledT[:, kk, a - 16:b - 16],
                                       [ftok, wt_pw.tok, pooled_toks])
                        prev["ya"] = tok
                        ev = DVE.do((lambda e, o=ta[:, bi, :], i0=bank_f32(bank)[:, 0:512],
                                     s=pv[:, PV_PSC + c:PV_PSC + c + 1], i1=gA[:, bi, :]:
                                     e.scalar_tensor_tensor(out=o, in0=i0, scalar=s, in1=i1,
                                                            op0=ALU.mult, op1=ALU.mult)),
                                    wait=[tok, ga_ev[bi], prev["m"][bi]])
                        bank_release(bank, ev)
                        prev["ta"][bi] = ev
                    if cc == 3:
                        w_release(wt_pw, tok)
                    for bi, (a, b) in enumerate(B2):
                        bank, ftok = bank_alloc()
                        tok = pe_group(bank, 512, 16, lambda k: wt_co.ap[:, k, :],
                                       lambda k, a=a, b=b: zT[:, k, a - 16:b - 16], [ftok, wt_co.tok, last_z])
                        ev = DVE.do((lambda e, o=tb[:, bi, :], i0=bank_f32(bank)[:, 0:512], i1=gB[:, bi, :]:
                                     e.tensor_tensor(out=o, in0=i0, in1=i1, op=ALU.mult)),
                                    wait=[tok, gb_ev[bi], prev["m"][bi]])
                        bank_release(bank, ev)
                        prev["tb"][bi] = ev
                        mv = DVE.do((lambda e, o=mT[:, c, a - 16:b - 16], i0=tb[:, bi, :], i1=ta[:, bi, :]:
                                     e.tensor_tensor(out=o, in0=i0, in1=i1, op=ALU.add)),
                                    wait=[ev, prev["ta"][bi], last_p0_pe, last_p0_act])
                        prev["m"][bi] = mv
                        last_m = mv
                    w_release(wt_co, tok)
                    last_p1_pe = tok
            if debug and phase_limit == 2:
                final_toks.append(dump("mT", R2[:, 0:8192].bitcast(BF16)))

        pending_tr = []
        hn1_done = [None] * 8
        last_p2_pe = None
        if phase_limit >= 3:
            last_p1_act = Tok(ACT.sem, ACT.cnt, "act")
            last_p1_dve = Tok(DVE.sem, DVE.cnt, "dve")
            xr = []
            for t in range(8):
                xr.append(SP.dma((lambda e, o=h1[:, t, :], s=xh[HALO + 128 * t:HALO + 128 * (t + 1), :]:
                                  e.dma_start(out=o, in_=s)),
                                 hsem[t], f"h{t}", 1, wait=[last_p1_pe, last_m]))
            tr2 = {}
            a2s = {}
            for cb in range(4):
                wts = [w_use(("wo", cb, kq)) for kq in range(4)]
                tok = None
                for t in range(8):
                    bank, ftok = bank_alloc()
                    tok = pe_group(bank, 512, 16, lambda k, t=t: mT[:, k, 128 * t:128 * (t + 1)],
                                   lambda k: wts[k // 4].ap[:, k % 4, :],
                                   [ftok, last_m] + [w.tok for w in wts])
                    ev = DVE.do((lambda e, o=h1[:, t, cb * 512:(cb + 1) * 512], i0=bank_f32(bank)[:, 0:512]:
                                 e.tensor_tensor(out=o, in0=i0, in1=o, op=ALU.add)),
                                wait=[tok, xr[t]])
                    bank_release(bank, ev)
                    last_p2_pe = tok
                    if cb == 3:
                        if len(pending_tr) >= 2:
                            tt, pp = pending_tr.pop(0)
                            tr2[tt], hn1_done[tt] = pp()
                        a2, pe_part = norm_transpose(t, 128, h1[:, t, :], ev, ss2, rs2, t, hn1b, t % 2, junk2,
                                                     PV_G2, hn1T, 128 * t, tr2.get(t - 2),
                                                     extra_wait=[last_p1_act, last_p1_dve, last_p1_pe])
                        a2s[t] = a2
                        pending_tr.append((t, pe_part))
                for w in wts:
                    w_release(w, tok)
            if phase_limit == 3:
                while pending_tr:
                    tt, pp = pending_tr.pop(0)
                    tr2[tt], hn1_done[tt] = pp()
                if debug:
                    final_toks.append(dump("h1", R1[:, 0:16384]))
                    final_toks.append(dump("hn1T", R3[:, 0:8192].bitcast(BF16)))

        out_toks = []
        if phase_limit >= 4:
            st = {"sg": [None, None], "sgi": 0, "dn_last": {}, "first": True}

            def flush_tr():
                while pending_tr:
                    tt, pp = pending_tr.pop(0)
                    _, hn1_done[tt] = pp()

            def gu_unit(g, jj, blk, wt_g, wt_u):
                need = [hn1_done[t] for t in range(4 * blk, 4 * blk + 4) if hn1_done[t] is not None]
                bg, ftok = bank_alloc()
                tg = pe_group(bg, 512, 16, lambda k: wt_g.ap[:, k, :],
                              lambda k, blk=blk: hn1T[:, k, 512 * blk:512 * (blk + 1)],
                              [ftok, wt_g.tok, last_p2_pe] + need)
                si = st["sgi"] % 2
                st["sgi"] += 1
                ea = ACT.do((lambda e, o=sg[:, si, :], i_=bank_f32(bg)[:, 0:512]:
                             e.activation(out=o, in_=i_, func=AF.Silu)),
                            wait=[tg, st["sg"][si], last_p2_pe])
                bank_release(bg, ea)
                bu, ftok = bank_alloc()
                tu = pe_group(bu, 512, 16, lambda k: wt_u.ap[:, k, :],
                              lambda k, blk=blk: hn1T[:, k, 512 * blk:512 * (blk + 1)],
                              [ftok, wt_u.tok])
                ed = DVE.do((lambda e, o=actT[:, g % 2, jj, 512 * blk:512 * (blk + 1)],
                             i0=bank_f32(bu)[:, 0:512], i1=sg[:, si, :]:
                             e.tensor_tensor(out=o, in0=i0, in1=i1, op=ALU.mult)),
                            wait=[tu, ea, st["dn_last"].get(g - 2), last_p2_pe])
                bank_release(bu, ed)
                st["sg"][si] = ed
                st["act_done_%d" % g] = ed
                return tg, tu

            def gate_up(g):
                if g == 0:
                    wts = []
                    for jj in range(4):
                        j = 4 * g + jj
                        wts.append((w_use(("gu", 128 * j)), w_use(("gu", FF + 128 * j))))
                    for blk in range(2):
                        for jj in range(4):
                            if blk == 0 and jj == 2 and pending_tr:
                                flush_tr()
                            tg, tu = gu_unit(g, jj, blk, wts[jj][0], wts[jj][1])
                            if blk == 1:
                                w_release(wts[jj][0], tg)
                                w_release(wts[jj][1], tu)
                    return
                for jj in range(4):
                    j = 4 * g + jj
                    wt_g = w_use(("gu", 128 * j))
                    wt_u = w_use(("gu", FF + 128 * j))
                    tg = tu = None
                    for blk in range(2):
                        tg, tu = gu_unit(g, jj, blk, wt_g, wt_u)
                    w_release(wt_g, tg)
                    w_release(wt_u, tu)

            fin = {}
            fin_pool = {}
            fin_scale = {}

            tmpg2 = R2[:, 5120:7168].rearrange("p (b n) -> p b n", b=2)

            def final_out(t, d1):
                ev_t, p3 = fin_pool[t]
                d2 = DVE.do((lambda e, o=rs3[:, t:t + 1]: e.reciprocal(out=o, in_=o)), wait=[d1])
                d3 = DVE.do((lambda e, o=h1[:, t, 0:1024], s=rs3[:, t:t + 1]:
                             e.scalar_tensor_tensor(out=o, in0=o, scalar=s, in1=g3[:, 0:1024],
                                                    op0=ALU.mult, op1=ALU.mult)),
                            wait=[d2, g3_tok])
                a3 = ACT.do((lambda e, o=h1[:, t, 1024:2048], s=rs3[:, t:t + 1], tg=tmpg2[:, t % 2, :]:
                             e.activation(out=o, in_=tg, func=AF.Copy, scale=s)),
                            wait=[d2, p3])
                fin_scale[t] = a3
                out_toks.append(ACT.dma((lambda e, o=out[128 * t:128 * (t + 1), 1024:2048], s=h1[:, t, 1024:2048]:
                                         e.dma_start(out=o, in_=s)),
                                        osem2[t], f"ob{t}", 1, wait=[a3]))
                out_toks.append(ACT.dma((lambda e, o=out[128 * t:128 * (t + 1), 0:1024], s=h1[:, t, 0:1024]:
                                         e.dma_start(out=o, in_=s)),
                                        osem[t], f"o{t}", 1, wait=[d3]))

            def down(g, merged_prev=False):
                gs = [g - 1, g] if merged_prev else [g]
                wts_all = [[w_use(("wd", gg, cb)) for cb in range(4)] for gg in gs]
                wts = [w for ws_ in wts_all for w in ws_]
                nk = 4 * len(gs)
                tok = None
                for t in range(8):
                    ev = None
                    for cb in range(4):
                        bank, ftok = bank_alloc()
                        tok = pe_group(bank, 512, nk,
                                       lambda kk, t=t: actT[:, gs[kk // 4] % 2, kk % 4, 128 * t:128 * (t + 1)],
                                       lambda kk, cb=cb: wts_all[kk // 4][cb].ap[:, kk % 4, :],
                                       [ftok, st["act_done_%d" % gs[0]]] + [w.tok for w in wts],
                                       mid_wait=((4, [st["act_done_%d" % gs[1]]]) if merged_prev else None))
                        ev = DVE.do((lambda e, o=h1[:, t, cb * 512:(cb + 1) * 512], i0=bank_f32(bank)[:, 0:512]:
                                     e.tensor_tensor(out=o, in0=i0, in1=o, op=ALU.add)),
                                    wait=[tok])
                        bank_release(bank, ev)
                    if g == NPASS - 1:
                        a1 = ACT.do((lambda e, i_=h1[:, t, :], acc=ss3[:, t:t + 1]:
                                     e.activation(out=junk2[:, :], in_=i_, func=AF.Square,
                                                  accum_out=acc)),
                                    wait=[ev])
                        d1 = ACT.do((lambda e, o=rs3[:, t:t + 1], i0=ss3[:, t:t + 1]:
                                     e.activation(out=o, in_=i0, func=AF.Sqrt, scale=1.0 / D, bias=eps_ap[:, :])), wait=[a1])
                        if t >= 1:
                            final_out(t - 1, fin[t - 1])
                        fin[t] = d1
                        p3 = POOL.do((lambda e, i0=h1[:, t, 1024:2048], tg=tmpg2[:, t % 2, :]:
                                      e.tensor_tensor(out=tg, in0=i0, in1=g3[:, 1024:2048], op=ALU.mult)),
                                     wait=[ev, g3_tok, fin_scale.get(t - 2)])
                        fin_pool[t] = (ev, p3)
                if g == NPASS - 1:
                    final_out(7, fin[7])
                st["dn_last"][g] = tok
                for w in wts:
                    w_release(w, tok)

            gate_up(0)
            for g in range(NPASS):
                if g + 1 < NPASS:
                    gate_up(g + 1)
                if g == NPASS - 2:
                    continue
                down(g, merged_prev=(g == NPASS - 1))

        ACT.wait(*out_toks)
        SP.wait(*final_toks)
        SP.wait(Tok(PE.sem, PE.cnt, "pe"), Tok(ACT.sem, ACT.cnt, "act"), Tok(DVE.sem, DVE.cnt, "dve"), g3_tok)
        if POOL.cnt:
            SP.wait(Tok(POOL.sem, POOL.cnt, "pool"))
        for s in range(NS):
            if wst["uses"][s]:
                SP.wait(Tok(wsem[s], 16 * wst["uses"][s], f"w{s}"))
        for s in range(2):
            if wst["pwuses"][s]:
                SP.wait(Tok(pwsem[s], 16 * wst["pwuses"][s], f"pw{s}"))

        with nc.Block() as block:
            @block.sync
            def _(e):
                SP.emit(e)

            @block.gpsimd
            def _(e):
                POOL.emit(e)

            @block.tensor
            def _(e):
                PE.emit(e)

            @block.scalar
            def _(e):
                ACT.emit(e)

            @block.vector
            def _(e):
                DVE.emit(e)
    return nc


def make_in_maps(x, meta_tokens, norm_mix_g, w_in, b_gate, pool_w, pool_scale, conv_w, conv_out_w, w_o,
                 norm_ffn_g, w_gate_up, w_down, norm_final_g):
    f = lambda a: np.ascontiguousarray(np.asarray(a, dtype=np.float32))
    x = f(x)
    meta = f(meta_tokens)

    def fm(v):
        return f(v).reshape(16, 128).T

    pvec = np.zeros((128, 128), np.float32)
    pvec[:, PV_G1:PV_G1 + 16] = fm(norm_mix_g)
    bg = f(b_gate)
    pvec[:, PV_BGA:PV_BGA + 16] = fm(bg[:D])
    pvec[:, PV_BGB:PV_BGB + 16] = fm(bg[D:])
    pvec[:, PV_PSC:PV_PSC + 16] = fm(pool_scale)
    cw = f(conv_w)
    for tap in range(3):
        pvec[:, PV_CW + tap:PV_CW + 48:3] = fm(cw[tap])
    pvec[:, PV_G2:PV_G2 + 16] = fm(norm_ffn_g)
    g3b = np.ascontiguousarray(np.broadcast_to(f(norm_final_g)[None, :], (128, D)))
    ident = np.eye(128, dtype=np.float32).astype(ml_dtypes.bfloat16)
    shared = {
        "w_in": f(w_in), "pw2": np.ascontiguousarray(f(pool_w).transpose(1, 0, 2).reshape(512, 2048)), "conv_out_w": f(conv_out_w),
        "w_o": f(w_o), "w_gate_up": f(w_gate_up), "w_down": f(w_down),
        "pvec": pvec, "g3b": g3b, "ident": ident,
    }
    maps = []
    for i in range(N_CORES):
        b, half = i // 2, i % 2
        halo = meta if half == 0 else x[b, TR - HALO:TR]
        xh = np.concatenate([halo, x[b, half * TR:(half + 1) * TR]], axis=0)
        m = dict(shared)
        m["xh"] = np.ascontiguousarray(xh)
        maps.append(m)
    return maps


def kernel(x, meta_tokens, norm_mix_g, w_in, b_gate, pool_w, pool_scale, conv_w, conv_out_w, w_o,
           norm_ffn_g, w_gate_up, w_down, norm_final_g):
    maps = make_in_maps(x, meta_tokens, norm_mix_g, w_in, b_gate, pool_w, pool_scale, conv_w, conv_out_w,
                        w_o, norm_ffn_g, w_gate_up, w_down, norm_final_g)
    nc = build_program()
    res = run_bass_kernel_spmd(nc, maps, core_ids=list(range(N_CORES)))
    outp = np.empty((4, 2 * TR, D), np.float32)
    for i in range(N_CORES):
        b, half = i // 2, i % 2
        outp[b, half * TR:(half + 1) * TR] = res.results[i]["out"]
    return outp
```

```python
from contextlib import ExitStack

import numpy as np
import ml_dtypes
import concourse.bass as bass
import concourse.mybir as mybir
from concourse.bass_utils import run_bass_kernel_spmd

F32 = mybir.dt.float32
BF16 = mybir.dt.bfloat16
AF = mybir.ActivationFunctionType
ALU = mybir.AluOpType

D = 2048
T = 1040
TR = 1024
HALO = 16
FF = 5632
NPASS = 11
B3 = [(0, 336), (336, 688), (688, 1040)]
B2 = [(16, 528), (528, 1040)]
POOL_W = (2, 4, 8, 16)
EPS = 1e-6
NS = 10
N_CORES = 8

PV_G1, PV_BGA, PV_BGB, PV_PSC, PV_CW, PV_G2 = 0, 16, 32, 48, 64, 112


class Tok:
    __slots__ = ("sem", "val", "key")

    def __init__(self, sem, val, key):
        self.sem, self.val, self.key = sem, val, key


class Eng:
    def __init__(self, name):
        self.name = name
        self.ops = []
        self.cnt = 0
        self.sem = None
        self.seen = {}

    def wait(self, *toks):
        for t in toks:
            if t is None:
                continue
            if isinstance(t, (list, tuple)):
                self.wait(*t)
                continue
            if self.seen.get(t.key, 0) >= t.val:
                continue
            self.seen[t.key] = t.val
            self.ops.append(("wait", t.sem, t.val))

    def do(self, fn, wait=(), sig=True):
        self.wait(*wait)
        if sig:
            self.cnt += 1
            self.ops.append(("op", fn, self.sem, 1))
            return Tok(self.sem, self.cnt, self.name)
        self.ops.append(("op", fn, None, 0))
        return None

    def dma(self, fn, sem, semkey, count, wait=()):
        self.wait(*wait)
        self.ops.append(("op", fn, sem, 16))
        return Tok(sem, 16 * count, semkey)

    def emit(self, e):
        for op in self.ops:
            if op[0] == "wait":
                e.wait_ge(op[1], op[2])
            else:
                ins = op[1](e)
                if op[2] is not None:
                    ins.then_inc(op[2], op[3])


def weight_plan():
    plan = []
    for c in range(16):
        plan += [("in", 4096 + 128 * c), ("in", 6144 + 128 * c), ("in", 2048 + 128 * c)]
    for gi in range(4):
        plan.append(("pw", gi))
        for cc in range(4):
            plan.append(("in", 128 * (4 * gi + cc)))
        for cc in range(4):
            c = 4 * gi + cc
            plan += [("in", 8192 + 128 * c), ("in", 10240 + 128 * c), ("co", 128 * c)]
    for cb in range(4):
        for kq in range(4):
            plan.append(("wo", cb, kq))

    def gu(g):
        out = []
        for jj in range(4):
            j = 4 * g + jj
            out += [("gu", 128 * j), ("gu", FF + 128 * j)]
        return out

    plan += gu(0)
    for g in range(NPASS):
        if g + 1 < NPASS:
            plan += gu(g + 1)
        for cb in range(4):
            plan.append(("wd", g, cb))
    return plan


def build_program(phase_limit=99, debug=False):
    nc = bass.Bass("TRN2", target_bir_lowering=False)
    dram = {}

    def din(name, shape, dt=F32):
        dram[name] = nc.dram_tensor(name, list(shape), dt, kind="ExternalInput").ap()
        return dram[name]

    xh = din("xh", [T, D])
    w_in = din("w_in", [D, 12288])
    pool_w = din("pw2", [512, 2048])
    conv_out_w = din("conv_out_w", [D, D])
    w_o = din("w_o", [D, D])
    w_gu = din("w_gate_up", [D, 2 * FF])
    w_dn = din("w_down", [FF, D])
    pvec = din("pvec", [128, 128])
    g3b = din("g3b", [128, D])
    ident_d = din("ident", [128, 128], BF16)
    out = nc.dram_tensor("out", [TR, D], F32, kind="ExternalOutput").ap()
    dbg = {}
    if debug:
        dbg["hnT"] = nc.dram_tensor("dbg_hnT", [128, 16 * T], BF16, kind="ExternalOutput").ap()
        dbg["zT"] = nc.dram_tensor("dbg_zT", [128, 16 * TR], BF16, kind="ExternalOutput").ap()
        dbg["mT"] = nc.dram_tensor("dbg_mT", [128, 16 * TR], BF16, kind="ExternalOutput").ap()
        dbg["h1"] = nc.dram_tensor("dbg_h1", [128, 8 * D], F32, kind="ExternalOutput").ap()
        dbg["hn1T"] = nc.dram_tensor("dbg_hn1T", [128, 16 * TR], BF16, kind="ExternalOutput").ap()

    w_in_v = w_in.rearrange("(ko p) c -> p ko c", p=128)
    co_v = conv_out_w.rearrange("(ko p) c -> p ko c", p=128)
    wo_v = w_o.rearrange("(ko p) c -> p ko c", p=128)
    gu_v = w_gu.rearrange("(ko p) c -> p ko c", p=128)
    wd_v = w_dn.rearrange("(ko p) c -> p ko c", p=128)
    pw_v = pool_w.rearrange("(kk p) c -> p kk c", p=128)

    with ExitStack() as es:
        def sb(name, n, dt=F32):
            return es.enter_context(nc.sbuf_tensor(name, [128, n], dt))

        R1 = sb("R1", 16640)
        R2 = sb("R2", 8192)
        R3 = sb("R3", 12368)
        WR = sb("WR", NS * 1024)
        PWB = sb("PWB", 2048)
        pv = sb("pv", 128)
        g3 = sb("g3", D)
        ident = sb("ident_sb", 128, BF16)
        stats = sb("stats", 64)
        ps = es.enter_context(nc.psum_tensor("ps", [128, 8, 512], F32))

        def sem(name):
            return es.enter_context(nc.semaphore(name))

        PE, ACT, DVE, POOL, SP = Eng("pe"), Eng("act"), Eng("dve"), Eng("pool"), Eng("sp")
        for e in (PE, ACT, DVE, POOL, SP):
            e.sem = sem("s_" + e.name)
        wsem = [sem(f"w{i}") for i in range(NS)]
        pwsem = [sem(f"pw{i}") for i in range(2)]
        xsem = [sem(f"x{i}") for i in range(9)]
        hsem = [sem(f"h{i}") for i in range(8)]
        osem = [sem(f"o{i}") for i in range(8)]
        osem2 = [sem(f"ob{i}") for i in range(8)]
        csem = sem("const")
        g3sem = sem("g3")
        dsem = sem("dbg")

        hnT = R1[:, 0:8320].bitcast(BF16).rearrange("p (k t) -> p k t", k=16)
        zT = R1[:, 8320:16512].bitcast(BF16).rearrange("p (k t) -> p k t", k=16)
        h1 = R1[:, 0:16384].rearrange("p (t d) -> p t d", t=8)
        mT = R2[:, 0:8192].bitcast(BF16).rearrange("p (k t) -> p k t", k=16)
        junk = R2[:, 4096:5120].bitcast(BF16)
        hnb = R2[:, 5120:7168].bitcast(BF16).rearrange("p (b d) -> p b d", b=2)
        actT = R2[:, 0:4096].bitcast(BF16).rearrange("p (b j t) -> p b j t", b=2, j=4)
        sg = R2[:, 4096:5120].rearrange("p (b n) -> p b n", b=2)
        gcs = R3[:, 0:1040]
        cv = R3[:, 1040:2080]
        ub = R3[:, 2080:3120]
        S1 = R3[:, 3120:4160]
        S2 = R3[:, 4160:5200]
        pooledT = R3[:, 5200:7248].bitcast(BF16).rearrange("p (k t) -> p k t", k=4)
        ybuf = R3[:, 7248:8272].rearrange("p (b n) -> p b n", b=2)
        gA = R3[:, 8272:9296].rearrange("p (b n) -> p b n", b=2)
        gB = R3[:, 9296:10320].rearrange("p (b n) -> p b n", b=2)
        ta = R3[:, 10320:11344].rearrange("p (b n) -> p b n", b=2)
        tb = R3[:, 11344:12368].rearrange("p (b n) -> p b n", b=2)
        hn1T = R3[:, 0:8192].bitcast(BF16).rearrange("p (k t) -> p k t", k=16)
        hn1b = R3[:, 8192:10240].bitcast(BF16).rearrange("p (b d) -> p b d", b=2)
        junk2 = R3[:, 10240:11264].bitcast(BF16)
        wring = WR[:, :].bitcast(BF16).rearrange("p (s n) -> p s n", s=NS)
        pwbuf = PWB[:, :].bitcast(BF16).rearrange("p (s n) -> p s n", s=2)
        ss1, rs1 = stats[:, 0:9], stats[:, 9:18]
        ss2, rs2 = stats[:, 18:26], stats[:, 26:34]
        ss3, rs3 = stats[:, 34:42], stats[:, 42:50]
        eps_ap = stats[:, 50:51]

        ring = {"u": 0, "free": [None] * 8}

        def bank_alloc():
            b = ring["u"] % 8
            ring["u"] += 1
            return b, ring["free"][b]

        def bank_release(b, *toks):
            ring["free"][b] = list(toks)

        def bank_f32(b):
            return ps[:, b, :]

        def bank_bf(b):
            return ps[:, b, :].bitcast(BF16).rearrange("p (k t) -> p k t", k=8)

        plan = weight_plan()
        ring_idx = {}
        ring_at = []
        for i_, key_ in enumerate(plan):
            if key_[0] != "pw":
                ring_idx[i_] = len(ring_at)
                ring_at.append(i_)
        wst = {"next_dma": 0, "cur": -1, "rel": {}, "tok": {}, "uses": [0] * NS, "pwuses": [0, 0]}

        def w_src(key):
            kind = key[0]
            if kind == "in":
                return w_in_v[:, :, key[1]:key[1] + 128], 16
            if kind == "co":
                return co_v[:, :, key[1]:key[1] + 128], 16
            if kind == "gu":
                return gu_v[:, :, key[1]:key[1] + 128], 16
            if kind == "pw":
                return pw_v[:, :, key[1] * 512:(key[1] + 1) * 512], 4
            if kind == "wo":
                return wo_v[:, 4 * key[2]:4 * key[2] + 4, key[1] * 512:(key[1] + 1) * 512], 4
            if kind == "wd":
                return wd_v[:, 4 * key[1]:4 * key[1] + 4, key[2] * 512:(key[2] + 1) * 512], 4
            raise KeyError(key)

        def w_view(i):
            key = plan[i]
            _, kdim = w_src(key)
            if key[0] == "pw":
                return pwbuf[:, key[1] % 2, :].rearrange("p (k c) -> p k c", k=kdim)
            return wring[:, ring_idx[i] % NS, :].rearrange("p (k c) -> p k c", k=kdim)

        def w_pump():
            while wst["next_dma"] < len(plan) and wst["next_dma"] <= wst["cur"] + NS - 1:
                j = wst["next_dma"]
                key = plan[j]
                waits = []
                if key[0] == "pw":
                    if key[1] >= 2:
                        pj = plan.index(("pw", key[1] - 2))
                        if pj not in wst["rel"]:
                            break
                        waits = [wst["rel"][pj]]
                    s = key[1] % 2
                    wst["pwuses"][s] += 1
                    semh, semk, cnt = pwsem[s], f"pw{s}", wst["pwuses"][s]
                else:
                    r = ring_idx[j]
                    if r >= NS:
                        pj = ring_at[r - NS]
                        if pj not in wst["rel"]:
                            break
                        waits = [wst["rel"][pj]]
                    s = r % NS
                    wst["uses"][s] += 1
                    semh, semk, cnt = wsem[s], f"w{s}", wst["uses"][s]
                src, _ = w_src(key)
                dst = w_view(j)
                wst["tok"][j] = POOL.dma(
                    (lambda e, dst=dst, src=src: e.dma_start(out=dst, in_=src)),
                    semh, semk, cnt, wait=waits)
                wst["next_dma"] += 1

        class WT:
            pass

        def w_use(key):
            wst["cur"] += 1
            i = wst["cur"]
            assert plan[i] == key, (i, plan[i], key)
            w_pump()
            assert i in wst["tok"], f"weight tile {i} {key} not prefetched (NS too small)"
            wt = WT()
            wt.i, wt.ap, wt.tok = i, w_view(i), wst["tok"][i]
            return wt

        def w_release(wt, tok):
            wst["rel"][wt.i] = tok
            w_pump()

        c1 = SP.dma(lambda e: e.dma_start(out=pv[:, :], in_=pvec), csem, "const", 1)
        c3 = SP.dma(lambda e: e.dma_start(out=ident[:, :], in_=ident_d), csem, "const", 2)
        c4 = DVE.do(lambda e: e.memset(eps_ap, EPS))
        const_tok = c3
        ACT.wait(c4)
        ACT.do(lambda e: e.activation(out=stats[:, 52:53], in_=eps_ap, func=AF.Sqrt), sig=False)
        for e in (PE, ACT, DVE):
            e.wait(const_tok)

        def pe_group(bank, n, K, lhs, rhs, waits, view=None, mid_wait=None):
            PE.wait(*waits)
            o = view if view is not None else bank_f32(bank)[:, 0:n]
            tok = None
            for k in range(K):
                if mid_wait is not None and k == mid_wait[0]:
                    PE.wait(*mid_wait[1])
                tok = PE.do(
                    (lambda e, o=o, l=lhs(k), r=rhs(k), st=(k == 0), sp=(k == K - 1):
                     e.matmul(o, lhsT=l, rhs=r, start=st, stop=sp)),
                    sig=(k == K - 1))
            return tok

        def norm_transpose(i, R, src_ap, src_tok, ss, rs, col, bufs, bufi, jk, gcol, dstT, t0, buf_free, extra_wait=()):
            a1 = ACT.do(lambda e: e.activation(out=jk[:R, :], in_=src_ap, func=AF.Square,
                                               accum_out=ss[:R, col:col + 1]),
                        wait=[src_tok, *extra_wait])
            d1 = ACT.do(lambda e: e.activation(out=rs[:R, col:col + 1], in_=ss[:R, col:col + 1], func=AF.Sqrt, scale=1.0 / D,
                                               bias=eps_ap[:R, :]),
                        wait=[a1])
            d2 = DVE.do(lambda e: e.reciprocal(out=rs[:R, col:col + 1], in_=rs[:R, col:col + 1]),
                        wait=[d1])
            hb = bufs[:R, bufi, :]
            a2 = ACT.do(lambda e: e.activation(out=hb, in_=src_ap, func=AF.Copy, scale=rs[:R, col:col + 1]),
                        wait=[d2, buf_free])

            def pe_part():
                last_tr = None
                evs = []
                for half in range(2):
                    b, ftok = bank_alloc()
                    PE.wait(ftok, a2)
                    bv = bank_bf(b)
                    for kk in range(8):
                        k = 8 * half + kk
                        last_tr = PE.do(
                            (lambda e, o=bv[:, kk, 0:R], i_=bufs[:R, bufi, k * 128:(k + 1) * 128]:
                             e.transpose(o, i_, ident[:R, :R])),
                            sig=(kk == 7))
                    gsl = pv[:, gcol + 8 * half:gcol + 8 * half + 8].unsqueeze(2).to_broadcast([128, 8, R])
                    ev = DVE.do(
                        (lambda e, o=dstT[:, 8 * half:8 * half + 8, t0:t0 + R], i0=bv[:, :, 0:R], i1=gsl:
                         e.tensor_tensor(out=o, in0=i0, in1=i1, op=ALU.mult)),
                        wait=[last_tr])
                    bank_release(b, ev)
                    evs.append(ev)
                return last_tr, evs[-1]

            return a2, pe_part

        tr_toks = {}
        a2_toks = {}
        hnT_done = None

        def xt_ap(i, R):
            if i < 6:
                return R3[:R, i * 2048:(i + 1) * 2048]
            return R1[:R, 8320 + (i - 6) * 2048:8320 + (i - 5) * 2048]

        xtok = []
        for i in range(9):
            R = 128 if i < 8 else 16
            xtok.append(SP.dma((lambda e, o=xt_ap(i, R), s=xh[128 * i:128 * i + R, :]: e.dma_start(out=o, in_=s)),
                               xsem[i], f"x{i}", 1))
        POOL.wait(xtok[6])
        g3_tok = SP.dma(lambda e: e.dma_start(out=g3[:, :], in_=g3b), g3sem, "g3", 1)

        def p0_stats(i):
            R = 128 if i < 8 else 16
            src = xt_ap(i, R)
            a1 = ACT.do(lambda e: e.activation(out=junk[:R, :], in_=src, func=AF.Square,
                                               accum_out=ss1[:R, i:i + 1]), wait=[xtok[i]])
            d1 = ACT.do(lambda e: e.activation(out=rs1[:R, i:i + 1], in_=ss1[:R, i:i + 1], func=AF.Sqrt, scale=1.0 / D,
                                               bias=eps_ap[:R, :]), wait=[a1])
            return DVE.do(lambda e: e.reciprocal(out=rs1[:R, i:i + 1], in_=rs1[:R, i:i + 1]), wait=[d1])

        def p0_rest(i, rtok):
            R = 128 if i < 8 else 16
            src = xt_ap(i, R)
            buf = i % 2
            hb = hnb[:R, buf, :]
            if i % 2 == 1:
                a2 = ACT.do(lambda e: e.activation(out=hb, in_=src, func=AF.Copy, scale=rs1[:R, i:i + 1]),
                            wait=[rtok, tr_toks.get(i - 2)])
            else:
                a2 = DVE.do(lambda e: e.tensor_scalar(out=hb, in0=src, scalar1=rs1[:R, i:i + 1], scalar2=None,
                                                      op0=ALU.mult), wait=[rtok, tr_toks.get(i - 2)])
            a2_toks[i] = a2
            last_tr = None
            ev = None
            for half in range(2):
                b_, ftok = bank_alloc()
                PE.wait(ftok, a2)
                bv = bank_bf(b_)
                for kk in range(8):
                    k = 8 * half + kk
                    last_tr = PE.do(
                        (lambda e, o=bv[:, kk, 0:R], i_=hnb[:R, buf, k * 128:(k + 1) * 128]:
                         e.transpose(o, i_, ident[:R, :R])),
                        sig=(kk == 7))
                gsl = pv[:, PV_G1 + 8 * half:PV_G1 + 8 * half + 8].unsqueeze(2).to_broadcast([128, 8, R])
                ev = DVE.do(
                    (lambda e, o=hnT[:, 8 * half:8 * half + 8, 128 * i:128 * i + R], i0=bv[:, :, 0:R], i1=gsl:
                     e.tensor_tensor(out=o, in0=i0, in1=i1, op=ALU.mult)),
                    wait=[last_tr])
                bank_release(b_, ev)
            tr_toks[i] = last_tr
            return ev

        rt = {}
        hn_tile = {}
        for i in range(9):
            rt[i] = p0_stats(i)
            if i >= 1:
                hn_tile[i - 1] = p0_rest(i - 1, rt[i - 1])
        hn_tile[8] = p0_rest(8, rt[8])
        hnT_done = hn_tile[8]
        last_p0_pe = tr_toks[8]
        last_p0_act = Tok(ACT.sem, ACT.cnt, "act")
        last_p0_dve = Tok(DVE.sem, DVE.cnt, "dve")

        def dump(name, src_ap):
            t = SP.dma((lambda e, o=dbg[name], s=src_ap: e.dma_start(out=o, in_=s)), dsem, "dbg",
                       dump.n + 1, wait=[Tok(DVE.sem, DVE.cnt, "dve"), Tok(PE.sem, PE.cnt, "pe"),
                                         Tok(ACT.sem, ACT.cnt, "act")])
            dump.n += 1
            return t
        dump.n = 0
        final_toks = []

        if debug and phase_limit == 0:
            final_toks.append(dump("hnT", R1[:, 0:8320].bitcast(BF16)))

        last_z = None
        if phase_limit >= 1:
            prev = {"cvread": [None] * 3, "tap": None, "z": [None, None]}
            ACT.wait(last_p0_dve)
            DVE.wait(last_p0_act)
            for c in range(16):
                wt_gc = w_use(("in", 4096 + 128 * c))
                wt_v = w_use(("in", 6144 + 128 * c))
                wt_gb = w_use(("in", 2048 + 128 * c))
                gc_ev = []
                tok = None
                for bi, (a, b) in enumerate(B3):
                    bank, ftok = bank_alloc()
                    tok = pe_group(bank, b - a, 16, lambda k: wt_gc.ap[:, k, :], lambda k, a=a, b=b: hnT[:, k, a:b],
                                   [ftok, wt_gc.tok, hnT_done])
                    ev = ACT.do((lambda e, o=gcs[:, a:b], i_=bank_f32(bank)[:, 0:b - a]:
                                 e.activation(out=o, in_=i_, func=AF.Copy)),
                                wait=[tok, prev["cvread"][bi]])
                    bank_release(bank, ev)
                    gc_ev.append(ev)
                w_release(wt_gc, tok)
                cv_ev = []
                for bi, (a, b) in enumerate(B3):
                    bank, ftok = bank_alloc()
                    tok = pe_group(bank, b - a, 16, lambda k: wt_v.ap[:, k, :], lambda k, a=a, b=b: hnT[:, k, a:b],
                                   [ftok, wt_v.tok])
                    ev = DVE.do((lambda e, o=cv[:, a:b], i0=bank_f32(bank)[:, 0:b - a], i1=gcs[:, a:b]:
                                 e.tensor_tensor(out=o, in0=i0, in1=i1, op=ALU.mult)),
                                wait=[tok, gc_ev[bi], prev["tap"]])
                    bank_release(bank, ev)
                    cv_ev.append(ev)
                    prev["cvread"][bi] = ev
                w_release(wt_v, tok)
                tap = [None, None]
                for step in range(3):
                    for bi, (a, b) in enumerate(B2):
                        y = ybuf[:, bi, :]
                        sh = 2 - step
                        wcol = pv[:, PV_CW + 3 * c + step:PV_CW + 3 * c + step + 1]
                        if step == 0:
                            tap[bi] = DVE.do((lambda e, y=y, i0=cv[:, a - sh:b - sh], w=wcol:
                                              e.tensor_scalar(out=y, in0=i0, scalar1=w, scalar2=None, op0=ALU.mult)),
                                             wait=[cv_ev, prev["z"][bi]])
                        else:
                            tap[bi] = DVE.do((lambda e, y=y, i0=cv[:, a - sh:b - sh], w=wcol:
                                              e.scalar_tensor_tensor(out=y, in0=i0, scalar=w, in1=y,
                                                                     op0=ALU.mult, op1=ALU.add)),
                                             wait=[tap[bi]])
                prev["tap"] = tap[1]
                for bi, (a, b) in enumerate(B2):
                    bank, ftok = bank_alloc()
                    tok = pe_group(bank, 512, 16, lambda k: wt_gb.ap[:, k, :], lambda k, a=a, b=b: hnT[:, k, a:b],
                                   [ftok, wt_gb.tok])
                    ev = DVE.do((lambda e, o=zT[:, c, a - 16:b - 16], i0=bank_f32(bank)[:, 0:512], i1=ybuf[:, bi, :]:
                                 e.tensor_tensor(out=o, in0=i0, in1=i1, op=ALU.mult)),
                                wait=[tok, tap[bi]])
                    bank_release(bank, ev)
                    prev["z"][bi] = ev
                    last_z = ev
                w_release(wt_gb, tok)
            if debug and phase_limit == 1:
                final_toks.append(dump("zT", R1[:, 8320:16512].bitcast(BF16)))

        last_m = None
        last_p1_pe = None
        m_toks = {}
        if phase_limit >= 2:
            prev = {"ubread": None, "s1": None, "s2": None, "ya": None, "ta": [None, None], "tb": [None, None],
                    "m": [None, None]}
            for gi in range(4):
                wwin = POOL_W[gi]
                wt_pw = w_use(("pw", gi))
                pooled_toks = []
                for cc in range(4):
                    c = 4 * gi + cc
                    wt_u = w_use(("in", 128 * c))
                    u_ev = []
                    tok = None
                    for bi, (a, b) in enumerate(B3):
                        bank, ftok = bank_alloc()
                        tok = pe_group(bank, b - a, 16, lambda k: wt_u.ap[:, k, :],
                                       lambda k, a=a, b=b: hnT[:, k, a:b], [ftok, wt_u.tok, hnT_done])
                        ev = ACT.do((lambda e, o=ub[:, a:b], i_=bank_f32(bank)[:, 0:b - a]:
                                     e.activation(out=o, in_=i_, func=AF.Copy)),
                                    wait=[tok, prev["ubread"]])
                        bank_release(bank, ev)
                        u_ev.append(ev)
                    w_release(wt_u, tok)
                    cur = ub
                    curtok = u_ev
                    bufs_ = [(S1, "s1"), (S2, "s2")]
                    lag = 1
                    step = 0
                    while lag < wwin:
                        dst, dkey = bufs_[step % 2]
                        off = 2 * lag - 1
                        tk = DVE.do((lambda e, o=dst[:, off:T], i0=cur[:, off:T], i1=cur[:, off - lag:T - lag]:
                                     e.tensor_tensor(out=o, in0=i0, in1=i1, op=ALU.add)),
                                    wait=[curtok, prev[dkey]])
                        cur, curtok = dst, tk
                        prev[dkey] = tk
                        lag *= 2
                        step += 1
                    pt = DVE.do((lambda e, o=pooledT[:, cc, :], i0=cur[:, HALO:T], i1=ub[:, HALO:T], s=1.0 / wwin:
                                 e.scalar_tensor_tensor(out=o, in0=i0, scalar=s, in1=i1,
                                                        op0=ALU.mult, op1=ALU.subtract)),
                                wait=[curtok, u_ev, prev["ya"]])
                    prev["ubread"] = pt
                    prev["s1"] = pt
                    prev["s2"] = pt
                    pooled_toks.append(pt)
                for cc in range(4):
                    c = 4 * gi + cc
                    wt_ga = w_use(("in", 8192 + 128 * c))
                    wt_gbr = w_use(("in", 10240 + 128 * c))
                    wt_co = w_use(("co", 128 * c))
                    ga_ev, gb_ev = [None, None], [None, None]
                    tok = None
                    for bi, (a, b) in enumerate(B2):
                        bank, ftok = bank_alloc()
                        tok = pe_group(bank, 512, 16, lambda k: wt_ga.ap[:, k, :],
                                       lambda k, a=a, b=b: hnT[:, k, a:b], [ftok, wt_ga.tok])
                        ev = ACT.do((lambda e, o=gA[:, bi, :], i_=bank_f32(bank)[:, 0:512],
                                     bcol=pv[:, PV_BGA + c:PV_BGA + c + 1]:
                                     e.activation(out=o, in_=i_, func=AF.Sigmoid, bias=bcol)),
                                    wait=[tok, prev["ta"][bi]])
                        bank_release(bank, ev)
                        ga_ev[bi] = ev
                    w_release(wt_ga, tok)
                    for bi, (a, b) in enumerate(B2):
                        bank, ftok = bank_alloc()
                        tok = pe_group(bank, 512, 16, lambda k: wt_gbr.ap[:, k, :],
                                       lambda k, a=a, b=b: hnT[:, k, a:b], [ftok, wt_gbr.tok])
                        ev = ACT.do((lambda e, o=gB[:, bi, :], i_=bank_f32(bank)[:, 0:512],
                                     bcol=pv[:, PV_BGB + c:PV_BGB + c + 1]:
                                     e.activation(out=o, in_=i_, func=AF.Sigmoid, bias=bcol)),
                                    wait=[tok, prev["tb"][bi]])
                        bank_release(bank, ev)
                        gb_ev[bi] = ev
                    w_release(wt_gbr, tok)
                    for bi, (a, b) in enumerate(B2):
                        bank, ftok = bank_alloc()
                        tok = pe_group(bank, 512, 4, lambda kk: wt_pw.ap[:, kk, cc * 128:(cc + 1) * 128],
                                       lambda kk, a=a, b=b: pooledT[:, kk, a - 16:b - 16],
                                       [ftok, wt_pw.tok, pooled_toks])
                        prev["ya"] = tok
                        ev = DVE.do((lambda e, o=ta[:, bi, :], i0=bank_f32(bank)[:, 0:512],
                                     s=pv[:, PV_PSC + c:PV_PSC + c + 1], i1=gA[:, bi, :]:
                                     e.scalar_tensor_tensor(out=o, in0=i0, scalar=s, in1=i1,
                                                            op0=ALU.mult, op1=ALU.mult)),
                                    wait=[tok, ga_ev[bi], prev["m"][bi]])
                        bank_release(bank, ev)
                        prev["ta"][bi] = ev
                    if cc == 3:
                        w_release(wt_pw, tok)
                    for bi, (a, b) in enumerate(B2):
                        bank, ftok = bank_alloc()
                        tok = pe_group(bank, 512, 16, lambda k: wt_co.ap[:, k, :],
                                       lambda k, a=a, b=b: zT[:, k, a - 16:b - 16], [ftok, wt_co.tok, last_z])
                        ev = DVE.do((lambda e, o=tb[:, bi, :], i0=bank_f32(bank)[:, 0:512], i1=gB[:, bi, :]:
                                     e.tensor_tensor(out=o, in0=i0, in1=i1, op=ALU.mult)),
                                    wait=[tok, gb_ev[bi], prev["m"][bi]])
                        bank_release(bank, ev)
                        prev["tb"][bi] = ev
                        mv = DVE.do((lambda e, o=mT[:, c, a - 16:b - 16], i0=tb[:, bi, :], i1=ta[:, bi, :]:
                                     e.tensor_tensor(out=o, in0=i0, in1=i1, op=ALU.add)),
                                    wait=[ev, prev["ta"][bi], last_p0_pe, last_p0_act])
                        prev["m"][bi] = mv
                        last_m = mv
                        m_toks[(c, bi)] = mv
                    w_release(wt_co, tok)
                    last_p1_pe = tok
            if debug and phase_limit == 2:
                final_toks.append(dump("mT", R2[:, 0:8192].bitcast(BF16)))

        pending_tr = []
        hn1_done = [None] * 8
        last_p2_pe = None
        if phase_limit >= 3:
            last_p1_act = Tok(ACT.sem, ACT.cnt, "act")
            last_p1_dve = Tok(DVE.sem, DVE.cnt, "dve")
            xr = []
            for t in range(8):
                xr.append(SP.dma((lambda e, o=h1[:, t, :], s=xh[HALO + 128 * t:HALO + 128 * (t + 1), :]:
                                  e.dma_start(out=o, in_=s)),
                                 hsem[t], f"h{t}", 1, wait=[last_p1_pe, last_m]))
            tr2 = {}
            a2s = {}
            for cb in range(4):
                wts = [w_use(("wo", cb, kq)) for kq in range(4)]
                tok = None
                for t in range(8):
                    bank, ftok = bank_alloc()
                    tok = pe_group(bank, 512, 16, lambda k, t=t: mT[:, k, 128 * t:128 * (t + 1)],
                                   lambda k: wts[k // 4].ap[:, k % 4, :],
                                   [ftok, m_toks[(14, 1)]] + [w.tok for w in wts],
                                   mid_wait=(15, [m_toks[(15, 0 if t < 4 else 1)]]))
                    ev = DVE.do((lambda e, o=h1[:, t, cb * 512:(cb + 1) * 512], i0=bank_f32(bank)[:, 0:512]:
                                 e.tensor_tensor(out=o, in0=i0, in1=o, op=ALU.add)),
                                wait=[tok, xr[t]])
                    bank_release(bank, ev)
                    last_p2_pe = tok
                    if cb == 3:
                        if len(pending_tr) >= 2:
                            tt, pp = pending_tr.pop(0)
                            tr2[tt], hn1_done[tt] = pp()
                        a2, pe_part = norm_transpose(t, 128, h1[:, t, :], ev, ss2, rs2, t, hn1b, t % 2, junk2,
                                                     PV_G2, hn1T, 128 * t, tr2.get(t - 2),
                                                     extra_wait=[last_p1_act, last_p1_dve, last_p1_pe])
                        a2s[t] = a2
                        pending_tr.append((t, pe_part))
                for w in wts:
                    w_release(w, tok)
            if phase_limit == 3:
                while pending_tr:
                    tt, pp = pending_tr.pop(0)
                    tr2[tt], hn1_done[tt] = pp()
                if debug:
                    final_toks.append(dump("h1", R1[:, 0:16384]))
                    final_toks.append(dump("hn1T", R3[:, 0:8192].bitcast(BF16)))

        out_toks = []
        if phase_limit >= 4:
            st = {"sg": [None, None], "sgi": 0, "dn_last": {}, "first": True}

            def flush_tr():
                while pending_tr:
                    tt, pp = pending_tr.pop(0)
                    _, hn1_done[tt] = pp()

            def gu_unit(g, jj, blk, wt_g, wt_u):
                need = [hn1_done[t] for t in range(4 * blk, 4 * blk + 4) if hn1_done[t] is not None]
                bg, ftok = bank_alloc()
                tg = pe_group(bg, 512, 16, lambda k: wt_g.ap[:, k, :],
                              lambda k, blk=blk: hn1T[:, k, 512 * blk:512 * (blk + 1)],
                              [ftok, wt_g.tok, last_p2_pe] + need)
                si = st["sgi"] % 2
                st["sgi"] += 1
                ea = ACT.do((lambda e, o=sg[:, si, :], i_=bank_f32(bg)[:, 0:512]:
                             e.activation(out=o, in_=i_, func=AF.Silu)),
                            wait=[tg, st["sg"][si], last_p2_pe])
                bank_release(bg, ea)
                bu, ftok = bank_alloc()
                tu = pe_group(bu, 512, 16, lambda k: wt_u.ap[:, k, :],
                              lambda k, blk=blk: hn1T[:, k, 512 * blk:512 * (blk + 1)],
                              [ftok, wt_u.tok])
                ed = DVE.do((lambda e, o=actT[:, g % 2, jj, 512 * blk:512 * (blk + 1)],
                             i0=bank_f32(bu)[:, 0:512], i1=sg[:, si, :]:
                             e.tensor_tensor(out=o, in0=i0, in1=i1, op=ALU.mult)),
                            wait=[tu, ea, st["dn_last"].get(g - 2), last_p2_pe])
                bank_release(bu, ed)
                st["sg"][si] = ed
                st["act_done_%d" % g] = ed
                return tg, tu

            def gate_up(g):
                if g == 0:
                    wts = []
                    for jj in range(4):
                        j = 4 * g + jj
                        wts.append((w_use(("gu", 128 * j)), w_use(("gu", FF + 128 * j))))
                    for blk in range(2):
                        for jj in range(4):
                            if blk == 0 and jj == 2 and pending_tr:
                                flush_tr()
                            tg, tu = gu_unit(g, jj, blk, wts[jj][0], wts[jj][1])
                            if blk == 1:
                                w_release(wts[jj][0], tg)
                                w_release(wts[jj][1], tu)
                    return
                for jj in range(4):
                    j = 4 * g + jj
                    wt_g = w_use(("gu", 128 * j))
                    wt_u = w_use(("gu", FF + 128 * j))
                    tg = tu = None
                    for blk in range(2):
                        tg, tu = gu_unit(g, jj, blk, wt_g, wt_u)
                    w_release(wt_g, tg)
                    w_release(wt_u, tu)

            fin = {}
            fin_pool = {}
            fin_scale = {}

            tmpg2 = R2[:, 5120:7168].rearrange("p (b n) -> p b n", b=2)

            def final_out(t, d1):
                ev_t, p3 = fin_pool[t]
                d2 = DVE.do((lambda e, o=rs3[:, t:t + 1]: e.reciprocal(out=o, in_=o)), wait=[d1])
                d3 = DVE.do((lambda e, o=h1[:, t, 0:1024], s=rs3[:, t:t + 1]:
                             e.scalar_tensor_tensor(out=o, in0=o, scalar=s, in1=g3[:, 0:1024],
                                                    op0=ALU.mult, op1=ALU.mult)),
                            wait=[d2, g3_tok])
                a3 = ACT.do((lambda e, o=h1[:, t, 1024:2048], s=rs3[:, t:t + 1], tg=tmpg2[:, t % 2, :]:
                             e.activation(out=o, in_=tg, func=AF.Copy, scale=s)),
                            wait=[d2, p3])
                fin_scale[t] = a3
                out_toks.append(ACT.dma((lambda e, o=out[128 * t:128 * (t + 1), 1024:2048], s=h1[:, t, 1024:2048]:
                                         e.dma_start(out=o, in_=s)),
                                        osem2[t], f"ob{t}", 1, wait=[a3]))
                out_toks.append(ACT.dma((lambda e, o=out[128 * t:128 * (t + 1), 0:1024], s=h1[:, t, 0:1024]:
                                         e.dma_start(out=o, in_=s)),
                                        osem[t], f"o{t}", 1, wait=[d3]))

            def down(g, merged_prev=False):
                gs = [g - 1, g] if merged_prev else [g]
                wts_all = [[w_use(("wd", gg, cb)) for cb in range(4)] for gg in gs]
                wts = [w for ws_ in wts_all for w in ws_]
                nk = 4 * len(gs)
                tok = None
                for t in range(8):
                    ev = None
                    for cb in range(4):
                        bank, ftok = bank_alloc()
                        tok = pe_group(bank, 512, nk,
                                       lambda kk, t=t: actT[:, gs[kk // 4] % 2, kk % 4, 128 * t:128 * (t + 1)],
                                       lambda kk, cb=cb: wts_all[kk // 4][cb].ap[:, kk % 4, :],
                                       [ftok, st["act_done_%d" % gs[0]]] + [w.tok for w in wts],
                                       mid_wait=((4, [st["act_done_%d" % gs[1]]]) if merged_prev else None))
                        ev = DVE.do((lambda e, o=h1[:, t, cb * 512:(cb + 1) * 512], i0=bank_f32(bank)[:, 0:512]:
                                     e.tensor_tensor(out=o, in0=i0, in1=o, op=ALU.add)),
                                    wait=[tok])
                        bank_release(bank, ev)
                    if g == NPASS - 1:
                        a1 = ACT.do((lambda e, i_=h1[:, t, :], acc=ss3[:, t:t + 1]:
                                     e.activation(out=junk2[:, :], in_=i_, func=AF.Square,
                                                  accum_out=acc)),
                                    wait=[ev])
                        d1 = ACT.do((lambda e, o=rs3[:, t:t + 1], i0=ss3[:, t:t + 1]:
                                     e.activation(out=o, in_=i0, func=AF.Sqrt, scale=1.0 / D, bias=eps_ap[:, :])), wait=[a1])
                        if t >= 1:
                            final_out(t - 1, fin[t - 1])
                        fin[t] = d1
                        p3 = POOL.do((lambda e, i0=h1[:, t, 1024:2048], tg=tmpg2[:, t % 2, :]:
                                      e.tensor_tensor(out=tg, in0=i0, in1=g3[:, 1024:2048], op=ALU.mult)),
                                     wait=[ev, g3_tok, fin_scale.get(t - 2)])
                        fin_pool[t] = (ev, p3)
                if g == NPASS - 1:
                    final_out(7, fin[7])
                st["dn_last"][g] = tok
                for w in wts:
                    w_release(w, tok)

            gate_up(0)
            for g in range(NPASS):
                if g + 1 < NPASS:
                    gate_up(g + 1)
                if g == NPASS - 2:
                    continue
                down(g, merged_prev=(g == NPASS - 1))

        ACT.wait(*out_toks)
        SP.wait(*final_toks)
        SP.wait(Tok(PE.sem, PE.cnt, "pe"), Tok(ACT.sem, ACT.cnt, "act"), Tok(DVE.sem, DVE.cnt, "dve"), g3_tok)
        if POOL.cnt:
            SP.wait(Tok(POOL.sem, POOL.cnt, "pool"))
        for s in range(NS):
            if wst["uses"][s]:
                SP.wait(Tok(wsem[s], 16 * wst["uses"][s], f"w{s}"))
        for s in range(2):
            if wst["pwuses"][s]:
                SP.wait(Tok(pwsem[s], 16 * wst["pwuses"][s], f"pw{s}"))

        with nc.Block() as block:
            @block.sync
            def _(e):
                SP.emit(e)

            @block.gpsimd
            def _(e):
                POOL.emit(e)

            @block.tensor
            def _(e):
                PE.emit(e)

            @block.scalar
            def _(e):
                ACT.emit(e)

            @block.vector
            def _(e):
                DVE.emit(e)
    return nc


def make_in_maps(x, meta_tokens, norm_mix_g, w_in, b_gate, pool_w, pool_scale, conv_w, conv_out_w, w_o,
                 norm_ffn_g, w_gate_up, w_down, norm_final_g):
    f = lambda a: np.ascontiguousarray(np.asarray(a, dtype=np.float32))
    x = f(x)
    meta = f(meta_tokens)

    def fm(v):
        return f(v).reshape(16, 128).T

    pvec = np.zeros((128, 128), np.float32)
    pvec[:, PV_G1:PV_G1 + 16] = fm(norm_mix_g)
    bg = f(b_gate)
    pvec[:, PV_BGA:PV_BGA + 16] = fm(bg[:D])
    pvec[:, PV_BGB:PV_BGB + 16] = fm(bg[D:])
    pvec[:, PV_PSC:PV_PSC + 16] = fm(pool_scale)
    cw = f(conv_w)
    for tap in range(3):
        pvec[:, PV_CW + tap:PV_CW + 48:3] = fm(cw[tap])
    pvec[:, PV_G2:PV_G2 + 16] = fm(norm_ffn_g)
    g3b = np.ascontiguousarray(np.broadcast_to(f(norm_final_g)[None, :], (128, D)))
    ident = np.eye(128, dtype=np.float32).astype(ml_dtypes.bfloat16)
    shared = {
        "w_in": f(w_in), "pw2": np.ascontiguousarray(f(pool_w).transpose(1, 0, 2).reshape(512, 2048)), "conv_out_w": f(conv_out_w),
        "w_o": f(w_o), "w_gate_up": f(w_gate_up), "w_down": f(w_down),
        "pvec": pvec, "g3b": g3b, "ident": ident,
    }
    maps = []
    for i in range(N_CORES):
        b, half = i // 2, i % 2
        halo = meta if half == 0 else x[b, TR - HALO:TR]
        xh = np.concatenate([halo, x[b, half * TR:(half + 1) * TR]], axis=0)
        m = dict(shared)
        m["xh"] = np.ascontiguousarray(xh)
        maps.append(m)
    return maps


def kernel(x, meta_tokens, norm_mix_g, w_in, b_gate, pool_w, pool_scale, conv_w, conv_out_w, w_o,
           norm_ffn_g, w_gate_up, w_down, norm_final_g):
    maps = make_in_maps(x, meta_tokens, norm_mix_g, w_in, b_gate, pool_w, pool_scale, conv_w, conv_out_w,
                        w_o, norm_ffn_g, w_gate_up, w_down, norm_final_g)
    nc = build_program()
    res = run_bass_kernel_spmd(nc, maps, core_ids=list(range(N_CORES)))
    outp = np.empty((4, 2 * TR, D), np.float32)
    for i in range(N_CORES):
        b, half = i // 2, i % 2
        outp[b, half * TR:(half + 1) * TR] = res.results[i]["out"]
    return outp
```
